# Optimizing a Trainium2 kernel written in Bass

```python
import jax, jax.numpy as jnp
from jax import lax
import numpy as np

D_MODEL = 1024
BATCH = 4
SEQ = 4096
DEPTH = 1

HEAD_DIM = 64
N_HEADS = D_MODEL // HEAD_DIM
DIL_HEADS = N_HEADS // 2
SB_HEADS = N_HEADS - DIL_HEADS
DIL_WIDTH = DIL_HEADS * HEAD_DIM
SB_WIDTH = SB_HEADS * HEAD_DIM
QKV_WIDTH = 3 * DIL_WIDTH + 3 * SB_WIDTH
DILATED_BRANCHES = ((128, 1), (512, 4), (2048, 16))
ROPE_DIM = HEAD_DIM // 4
ROPE_THETA = 500000.0
Q_BLOCK = 128
PEER_HEADS = 8
N_SUBKEYS = 128
N_EXPERTS = N_SUBKEYS * N_SUBKEYS
PEER_KEY_DIM = 256
PEER_TOPK = 16
TOKEN_BLOCK = 128
LN_EPS = 1e-5
DEEPNORM_ALPHA = (2.0 * DEPTH) ** 0.25
DEEPNORM_BETA = (8.0 * DEPTH) ** -0.25

kernel_name = "hybrid_dilated_stickbreak_peer_deepnorm"


def layer_norm(x, g, b):
    xf = x.astype(jnp.float32)
    mu = jnp.mean(xf, axis=-1, keepdims=True)
    var = jnp.mean(jnp.square(xf - mu), axis=-1, keepdims=True)
    y = (xf - mu) * lax.rsqrt(var + LN_EPS)
    return (y * g.astype(jnp.float32) + b.astype(jnp.float32)).astype(x.dtype)


def split_heads(t, n_heads):
    b, s, _ = t.shape
    return t.reshape(b, s, n_heads, HEAD_DIM).transpose(0, 2, 1, 3)


def merge_heads(t):
    b, h, s, d = t.shape
    return t.transpose(0, 2, 1, 3).reshape(b, s, h * d)


def partial_rotary(x, pos):
    half = ROPE_DIM // 2
    inv_freq = ROPE_THETA ** (-jnp.arange(half, dtype=jnp.float32) * 2.0 / ROPE_DIM)
    ang = pos.astype(jnp.float32)[:, None] * inv_freq[None, :]
    cos, sin = jnp.cos(ang), jnp.sin(ang)
    xr = x[..., :ROPE_DIM].astype(jnp.float32)
    x1, x2 = xr[..., :half], xr[..., half:]
    rot = jnp.concatenate([x1 * cos - x2 * sin, x1 * sin + x2 * cos], axis=-1).astype(x.dtype)
    return jnp.concatenate([rot, x[..., ROPE_DIM:]], axis=-1)


def to_query_blocks(q):
    b, h, s, d = q.shape
    return q.reshape(b, h, s // Q_BLOCK, Q_BLOCK, d).transpose(2, 0, 1, 3, 4)


def from_query_blocks(o):
    n, b, h, qb, d = o.shape
    return o.transpose(1, 2, 0, 3, 4).reshape(b, h, n * qb, d)


def dilated_window_attention(q, k, v):
    scale = HEAD_DIM ** -0.5
    n_blk = q.shape[2] // Q_BLOCK

    def block(args):
        blk, qblk = args
        t = blk * Q_BLOCK + jnp.arange(Q_BLOCK)
        outs, lses = [], []
        for window, dil in DILATED_BRANCHES:
            n_off = window // dil + 1
            idx = t[:, None] - dil * jnp.arange(n_off)[None, :]
            valid = idx >= 0
            idx = jnp.maximum(idx, 0)
            kg = jnp.take(k, idx, axis=2)
            vg = jnp.take(v, idx, axis=2)
            s = jnp.einsum('bhqd,bhqnd->bhqn', qblk, kg).astype(jnp.float32) * scale
            s = jnp.where(valid, s, -jnp.inf)
            lse = jax.nn.logsumexp(s, axis=-1)
            p = jnp.exp(s - lse[..., None])
            outs.append(jnp.einsum('bhqn,bhqnd->bhqd', p.astype(vg.dtype), vg).astype(jnp.float32))
            lses.append(lse)
        w = jax.nn.softmax(jnp.stack(lses, axis=-1), axis=-1)
        o = jnp.einsum('bhqr,rbhqd->bhqd', w, jnp.stack(outs, axis=0))
        return o.astype(q.dtype)

    out = lax.map(block, (jnp.arange(n_blk), to_query_blocks(q)))
    return from_query_blocks(out)


def stick_breaking_attention(q, k, v):
    scale = HEAD_DIM ** -0.5
    s_len = q.shape[2]
    n_blk = s_len // Q_BLOCK
    s_idx = jnp.arange(s_len)

    def block(args):
        blk, qblk = args
        t = blk * Q_BLOCK + jnp.arange(Q_BLOCK)
        causal = s_idx[None, :] < t[:, None]
        z = jnp.einsum('bhqd,bhsd->bhqs', qblk, k).astype(jnp.float32) * scale
        log_beta = jax.nn.log_sigmoid(z)
        log_1m = jnp.where(causal, jax.nn.log_sigmoid(-z), 0.0)
        later = lax.cumsum(log_1m, axis=3, reverse=True) - log_1m
        a = jnp.where(causal, jnp.exp(log_beta + later), 0.0)
        return jnp.einsum('bhqs,bhsd->bhqd', a.astype(v.dtype), v)

    out = lax.map(block, (jnp.arange(n_blk), to_query_blocks(q)))
    return from_query_blocks(out)


def hybrid_mixer(x, w_in, w_out):
    s_len = x.shape[1]
    pos = jnp.arange(s_len)
    proj = x @ w_in
    cuts = [DIL_WIDTH, 2 * DIL_WIDTH, 3 * DIL_WIDTH,
            3 * DIL_WIDTH + SB_WIDTH, 3 * DIL_WIDTH + 2 * SB_WIDTH]
    qa, ka, va, qb, kb, vb = jnp.split(proj, cuts, axis=-1)
    qa = partial_rotary(split_heads(qa, DIL_HEADS), pos)
    ka = partial_rotary(split_heads(ka, DIL_HEADS), pos)
    oa = dilated_window_attention(qa, ka, split_heads(va, DIL_HEADS))
    ob = stick_breaking_attention(split_heads(qb, SB_HEADS), split_heads(kb, SB_HEADS),
                                  split_heads(vb, SB_HEADS))
    o = jnp.concatenate([merge_heads(oa), merge_heads(ob)], axis=-1)
    return o @ w_out


def peer_ffn(x, w_query, sub_keys, expert_u, expert_v):
    b, s_len, d = x.shape
    q = (x @ w_query).reshape(b, s_len, PEER_HEADS, 2, PEER_KEY_DIM // 2)
    scores = jnp.einsum('bshpc,hpnc->bshpn', q, sub_keys).astype(jnp.float32)
    top_s, top_i = lax.top_k(scores, PEER_TOPK)
    cand_s = top_s[..., 0, :, None] + top_s[..., 1, None, :]
    cand_i = top_i[..., 0, :, None] * N_SUBKEYS + top_i[..., 1, None, :]
    cand_s = cand_s.reshape(b, s_len, PEER_HEADS, PEER_TOPK * PEER_TOPK)
    cand_i = cand_i.reshape(b, s_len, PEER_HEADS, PEER_TOPK * PEER_TOPK)
    best_s, best_pos = lax.top_k(cand_s, PEER_TOPK)
    expert_idx = jnp.take_along_axis(cand_i, best_pos, axis=-1)
    gates = jax.nn.softmax(best_s, axis=-1)

    n_tok = b * s_len
    n_blk = n_tok // TOKEN_BLOCK
    n_sel = PEER_HEADS * PEER_TOPK
    xt = x.reshape(n_blk, TOKEN_BLOCK, d)
    it = expert_idx.reshape(n_blk, TOKEN_BLOCK, n_sel)
    gt = gates.reshape(n_blk, TOKEN_BLOCK, n_sel)

    def block(args):
        xb, ib, gb = args
        u = jnp.take(expert_u, ib, axis=0)
        h = jnp.einsum('td,ted->te', xb, u)
        act = gb.astype(xb.dtype) * jax.nn.gelu(h, approximate=False)
        vv = jnp.take(expert_v, ib, axis=0)
        return jnp.einsum('te,ted->td', act, vv)

    y = lax.map(block, (xt, it, gt))
    return y.reshape(b, s_len, d)


def setup_inputs(seed: int = 0) -> dict:
    key = jax.random.key(seed)
    ks = jax.random.split(key, 12)
    f32 = jnp.float32
    x = jax.random.normal(ks[0], (BATCH, SEQ, D_MODEL), f32)
    col_scale = jnp.concatenate([
        jnp.ones((2 * DIL_WIDTH,), f32), jnp.full((DIL_WIDTH,), DEEPNORM_BETA, f32),
        jnp.ones((2 * SB_WIDTH,), f32), jnp.full((SB_WIDTH,), DEEPNORM_BETA, f32)])
    w_in = jax.random.normal(ks[1], (DEPTH, D_MODEL, QKV_WIDTH), f32) * (D_MODEL ** -0.5) * col_scale
    w_out = jax.random.normal(ks[2], (DEPTH, D_MODEL, D_MODEL), f32) * (D_MODEL ** -0.5) * DEEPNORM_BETA
    ln1_g = 1.0 + 0.02 * jax.random.normal(ks[3], (DEPTH, D_MODEL), f32)
    ln1_b = 0.02 * jax.random.normal(ks[4], (DEPTH, D_MODEL), f32)
    peer_wq = jax.random.normal(ks[5], (DEPTH, D_MODEL, PEER_HEADS * PEER_KEY_DIM), f32) * (D_MODEL ** -0.5)
    peer_sub_keys = jax.random.normal(ks[6], (DEPTH, PEER_HEADS, 2, N_SUBKEYS, PEER_KEY_DIM // 2), f32) \
        * ((PEER_KEY_DIM // 2) ** -0.5)
    peer_u = jax.random.normal(ks[7], (DEPTH, N_EXPERTS, D_MODEL), f32) * (D_MODEL ** -0.5)
    peer_v = jax.random.normal(ks[8], (DEPTH, N_EXPERTS, D_MODEL), f32) * DEEPNORM_BETA * (PEER_HEADS ** -0.5)
    ln2_g = 1.0 + 0.02 * jax.random.normal(ks[9], (DEPTH, D_MODEL), f32)
    ln2_b = 0.02 * jax.random.normal(ks[10], (DEPTH, D_MODEL), f32)
    return {"x": x, "w_in": w_in, "w_out": w_out, "ln1_g": ln1_g, "ln1_b": ln1_b,
            "peer_wq": peer_wq, "peer_sub_keys": peer_sub_keys, "peer_u": peer_u,
            "peer_v": peer_v, "ln2_g": ln2_g, "ln2_b": ln2_b}


def reference(x, w_in, w_out, ln1_g, ln1_b, peer_wq, peer_sub_keys, peer_u, peer_v, ln2_g, ln2_b):
    for layer in range(DEPTH):
        mix = hybrid_mixer(x, w_in[layer], w_out[layer])
        x = layer_norm(DEEPNORM_ALPHA * x + mix, ln1_g[layer], ln1_b[layer])
        ffn = peer_ffn(x, peer_wq[layer], peer_sub_keys[layer], peer_u[layer], peer_v[layer])
        x = layer_norm(DEEPNORM_ALPHA * x + ffn, ln2_g[layer], ln2_b[layer])
    return x
```

```python
import os
import numpy as np
import concourse.bass as bass
import concourse.mybir as mybir
from concourse.bass import IndirectOffsetOnAxis
from concourse.bass_utils import run_bass_kernel_spmd
from contextlib import ExitStack

F32 = mybir.dt.float32
BF16 = mybir.dt.bfloat16
U32 = mybir.dt.uint32
I32 = mybir.dt.int32
ACT = mybir.ActivationFunctionType
ALU = mybir.AluOpType
AX = mybir.AxisListType

D = 1024
SEQ = 4096
NB = 32
NQ = 16
ALPHA = 2.0 ** 0.25
LN_EPS = 1e-5
THETA = 500000.0
NEG = -1.0e30


class StopBuild(Exception):
    pass


class Buf:
    _n = 0

    def __init__(self, t, name=None):
        self.t = t
        self.w = None
        self.r = {}
        self.dsem = None
        self.dcnt = 0
        Buf._n += 1
        self.name = name or f"b{Buf._n}"

    def __getitem__(self, k):
        return self.t[k]


class Sched:
    def __init__(self, nc):
        self.nc = nc
        self.eng = {'pe': nc.tensor, 'act': nc.scalar, 'dve': nc.vector,
                    'pool': nc.gpsimd, 'sp': nc.sync}
        self.sem, self.cnt, self.seen = {}, {}, {}
        for k in self.eng:
            self.sem[k] = nc.alloc_semaphore(f"s_{k}")
            self.cnt[k] = 0
            self.seen[k] = {}
        self.n_ins = 0
        self.n_wait = 0
        self.allbufs = []

    def sb(self, name, shape, dt, es=None):
        self._uid = getattr(self, "_uid", 0) + 1
        uname = f"{name}_{self._uid}"
        if es is None:
            t = self.nc.alloc_sbuf_tensor(uname, shape, dt)
        else:
            t = es.enter_context(self.nc.sbuf_tensor(uname, shape, dt))
        b = Buf(t, uname)
        self.allbufs.append(b)
        return b

    def barrier(self):
        evs = [(self.sem[k], self.cnt[k]) for k in self.eng if self.cnt[k]]
        for b in self.allbufs:
            if b.dsem is not None and b.dcnt:
                evs.append((b.dsem, b.dcnt))
        for e in self.eng:
            self._wait(e, [ev for ev in evs if ev[0] is not self.sem[e]])

    def _wait(self, e, deps):
        E = self.eng[e]
        seen = self.seen[e]
        best = {}
        for ev in deps:
            if ev is None:
                continue
            sem, val = ev
            key = id(sem)
            if key not in best or best[key][1] < val:
                best[key] = (sem, val)
        for key, (sem, val) in best.items():
            if seen.get(key, 0) >= val:
                continue
            if e == 'pe' and sem is self.sem['pe']:
                continue
            E.wait_ge(sem, val)
            self.n_wait += 1
            seen[key] = val

    def _deps(self, reads, writes):
        deps = []
        for b in reads:
            deps.append(b.w)
        for b in writes:
            deps.append(b.w)
            deps.extend(b.r.values())
        return deps

    def _mark(self, ev, reads, writes):
        for b in writes:
            b.w = ev
            b.r = {}
        for b in reads:
            if b in writes:
                continue
            b.r[id(ev[0])] = ev

    def op(self, e, fn, reads=(), writes=()):
        self._wait(e, self._deps(reads, writes))
        ins = fn(self.eng[e])
        self.cnt[e] += 1
        ins.then_inc(self.sem[e], 1)
        self._mark((self.sem[e], self.cnt[e]), reads, writes)
        self.n_ins += 1
        return ins

    def dma(self, q, fn, track, reads=(), writes=()):
        if track.dsem is None:
            track.dsem = self.nc.alloc_semaphore(f"d_{track.name}")
        deps = self._deps(reads, writes)
        if track.dcnt:
            deps.append((track.dsem, track.dcnt))
        self._wait(q, deps)
        ins = fn(self.eng[q])
        track.dcnt += 16
        ins.then_inc(track.dsem, 16)
        self._mark((track.dsem, track.dcnt), reads, writes)
        self.n_ins += 1
        return ins

    def finish(self, e, bufs):
        deps = []
        for b in bufs:
            deps.append(b.w)
            deps.extend(b.r.values())
            if b.dsem is not None and b.dcnt:
                deps.append((b.dsem, b.dcnt))
        self._wait(e, deps)


def _mult(delta):
    m = ((delta >= 0) & (delta <= 128)).astype(np.float32)
    m += ((delta >= 0) & (delta <= 512) & (delta % 4 == 0)).astype(np.float32)
    m += ((delta >= 0) & (delta <= 2048) & (delta % 16 == 0)).astype(np.float32)
    return m


def host_consts(hf):
    f32 = np.float32
    c = {}
    c["ident"] = np.eye(128, dtype=f32)
    jj = np.arange(128)
    c["mtri"] = (jj[:, None] > jj[None, :]).astype(f32)
    rmat = np.zeros((128, 128), f32)
    for p in range(128):
        i = p % 64
        if i < 8:
            rmat[p + 8, p] = -1.0
        elif i < 16:
            rmat[p - 8, p] = 1.0
    c["rmat"] = rmat
    z = np.zeros((128, 255), f32)
    z[:, 127] = 1.0
    c["zcol"] = z
    c["iota16"] = np.tile(np.arange(16, dtype=f32)[None, :], (128, 1))
    c["iota128"] = np.tile(np.arange(128, dtype=f32)[None, :], (128, 1))
    inv_freq = (f32(THETA) ** (-(np.arange(8, dtype=f32) * f32(2.0) / f32(16)))).astype(f32)

    def rope(pos):
        ang = pos.astype(f32)[:, None] * inv_freq[None, :]
        cos, sin = np.cos(ang).astype(f32), np.sin(ang).astype(f32)
        C = np.ones((128, pos.shape[0]), f32)
        Sn = np.zeros((128, pos.shape[0]), f32)
        for p in range(128):
            i = p % 64
            if i < 16:
                C[p] = cos[:, i % 8]
                Sn[p] = sin[:, i % 8]
        return C, Sn

    c["ropeK_C"], c["ropeK_S"] = rope(np.arange(SEQ))
    own = np.concatenate([(2 * j + hf) * 128 + np.arange(128) for j in range(NQ)])
    c["ropeQ_C"], c["ropeQ_S"] = rope(own)
    s = np.arange(128)[:, None]
    q = np.arange(128)[None, :]
    mda = np.zeros((128, 18, 128), f32)
    for mi in range(18):
        o = mi - 16
        mda[:, mi, :] = _mult((hf - o) * 128 + q - s)
    c["maskDA"] = mda
    msb = np.zeros((128, 2, 128), f32)
    for o in range(2):
        msb[:, o, :] = (((hf - o) * 128 + q - s) > 0).astype(f32)
    c["maskSB"] = msb
    return c


CONST_SHAPES = {
    "ident": [128, 128], "mtri": [128, 128], "rmat": [128, 128], "zcol": [128, 255],
    "iota16": [128, 16], "iota128": [128, 128], "ropeK_C": [128, SEQ], "ropeK_S": [128, SEQ],
    "ropeQ_C": [128, 2048], "ropeQ_S": [128, 2048],
    "maskDA": [128, 18, 128], "maskSB": [128, 2, 128],
}


def build(stage="full", dbg=False):
    nc = bass.Bass("TRN2", target_bir_lowering=False)
    S = Sched(nc)

    def din(name, shape, dt=F32):
        return nc.dram_tensor(name, shape, dt, kind="ExternalInput").ap()

    xT_d = din("xT", [D, SEQ])
    xTq_d = din("xTq", [D, 2048])
    xown_d = din("xown", [2048, D])
    w_in_d = din("w_in", [D, 3072])
    w_out_d = din("w_out", [D, D])
    ln1g_d = din("ln1_g", [1, D])
    ln1b_d = din("ln1_b", [1, D])
    wq_d = din("wq", [D, 2048])
    keys_d = din("keys", [16 * 128, 128])
    pu_d = din("peer_u", [16384, D])
    pv_d = din("peer_v", [16384, D])
    ln2g_d = din("ln2_g", [1, D])
    ln2b_d = din("ln2_b", [1, D])
    cd = {k: din("c_" + k, v) for k, v in CONST_SHAPES.items()}
    out_d = nc.dram_tensor("out", [2048, D], F32, kind="ExternalOutput").ap()
    dbg_outs = {}

    def dbg_dump(name, buf, ap, shape, dt=F32, allb=()):
        if not dbg:
            return
        t = nc.dram_tensor("dbg_" + name, shape, dt, kind="ExternalOutput").ap()
        S._wait('sp', [b.w for b in allb])
        S.dma('sp', lambda e: e.dma_start(out=t, in_=ap), buf, reads=[buf])
        dbg_outs[name] = buf

    PS = [Buf(nc.alloc_psum_tensor(f"ps{i}", [128, 512], F32), f"ps{i}") for i in range(8)]

    ident = S.sb("ident", [128, 128], F32)
    identb = S.sb("identb", [128, 128], BF16)
    mtri = S.sb("mtri", [128, 128], BF16)
    onesb = S.sb("onesb", [128, 128], BF16)
    zerob = S.sb("zerob", [128, 512], BF16)
    rmat = S.sb("rmat", [128, 128], BF16)
    zcol = S.sb("zcol", [128, 255], BF16)
    iota16 = S.sb("iota16", [128, 16], F32)
    iota128 = S.sb("iota128", [128, 128], F32)
    maskDA = S.sb("maskDA", [128, 18, 128], BF16)
    maskSB = S.sb("maskSB", [128, 2, 128], BF16)
    es_o = ExitStack()
    o_all = S.sb("o_all", [128, NQ, D], BF16, es_o)
    if dbg:
        S.op('pool', lambda e: e.memset(o_all[:], 0.0), writes=[o_all])
    es0 = ExitStack()
    cstage = S.sb("cstage", [128, 18 * 128], F32, es0)

    def load_const(dst, name, n):
        src = cd[name]
        if len(src.shape) == 3:
            src = src.rearrange("p a b -> p (a b)")
        S.dma('sp', lambda e: e.dma_start(out=cstage[:, 0:n], in_=src), cstage, writes=[cstage])
        dflat = dst[:]
        if len(dflat.shape) == 3:
            dflat = dflat.rearrange("p a b -> p (a b)")
        S.op('dve', lambda e: e.tensor_copy(out=dflat, in_=cstage[:, 0:n]), reads=[cstage], writes=[dst])

    S.dma('sp', lambda e: e.dma_start(out=ident[:], in_=cd["ident"]), ident, writes=[ident])
    S.dma('sp', lambda e: e.dma_start(out=iota16[:], in_=cd["iota16"]), iota16, writes=[iota16])
    S.dma('sp', lambda e: e.dma_start(out=iota128[:], in_=cd["iota128"]), iota128, writes=[iota128])
    load_const(identb, "ident", 128)
    load_const(mtri, "mtri", 128)
    load_const(rmat, "rmat", 128)
    load_const(zcol, "zcol", 255)
    load_const(maskDA, "maskDA", 18 * 128)
    load_const(maskSB, "maskSB", 2 * 128)
    S.op('pool', lambda e: e.memset(onesb[:], 1.0), writes=[onesb])
    S.op('pool', lambda e: e.memset(zerob[:], 0.0), writes=[zerob])
    S.barrier()
    es0.close()

    TC = 256
    NTC = SEQ // TC
    NTCQ = 2048 // TC

    es_kv = ExitStack()
    KT = es_kv.enter_context(nc.sbuf_tensor("KT", [128, 4, SEQ], BF16))
    QT = es_kv.enter_context(nc.sbuf_tensor("QT", [128, 8, 2048], BF16))
    V = es_kv.enter_context(nc.sbuf_tensor("V", [128, NB, 8, 65], BF16))
    KTb = [[Buf(KT, f"KT{hp}_{tc}") for tc in range(NTC)] for hp in range(4)]
    QTb = [[Buf(QT, f"QT{h}_{tc}") for tc in range(NTCQ)] for h in range(8)]
    S.op('pool', lambda e: e.memset(QT[:, :, :], 0.0), writes=[b for r in QTb for b in r])
    Vb = [Buf(V, f"V{g}") for g in range(NB)]
    Vones = Buf(V, "Vones")
    S.op('pool', lambda e: e.memset(V[:, :, :, 64:65], 1.0), writes=[Vones])

    def ktb(hp, kb):
        return KTb[hp][(kb * 128) // TC]

    def qtb(h, j):
        return QTb[h][(j * 128) // TC]

    ut_s = nc.dram_tensor("ut_s", [128, 128, 1024], BF16).ap()
    v_s = nc.dram_tensor("v_s", [16384, D], BF16).ap()

    class C0:
        NPH = 7

        def __init__(self):
            self.c = 0
            self.phase = 0
            self.loaded = set()
            self.t = None
            self.prefetch = True

        def bind(self, es_):
            self.t = {
                "us": [S.sb(f"ustg{i}", [128, D], F32, es_) for i in range(2)],
                "vs": [S.sb(f"vstg{i}", [128, D], F32, es_) for i in range(2)],
                "ub": S.sb("utb", [128, D], BF16, es_),
                "vb": S.sb("vbb", [128, D], BF16, es_),
                "bank": Buf(PS[1].t, f"c0bank{self.c}"),
            }
            self.loaded = set()

        def load(self, c):
            if c in self.loaded or c >= 128:
                return
            us, vs = self.t["us"][c % 2], self.t["vs"][c % 2]
            S.dma('sp', lambda e: e.dma_start(out=us[:], in_=pu_d[c * 128:(c + 1) * 128, :]), us, writes=[us])
            S.dma('sp', lambda e: e.dma_start(out=vs[:], in_=pv_d[c * 128:(c + 1) * 128, :]), vs, writes=[vs])
            self.loaded.add(c)

        def pe_round(self, c, r):
            us, bank = self.t["us"][c % 2], self.t["bank"]
            for q4 in range(4):
                dc = r * 4 + q4
                S.op('pe', lambda e: e.transpose(out=bank[:, q4 * 128:(q4 + 1) * 128],
                                                in_=us[:, dc * 128:(dc + 1) * 128], identity=ident[:]),
                     reads=[us, ident], writes=[bank])

        def act_round(self, c, r):
            ub, bank = self.t["ub"], self.t["bank"]
            S.op('act', lambda e: e.copy(out=ub[:, r * 512:(r + 1) * 512], in_=bank[:, :]), reads=[bank], writes=[ub])

        def cast(self, c):
            vs, vb = self.t["vs"][c % 2], self.t["vb"]
            S.op('pool', lambda e: e.tensor_copy(out=vb[:], in_=vs[:]), reads=[vs], writes=[vb])

        def store(self, c):
            ub, vb = self.t["ub"], self.t["vb"]
            S.dma('sp', lambda e: e.dma_start(out=ut_s[c], in_=ub[:]), ub, reads=[ub])
            S.dma('sp', lambda e: e.dma_start(out=v_s[c * 128:(c + 1) * 128, :], in_=vb[:]), vb, reads=[vb])

        def tick(self):
            if self.c >= 128:
                return
            c, ph = self.c, self.phase
            if ph == 0:
                self.load(c)
                self.pe_round(c, 0)
                if self.prefetch:
                    self.load(c + 1)
            elif ph == 1:
                self.act_round(c, 0)
            elif ph == 2:
                self.pe_round(c, 1)
            elif ph == 3:
                self.act_round(c, 1)
                self.cast(c)
            elif ph == 4:
                self.store(c)
            self.phase += 1
            if self.phase == self.NPH:
                self.phase = 0
                self.loaded.discard(c)
                self.c += 1

        def flush(self):
            self.prefetch = False
            while self.c < 128 and (self.phase != 0 or self.c in self.loaded):
                self.tick()
            self.prefetch = True

        def finish_all(self):
            while self.c < 128:
                self.tick()

    c0 = C0()
    P = {}

    def alloc_proj(es):
        P["w_bf"] = S.sb("w_bf", [128, 8, 1536], BF16, es)
        P["wst"] = [S.sb(f"wst{i}", [128, 1536], F32, es) for i in range(1)]
        P["xst"] = [S.sb(f"xst{i}", [128, 8, TC], F32, es) for i in range(2)]
        P["xcb"] = [S.sb(f"xcb{i}", [128, 8, TC], BF16, es) for i in range(2)]
        P["ropC"] = [S.sb(f"ropC{i}", [128, TC], F32, es) for i in range(2)]
        P["ropS"] = [S.sb(f"ropS{i}", [128, TC], F32, es) for i in range(2)]
        P["rawb"] = [S.sb(f"rawb{i}", [128, TC], BF16, es) for i in range(2)]
        P["rt1"] = [S.sb(f"rt1_{i}", [128, TC], F32, es) for i in range(2)]
        P["rt2"] = [S.sb(f"rt2_{i}", [128, TC], F32, es) for i in range(2)]

    def alloc_attn(es):
        P["e_t"] = [S.sb(f"e_t{i}", [128, 512], F32, es) for i in range(3)]
        P["sp_t"] = [S.sb(f"sp_t{i}", [128, 512], F32, es) for i in range(3)]
        P["l1m_t"] = [S.sb(f"l1m_t{i}", [128, 512], BF16, es) for i in range(3)]
        P["tt_t"] = [S.sb(f"tt_t{i}", [128, 512], F32, es) for i in range(3)]
        P["a_t"] = [S.sb(f"a_t{i}", [128, 512], BF16, es) for i in range(3)]
        P["am_t"] = [S.sb(f"am_t{i}", [128, 512], BF16, es) for i in range(3)]
        P["lsum"] = [S.sb(f"lsum{i}", [128, 512], BF16, es) for i in range(6)]
        P["rec_t"] = [S.sb(f"rec_t{i}", [128, 4], F32, es) for i in range(2)]
        c0.bind(es)

    cnt = {"w": 0, "x": 0, "pk": 0, "ev": 0}

    def load_pass_weights(colq, colk, colv):
        for dc in range(8):
            st = P["wst"][0]
            cnt["w"] += 1
            for i, c0 in enumerate((colq, colk, colv)):
                q = 'sp'
                S.dma(q, lambda e, c0=c0, i=i: e.dma_start(
                    out=st[:, i * 512:(i + 1) * 512], in_=w_in_d[dc * 128:(dc + 1) * 128, c0:c0 + 512]),
                    st, writes=[st])
            S.op('pool', lambda e: e.tensor_copy(out=P["w_bf"][:, dc, :], in_=st[:]), reads=[st], writes=[P["w_bf"]])

    def evac(kind, rope, ps, dsts, Cb, Sb, scale):
        k = cnt["ev"] % 2
        cnt["ev"] += 1
        if not rope:
            for (p0, p1, dst_ap, dst_buf) in dsts:
                S.op('act', lambda e: e.mul(out=dst_ap, in_=ps[p0:p1, 0:TC], mul=scale),
                     reads=[ps], writes=[dst_buf])
            return
        raw, t1, t2 = P["rawb"][k], P["rt1"][k], P["rt2"][k]
        psw = PS[2 + k]
        S.op('act', lambda e: e.mul(out=raw[:], in_=ps[:, 0:TC], mul=scale),
             reads=[ps], writes=[raw])
        S.op('pe', lambda e: e.matmul(out=psw[:, 0:TC], lhsT=rmat[:], rhs=raw[:], start=True, stop=True),
             reads=[rmat, raw], writes=[psw])
        S.op('dve', lambda e: e.tensor_tensor(out=t1[:], in0=psw[:, 0:TC], in1=Sb[:], op=ALU.mult),
             reads=[psw, Sb], writes=[t1])
        S.op('dve', lambda e: e.scalar_tensor_tensor(out=t2[:], in0=ps[:, 0:TC], scalar=scale, in1=Cb[:],
                                                    op0=ALU.mult, op1=ALU.mult),
             reads=[ps, Cb], writes=[t2])
        for (p0, p1, dst_ap, dst_buf) in dsts:
            S.op('pool', lambda e: e.tensor_tensor(out=dst_ap, in0=t1[p0:p1, :], in1=t2[p0:p1, :], op=ALU.add),
                 reads=[t1, t2], writes=[dst_buf])

    def project_pass(rope):
        for tc in range(NTC):
            k = cnt["x"] % 2
            cnt["x"] += 1
            xs, xb, Cb, Sb = P["xst"][k], P["xcb"][k], P["ropC"][k], P["ropS"][k]
            S.dma('sp', lambda e: e.dma_start(
                out=xs[:], in_=xT_d[:, tc * TC:(tc + 1) * TC].rearrange("(c p) t -> p c t", p=128)),
                xs, writes=[xs])
            S.op('dve', lambda e: e.tensor_copy(out=xb[:], in_=xs[:]), reads=[xs], writes=[xb])
            if rope:
                S.dma('sp', lambda e: e.dma_start(out=Cb[:], in_=cd["ropeK_C"][:, tc * TC:(tc + 1) * TC]),
                      Cb, writes=[Cb])
                S.dma('sp', lambda e: e.dma_start(out=Sb[:], in_=cd["ropeK_S"][:, tc * TC:(tc + 1) * TC]),
                      Sb, writes=[Sb])
            for hp in range(4):
                ps = PS[cnt["pk"] % 2]
                cnt["pk"] += 1
                for dc in range(8):
                    S.op('pe', lambda e, dc=dc: e.matmul(
                        out=ps[:, 0:TC], lhsT=P["w_bf"][:, dc, 512 + hp * 128:512 + (hp + 1) * 128],
                        rhs=xb[:, dc, :], start=(dc == 0), stop=(dc == 7)),
                        reads=[P["w_bf"], xb], writes=[ps])
                evac("k", rope, ps, [(0, 128, KT[:, hp, tc * TC:(tc + 1) * TC], KTb[hp][tc])], Cb, Sb, 1.0)
            for blk in range(TC // 128):
                g = tc * (TC // 128) + blk
                ps = PS[cnt["pk"] % 2]
                cnt["pk"] += 1
                for dc in range(8):
                    S.op('pe', lambda e, dc=dc: e.matmul(
                        out=ps[:, :], lhsT=xb[:, dc, blk * 128:(blk + 1) * 128],
                        rhs=P["w_bf"][:, dc, 1024:1536], start=(dc == 0), stop=(dc == 7)),
                        reads=[P["w_bf"], xb], writes=[ps])
                S.op('act', lambda e: e.copy(
                    out=V[:, g, :, 0:64], in_=ps[:, :].rearrange("p (h d) -> p h d", d=64)),
                    reads=[ps], writes=[Vb[g]])
        for tc in range(NTCQ):
            k = cnt["x"] % 2
            cnt["x"] += 1
            xs, xb, Cb, Sb = P["xst"][k], P["xcb"][k], P["ropC"][k], P["ropS"][k]
            S.dma('sp', lambda e: e.dma_start(
                out=xs[:], in_=xTq_d[:, tc * TC:(tc + 1) * TC].rearrange("(c p) t -> p c t", p=128)),
                xs, writes=[xs])
            S.op('dve', lambda e: e.tensor_copy(out=xb[:], in_=xs[:]), reads=[xs], writes=[xb])
            if rope:
                S.dma('sp', lambda e: e.dma_start(out=Cb[:], in_=cd["ropeQ_C"][:, tc * TC:(tc + 1) * TC]),
                      Cb, writes=[Cb])
                S.dma('sp', lambda e: e.dma_start(out=Sb[:], in_=cd["ropeQ_S"][:, tc * TC:(tc + 1) * TC]),
                      Sb, writes=[Sb])
            for hp in range(4):
                ps = PS[cnt["pk"] % 2]
                cnt["pk"] += 1
                for dc in range(8):
                    S.op('pe', lambda e, dc=dc: e.matmul(
                        out=ps[:, 0:TC], lhsT=P["w_bf"][:, dc, hp * 128:(hp + 1) * 128],
                        rhs=xb[:, dc, :], start=(dc == 0), stop=(dc == 7)),
                        reads=[P["w_bf"], xb], writes=[ps])
                evac("q", rope, ps, [(0, 64, QT[0:64, 2 * hp, tc * TC:(tc + 1) * TC], QTb[2 * hp][tc]),
                                     (64, 128, QT[64:128, 2 * hp + 1, tc * TC:(tc + 1) * TC], QTb[2 * hp + 1][tc])],
                     Cb, Sb, 0.125)

    def zmatmuls(pz, hb, j, kb):
        for hh in range(4):
            h = hb * 4 + hh
            hp = h // 2
            S.op('pe', lambda e, hh=hh, hp=hp, h=h: e.matmul(
                out=pz[:, hh * 128:(hh + 1) * 128],
                lhsT=KT[:, hp, kb * 128:(kb + 1) * 128],
                rhs=QT[:, h, j * 128:(j + 1) * 128], start=True, stop=True),
                reads=[ktb(hp, kb), qtb(h, j)], writes=[pz])

    def run_pipeline(iters, skews, hook=None):
        N = len(iters)
        for n in range(N + max(skews)):
            if hook is not None:
                hook()
            for st, sk in enumerate(skews):
                i = n - sk
                if 0 <= i < N:
                    iters[i][st]()

    PZ = [PS[0], PS[2], PS[3]]
    PL = [PS[4], PS[5]]

    def attention_da(jlist):
        iters = []
        nb = 0
        for j in jlist:
            for hb in range(2):
                po = PS[6 + (nb % 2)]
                rec = P["rec_t"][nb % 2]
                nb += 1
                kbs = [2 * j + o for o in range(1, -17, -1) if 2 * j + o >= 0]
                for n, kb in enumerate(kbs):
                    it = len(iters)
                    k = it % 3
                    pz, a, am = PZ[k], P["a_t"][k], P["am_t"][k]
                    mi = kb - 2 * j + 16
                    first, last = (n == 0), (n == len(kbs) - 1)

                    def stA(pz=pz, a=a, am=am, hb=hb, j=j, kb=kb, mi=mi):
                        zmatmuls(pz, hb, j, kb)
                        S.op('act', lambda e: e.activation(out=a[:], in_=pz[:, :], func=ACT.Exp),
                             reads=[pz], writes=[a])
                        S.op('dve', lambda e: e.tensor_tensor(
                            out=am[:].rearrange("p (h q) -> p h q", h=4),
                            in0=a[:].rearrange("p (h q) -> p h q", h=4),
                            in1=maskDA[:, mi, :].unsqueeze(1).to_broadcast([128, 4, 128]), op=ALU.mult),
                            reads=[a, maskDA], writes=[am])

                    def stB(po=po, am=am, hb=hb, j=j, kb=kb, first=first, last=last, rec=rec):
                        if first:
                            S.op('pe', lambda e: e.matmul(out=po[:, 0:260], lhsT=zerob[:, 0:128], rhs=zerob[:, 0:260],
                                                          start=True, stop=False, skip_group_check=True),
                                 reads=[zerob], writes=[po])
                        for hh in range(4):
                            h = hb * 4 + hh
                            S.op('pe', lambda e, hh=hh, h=h: e.matmul(
                                out=po[:, hh * 65:(hh + 1) * 65], lhsT=am[:, hh * 128:(hh + 1) * 128],
                                rhs=V[:, kb, h, 0:65], start=False, stop=False, skip_group_check=True),
                                reads=[am, Vb[kb], Vones], writes=[po])
                        if last:
                            S.op('pe', lambda e: e.matmul(out=po[:, 0:260], lhsT=zerob[:, 0:128], rhs=zerob[:, 0:260],
                                                          start=False, stop=True, skip_group_check=True),
                                 reads=[zerob], writes=[po])
                            pov = po[:, 0:260].rearrange("p (h d) -> p h d", d=65)
                            S.op('dve', lambda e: e.reciprocal(out=rec[:], in_=pov[:, :, 64]), reads=[po], writes=[rec])
                            S.op('dve', lambda e: e.tensor_tensor(
                                out=o_all[:, j, hb * 256:(hb + 1) * 256].rearrange("p (h d) -> p h d", d=64),
                                in0=pov[:, :, 0:64], in1=rec[:].unsqueeze(2).to_broadcast([128, 4, 64]), op=ALU.mult),
                                reads=[po, rec], writes=[o_all])

                    iters.append([stA, stB])
        run_pipeline(iters, [0, 1], hook=c0.tick)
        c0.flush()

    def attention_sb(jlist):
        iters = []
        nb = 0
        for j in jlist:
            for hb in range(2):
                po = PS[6 + (nb % 2)]
                LS = P["lsum"][(nb % 2) * 3:(nb % 2) * 3 + 3]
                nb += 1
                kbs = list(range(2 * j + 1, -1, -1))
                for n, kb in enumerate(kbs):
                    it = len(iters)
                    k = it % 3
                    pz, pl = PZ[k], PL[it % 2]
                    e_, sp, l1m, tt, a, am = (P["e_t"][k], P["sp_t"][k], P["l1m_t"][k], P["tt_t"][k],
                                              P["a_t"][k], P["am_t"][k])
                    o = kb - 2 * j
                    masked = o >= 0
                    first, last = (n == 0), (n == len(kbs) - 1)
                    lcur, lnew = LS[n % 3], LS[(n + 1) % 3]
                    mk = maskSB[:, max(o, 0), :].unsqueeze(1).to_broadcast([128, 4, 128])

                    def stA1(pz=pz, e_=e_, sp=sp, hb=hb, j=j, kb=kb):
                        zmatmuls(pz, hb, j, kb)
                        S.op('act', lambda e: e.activation(out=e_[:], in_=pz[:, :], func=ACT.Exp, scale=-1.0),
                             reads=[pz], writes=[e_])
                        S.op('act', lambda e: e.activation(out=sp[:], in_=e_[:], func=ACT.Ln, bias=1.0),
                             reads=[e_], writes=[sp])

                    def stA2(pz=pz, sp=sp, l1m=l1m, masked=masked, mk=mk, first=first, last=last, lcur=lcur, lnew=lnew):
                        S.op('dve', lambda e: e.scalar_tensor_tensor(
                            out=l1m[:], in0=pz[:, :], scalar=-1.0, in1=sp[:], op0=ALU.mult, op1=ALU.subtract),
                            reads=[pz, sp], writes=[l1m])
                        if masked:
                            S.op('pool', lambda e: e.tensor_tensor(
                                out=l1m[:].rearrange("p (h q) -> p h q", h=4),
                                in0=l1m[:].rearrange("p (h q) -> p h q", h=4), in1=mk, op=ALU.mult),
                                reads=[l1m, maskSB], writes=[l1m])
                        if not last:
                            if first:
                                S.op('dve', lambda e: e.tensor_copy(out=lnew[:], in_=l1m[:]),
                                     reads=[l1m], writes=[lnew])
                            else:
                                S.op('dve', lambda e: e.tensor_tensor(out=lnew[:], in0=lcur[:], in1=l1m[:], op=ALU.add),
                                     reads=[l1m, lcur], writes=[lnew])

                    def stB1(pl=pl, sp=sp, l1m=l1m, tt=tt, first=first, lcur=lcur):
                        S.op('pe', lambda e: e.matmul(out=pl[:, :], lhsT=mtri[:], rhs=l1m[:], start=True, stop=first),
                             reads=[mtri, l1m], writes=[pl])
                        if not first:
                            S.op('pe', lambda e: e.matmul(out=pl[:, :], lhsT=onesb[:], rhs=lcur[:], start=False, stop=True),
                                 reads=[onesb, lcur], writes=[pl])
                        S.op('dve', lambda e: e.tensor_tensor(out=tt[:], in0=pl[:, :], in1=sp[:], op=ALU.subtract),
                             reads=[pl, sp], writes=[tt])

                    def stB2(tt=tt, a=a, am=am, masked=masked, mk=mk):
                        S.op('act', lambda e: e.activation(out=a[:], in_=tt[:], func=ACT.Exp), reads=[tt], writes=[a])
                        if masked:
                            S.op('pool', lambda e: e.tensor_tensor(
                                out=am[:].rearrange("p (h q) -> p h q", h=4),
                                in0=a[:].rearrange("p (h q) -> p h q", h=4), in1=mk, op=ALU.mult),
                                reads=[a, maskSB], writes=[am])

                    def stC(po=po, a=a, am=am, masked=masked, hb=hb, j=j, kb=kb, first=first, last=last):
                        src = am if masked else a
                        if first:
                            S.op('pe', lambda e: e.matmul(out=po[:, 0:256], lhsT=zerob[:, 0:128], rhs=zerob[:, 0:256],
                                                          start=True, stop=False, skip_group_check=True),
                                 reads=[zerob], writes=[po])
                        for hh in range(4):
                            h = hb * 4 + hh
                            S.op('pe', lambda e, hh=hh, h=h: e.matmul(
                                out=po[:, hh * 64:(hh + 1) * 64], lhsT=src[:, hh * 128:(hh + 1) * 128],
                                rhs=V[:, kb, h, 0:64], start=False, stop=False, skip_group_check=True),
                                reads=[src, Vb[kb]], writes=[po])
                        if last:
                            S.op('pe', lambda e: e.matmul(out=po[:, 0:256], lhsT=zerob[:, 0:128], rhs=zerob[:, 0:256],
                                                          start=False, stop=True, skip_group_check=True),
                                 reads=[zerob], writes=[po])
                            S.op('act', lambda e: e.copy(out=o_all[:, j, 512 + hb * 256:512 + (hb + 1) * 256],
                                                         in_=po[:, 0:256]), reads=[po], writes=[o_all])

                    iters.append([stA1, stB1, stA2, stB2, stC])
        run_pipeline(iters, [0, 1, 0, 1, 2], hook=c0.tick)
        c0.flush()

    jl = list(range(NQ))
    if stage == "projda":
        jl = []
    if stage in ("ln1", "topk", "peer2") or stage.startswith("topk_"):
        jl = [0, 1]
    if stage == "peer4":
        jl = [0, 1, 2, 3]
    es = ExitStack()
    alloc_proj(es)
    load_pass_weights(0, 512, 1024)
    project_pass(rope=True)
    if dbg and stage in ("projda",):
        dbg_dump("KT", KTb[0][0], KT[:, :, :], [128, 4, SEQ], BF16, [b for r in KTb for b in r])
        dbg_dump("QT", QTb[0][0], QT[:, :, :], [128, 8, 2048], BF16, [b for r in QTb for b in r])
        dbg_dump("V", Vb[0], V[:, :, :, :], [128, NB, 8, 65], BF16, Vb)
        S.finish('sp', list(dbg_outs.values()) + [b for r in KTb for b in r] + [b for r in QTb for b in r] + Vb)
        return nc
    S.barrier()
    es.close()
    es = ExitStack()
    alloc_attn(es)
    attention_da({"da1": [0, 9], "sb1": []}.get(stage, jl))
    if stage in ("da", "da1"):
        dbg_dump("o_all", o_all, o_all[:, :, :], [128, NQ, D], BF16)
        S.finish('sp', [o_all])
        return nc
    S.barrier()
    es.close()
    es = ExitStack()
    alloc_proj(es)
    load_pass_weights(1536, 2048, 2560)
    project_pass(rope=False)
    S.barrier()
    es.close()
    es = ExitStack()
    alloc_attn(es)
    attention_sb({"sb1": [0, 9]}.get(stage, jl))
    if stage in ("sb", "sb1", "attn"):
        dbg_dump("o_all", o_all, o_all[:, :, :], [128, NQ, D], BF16)
        S.finish('sp', [o_all])
        return nc


    S.barrier()
    es.close()
    es_kv.close()
    PSb = [Buf(PS[i].t, f"psc{i}") for i in range(8)]
    if c0.c < 128:
        es_c0 = ExitStack()
        c0.bind(es_c0)
        c0.finish_all()
        S.barrier()
        es_c0.close()
    x1_s = nc.dram_tensor("x1_s", [2048, D], F32).ap()
    esP = ExitStack()
    lnp = S.sb("lnp", [128, 2, D], F32, esP)
    i1T_all = S.sb("i1T_all", [128, 2048], BF16, esP)
    i2T_all = S.sb("i2T_all", [128, 2048], BF16, esP)
    gT_all = S.sb("gT_all", [128, 2048], BF16, esP)
    stats = S.sb("stats", [128, 2, 6], F32, esP)
    mv = S.sb("mv", [128, 2], F32, esP)
    rstd = S.sb("rstd", [128, 1], F32, esP)
    for i, dsrc in enumerate((ln1g_d, ln1b_d)):
        S.dma('sp', lambda e: e.dma_start(out=lnp[:, i, :], in_=dsrc.to_broadcast([128, D])), lnp, writes=[lnp])
    LNS = {"lnp": lnp, "stats": stats, "mv": mv, "rstd": rstd}
    if stage != "full":
        for tb_ in (i1T_all, i2T_all, gT_all):
            S.op('pool', lambda e: e.memset(tb_[:], 0.0), writes=[tb_])
    x1T_s = nc.dram_tensor("x1T_s", [128, 8 * 2048], BF16).ap()
    idx_s = nc.dram_tensor("idx_s", [3, 128, 2048], BF16).ap()
    x1T_sv = x1T_s.rearrange("p (a b) -> p a b", a=8)

    nchunks = 128
    es = ExitStack()
    w_out_bf = S.sb("w_out_bf", [128, 8, D], BF16, es)
    wq_bf = S.sb("wq_bf", [128, 8, 2048], BF16, es)
    keysT = S.sb("keysT", [128, 16, 128], BF16, es)
    def load_c1_weights(stg):
        for dc in range(8):
            S.dma('sp', lambda e: e.dma_start(out=stg[:, 0:D], in_=w_out_d[dc * 128:(dc + 1) * 128, :]), stg, writes=[stg])
            S.op('pool', lambda e: e.tensor_copy(out=w_out_bf[:, dc, :], in_=stg[:, 0:D]), reads=[stg], writes=[w_out_bf])
        for dc in range(8):
            for hh_ in range(2):
                S.dma('sp', lambda e: e.dma_start(out=stg[:, :], in_=wq_d[dc * 128:(dc + 1) * 128, hh_ * 1024:(hh_ + 1) * 1024]),
                      stg, writes=[stg])
                S.op('dve', lambda e: e.tensor_copy(out=wq_bf[:, dc, hh_ * 1024:(hh_ + 1) * 1024], in_=stg[:, :]),
                     reads=[stg], writes=[wq_bf])
        for hpc in range(16):
            S.dma('sp', lambda e: e.dma_start(out=stg[:, 0:128], in_=keys_d[hpc * 128:(hpc + 1) * 128, :]), stg, writes=[stg])
            S.op('pe', lambda e: e.transpose(out=PSb[4][:, 0:128], in_=stg[:, 0:128], identity=ident[:]),
                 reads=[stg, ident], writes=[PSb[4]])
            S.op('act', lambda e: e.copy(out=keysT[:, hpc, :], in_=PSb[4][:, 0:128]), reads=[PSb[4]], writes=[keysT])

    def mkset(tag, pmix, pgen):
        T = {}
        T["xo"] = S.sb(f"xo{tag}", [128, D], F32, es)
        T["oT"] = S.sb(f"oT{tag}", [128, 8, 128], BF16, es)
        T["r1"] = S.sb(f"r1{tag}", [128, D], F32, es)
        T["x1"] = S.sb(f"x1{tag}", [128, D], F32, es)
        T["x1b"] = S.sb(f"x1b{tag}", [128, D], BF16, es)
        T["x1T"] = S.sb(f"x1T{tag}", [128, 8, 128], BF16, es)
        T["qT"] = S.sb(f"qT{tag}", [128, 16, 128], BF16, es)
        T["big"] = S.sb(f"big{tag}", [128, 2048], F32, es)
        T["wk"] = [S.sb(f"wk{tag}{i}", [128, 128], F32, es) for i in range(2)]
        T["cw"] = [S.sb(f"cw{tag}{i}", [128, 256], F32, es) for i in range(2)]
        T["tv"] = S.sb(f"tv{tag}", [128, 16, 16], F32, es)
        T["ti"] = S.sb(f"ti{tag}", [128, 16, 16], U32, es)
        T["tif"] = S.sb(f"tif{tag}", [128, 16, 16], F32, es)
        for nm in ("bs", "paf", "pbf", "dd", "ee", "gates", "i1", "i2"):
            T[nm] = S.sb(f"{nm}{tag}", [128, 8, 16], F32, es)
        for nm in ("bp", "pa", "pb"):
            T[nm] = S.sb(f"{nm}{tag}", [128, 8, 16], U32, es)
        T["ssum"] = S.sb(f"ssum{tag}", [128, 8], F32, es)
        T["rsum"] = S.sb(f"rsum{tag}", [128, 8], F32, es)
        T["stats"] = S.sb(f"stats{tag}", [128, 2, 6], F32, es)
        T["mv"] = S.sb(f"mv{tag}", [128, 2], F32, es)
        T["rstd"] = S.sb(f"rstd{tag}", [128, 1], F32, es)
        T["pmix"] = pmix
        T["pgen"] = pgen
        return T

    TA = mkset("A", [PSb[0], PSb[1]], [PSb[4], PSb[5]])
    TB = mkset("B", [PSb[2], PSb[3]], [PSb[6], PSb[7]])

    def layer_norm(src, dst, gi, bi, T=None):
        lnp = LNS["lnp"]
        TT = T if T is not None else LNS
        stats, mv, rstd = TT["stats"], TT["mv"], TT["rstd"]
        for c in range(2):
            yield S.op('dve', lambda e: e.bn_stats(out=stats[:, c, :], in_=src[:, c * 512:(c + 1) * 512]),
                 reads=[src], writes=[stats])
        yield S.op('dve', lambda e: e.bn_aggr(out=mv[:], in_=stats[:].rearrange("p a b -> p (a b)")),
             reads=[stats], writes=[mv])
        yield S.op('act', lambda e: e.activation(out=rstd[:], in_=mv[:, 1:2], func=ACT.Sqrt, bias=LN_EPS),
             reads=[mv], writes=[rstd])
        yield S.op('dve', lambda e: e.reciprocal(out=rstd[:], in_=rstd[:]), reads=[rstd], writes=[rstd])
        yield S.op('dve', lambda e: e.tensor_scalar(out=dst[:], in0=src[:], scalar1=mv[:, 0:1], scalar2=rstd[:, 0:1],
                                              op0=ALU.subtract, op1=ALU.mult), reads=[src, mv, rstd], writes=[dst])
        yield S.op('pool', lambda e: e.tensor_tensor(out=dst[:], in0=dst[:], in1=lnp[:, gi, :], op=ALU.mult),
             reads=[dst, lnp], writes=[dst])
        yield S.op('pool', lambda e: e.tensor_tensor(out=dst[:], in0=dst[:], in1=lnp[:, bi, :], op=ALU.add),
             reads=[dst, lnp], writes=[dst])

    def ln1_block(j, T):
        X1, X1b, xo, oT, r1 = T["x1"], T["x1b"], T["xo"], T["oT"], T["r1"]
        PM = T["pmix"]
        pT = T["pgen"][0]
        pTb = pT[:, :].bitcast(BF16)
        for fc in range(8):
            yield S.op('pe', lambda e: e.transpose(out=pTb[:, fc * 128:(fc + 1) * 128],
                                            in_=o_all[:, j, fc * 128:(fc + 1) * 128], identity=identb[:]),
                 reads=[o_all, identb], writes=[pT])
        yield S.op('act', lambda e: e.copy(out=oT[:].rearrange("p a b -> p (a b)"), in_=pTb[:, 0:1024]),
             reads=[pT], writes=[oT])
        for half in range(2):
            for fc in range(8):
                yield S.op('pe', lambda e: e.matmul(out=PM[half][:, :], lhsT=oT[:, fc, :],
                                              rhs=w_out_bf[:, fc, half * 512:(half + 1) * 512],
                                              start=(fc == 0), stop=(fc == 7)),
                     reads=[oT, w_out_bf], writes=[PM[half]])
        yield S.dma('sp', lambda e: e.dma_start(out=xo[:], in_=xown_d[j * 128:(j + 1) * 128, :]), xo, writes=[xo])
        for half in range(2):
            yield S.op('dve', lambda e: e.scalar_tensor_tensor(
                out=r1[:, half * 512:(half + 1) * 512], in0=xo[:, half * 512:(half + 1) * 512], scalar=ALPHA,
                in1=PM[half][:, :], op0=ALU.mult, op1=ALU.add), reads=[xo, PM[half]], writes=[r1])
        yield from layer_norm(r1, X1, 0, 1, T)
        yield S.op('act', lambda e: e.copy(out=X1b[:], in_=X1[:]), reads=[X1], writes=[X1b])
        yield S.dma('sp', lambda e: e.dma_start(out=x1_s[j * 128:(j + 1) * 128, :], in_=X1[:]), X1, reads=[X1])

    def topk_block(j, T):
        X1b, x1T, qT, big, wk, cw, tv, ti, tif = (T[k] for k in ("x1b", "x1T", "qT", "big", "wk", "cw", "tv", "ti", "tif"))
        bs, bp, pa, pb, paf, pbf, dd, ee, ssum, rsum, gates, i1, i2 = (T[k] for k in ("bs", "bp", "pa", "pb", "paf", "pbf", "dd", "ee", "ssum", "rsum", "gates", "i1", "i2"))
        PG = T["pgen"]
        pT = PG[0]
        pTb = pT[:, :].bitcast(BF16)
        for dc in range(8):
            yield S.op('pe', lambda e: e.transpose(out=pTb[:, dc * 128:(dc + 1) * 128],
                                            in_=X1b[:, dc * 128:(dc + 1) * 128], identity=identb[:]),
                 reads=[X1b, identb], writes=[pT])
        yield S.op('act', lambda e: e.copy(out=x1T[:].rearrange("p a b -> p (a b)"), in_=pTb[:, 0:1024]),
             reads=[pT], writes=[x1T])
        yield S.dma('sp', lambda e: e.dma_start(out=x1T_sv[:, :, j * 128:(j + 1) * 128], in_=x1T[:]), x1T, reads=[x1T])
        for g4 in range(4):
            bank = PG[g4 % 2]
            for i in range(4):
                hpc = g4 * 4 + i
                for dc in range(8):
                    yield S.op('pe', lambda e: e.matmul(out=bank[:, i * 128:(i + 1) * 128],
                                                  lhsT=wq_bf[:, dc, hpc * 128:(hpc + 1) * 128],
                                                  rhs=x1T[:, dc, :],
                                                  start=(dc == 0), stop=(dc == 7)),
                         reads=[wq_bf, x1T], writes=[bank])
            yield S.op('act', lambda e: e.copy(out=qT[:, g4 * 4:(g4 + 1) * 4, :].rearrange("p a b -> p (a b)"),
                                         in_=bank[:, :]), reads=[bank], writes=[qT])
        sc = big
        for g4 in range(4):
            bank = PG[g4 % 2]
            for i in range(4):
                hpc = g4 * 4 + i
                yield S.op('pe', lambda e: e.matmul(out=bank[:, i * 128:(i + 1) * 128], lhsT=qT[:, hpc, :],
                                              rhs=keysT[:, hpc, :], start=True, stop=True),
                     reads=[qT, keysT], writes=[bank])
            yield S.op('act', lambda e: e.copy(out=sc[:, g4 * 512:(g4 + 1) * 512], in_=bank[:, :]),
                 reads=[bank], writes=[sc])
        for hpc in range(16):
            src = sc[:, hpc * 128:(hpc + 1) * 128]
            w = wk[hpc % 2]
            yield S.op('dve', lambda e: e.max(out=tv[:, hpc, 0:8], in_=src), reads=[sc], writes=[tv])
            yield S.op('dve', lambda e: e.max_index(out=ti[:, hpc, 0:8], in_max=tv[:, hpc, 0:8], in_values=src),
                 reads=[sc, tv], writes=[ti])
            yield S.op('dve', lambda e: e.match_replace(out=w[:], in_to_replace=tv[:, hpc, 0:8], in_values=src,
                                                 imm_value=NEG), reads=[sc, tv], writes=[w])
            yield S.op('dve', lambda e: e.max(out=tv[:, hpc, 8:16], in_=w[:]), reads=[w], writes=[tv])
            yield S.op('dve', lambda e: e.max_index(out=ti[:, hpc, 8:16], in_max=tv[:, hpc, 8:16], in_values=w[:]),
                 reads=[w, tv], writes=[ti])
        yield S.op('dve', lambda e: e.tensor_copy(out=tif[:], in_=ti[:]), reads=[ti], writes=[tif])
        cs = big
        cs4 = cs[:].rearrange("p (h a b) -> p h a b", h=8, a=16)
        yield S.op('dve', lambda e: e.tensor_tensor(
            out=cs4, in0=tv[:, 0::2, :].unsqueeze(3).to_broadcast([128, 8, 16, 16]),
            in1=tv[:, 1::2, :].unsqueeze(2).to_broadcast([128, 8, 16, 16]), op=ALU.add),
            reads=[tv], writes=[cs])
        for h in range(8):
            src = cs[:, h * 256:(h + 1) * 256]
            w = cw[h % 2]
            yield S.op('dve', lambda e: e.max(out=bs[:, h, 0:8], in_=src), reads=[cs], writes=[bs])
            yield S.op('dve', lambda e: e.max_index(out=bp[:, h, 0:8], in_max=bs[:, h, 0:8], in_values=src),
                 reads=[cs, bs], writes=[bp])
            yield S.op('dve', lambda e: e.match_replace(out=w[:], in_to_replace=bs[:, h, 0:8], in_values=src,
                                                 imm_value=NEG), reads=[cs, bs], writes=[w])
            yield S.op('dve', lambda e: e.max(out=bs[:, h, 8:16], in_=w[:]), reads=[w], writes=[bs])
            yield S.op('dve', lambda e: e.max_index(out=bp[:, h, 8:16], in_max=bs[:, h, 8:16], in_values=w[:]),
                 reads=[w, bs], writes=[bp])
        yield S.op('dve', lambda e: e.tensor_tensor(out=dd[:], in0=bs[:], in1=bs[:, :, 0:1].to_broadcast([128, 8, 16]),
                                              op=ALU.subtract), reads=[bs], writes=[dd])
        yield S.op('act', lambda e: e.activation(out=ee[:], in_=dd[:], func=ACT.Exp), reads=[dd], writes=[ee])
        yield S.op('dve', lambda e: e.tensor_reduce(out=ssum[:], in_=ee[:], axis=AX.X, op=ALU.add), reads=[ee], writes=[ssum])
        yield S.op('dve', lambda e: e.reciprocal(out=rsum[:], in_=ssum[:]), reads=[ssum], writes=[rsum])
        yield S.op('dve', lambda e: e.tensor_tensor(out=gates[:], in0=ee[:], in1=rsum[:].unsqueeze(2).to_broadcast([128, 8, 16]),
                                              op=ALU.mult), reads=[ee, rsum], writes=[gates])
        yield S.op('dve', lambda e: e.tensor_scalar(out=pa[:], in0=bp[:], scalar1=4, scalar2=None,
                                              op0=ALU.logical_shift_right), reads=[bp], writes=[pa])
        yield S.op('dve', lambda e: e.tensor_scalar(out=pb[:], in0=bp[:], scalar1=15, scalar2=None,
                                              op0=ALU.bitwise_and), reads=[bp], writes=[pb])
        yield S.op('dve', lambda e: e.tensor_copy(out=paf[:], in_=pa[:]), reads=[pa], writes=[paf])
        yield S.op('dve', lambda e: e.tensor_copy(out=pbf[:], in_=pb[:]), reads=[pb], writes=[pbf])
        eq = big
        eq4 = eq[:].rearrange("p (h k a) -> p h k a", h=8, k=16)
        io4 = iota16[:].unsqueeze(1).unsqueeze(1).to_broadcast([128, 8, 16, 16])
        for (pf, par, dst) in ((paf, 0, i1), (pbf, 1, i2)):
            yield S.op('dve', lambda e: e.tensor_tensor(out=eq4, in0=pf[:].unsqueeze(3).to_broadcast([128, 8, 16, 16]),
                                                  in1=io4, op=ALU.is_equal), reads=[pf, iota16], writes=[eq])
            yield S.op('dve', lambda e: e.tensor_tensor(out=eq4, in0=eq4,
                                                  in1=tif[:, par::2, :].unsqueeze(2).to_broadcast([128, 8, 16, 16]),
                                                  op=ALU.mult), reads=[eq, tif], writes=[eq])
            yield S.op('dve', lambda e: e.tensor_reduce(out=dst[:], in_=eq4, axis=AX.X, op=ALU.add), reads=[eq], writes=[dst])
        pt2 = PG[1]
        for n3, (srcb, dstb) in enumerate(((i1, i1T_all), (i2, i2T_all), (gates, gT_all))):
            yield S.op('pe', lambda e: e.transpose(out=pt2[:, n3 * 128:(n3 + 1) * 128],
                                            in_=srcb[:].rearrange("p a b -> p (a b)"), identity=ident[:]),
                 reads=[srcb, ident], writes=[pt2])
        for n3, (srcb, dstb) in enumerate(((i1, i1T_all), (i2, i2T_all), (gates, gT_all))):
            yield S.op('act', lambda e: e.copy(out=dstb[:, j * 128:(j + 1) * 128], in_=pt2[:, n3 * 128:(n3 + 1) * 128]),
                 reads=[pt2], writes=[dstb])

    es_stg = ExitStack()
    stg_ = S.sb("stg", [128, 1024], F32, es_stg)
    load_c1_weights(stg_)
    S.barrier()
    es_stg.close()
    nblk = {"peer2": 2, "peer4": 4}.get(stage, NQ)
    def blk(j, T):
        yield from ln1_block(j, T)
        yield from topk_block(j, T)

    for jp in range(0, nblk, 2):
        gens = [blk(jp, TA), blk(jp + 1, TB)]
        while gens:
            for gq in list(gens):
                try:
                    next(gq)
                except StopIteration:
                    gens.remove(gq)
    for n3, tb_ in enumerate((i1T_all, i2T_all, gT_all)):
        S.dma('sp', lambda e: e.dma_start(out=idx_s[n3], in_=tb_[:]), tb_, reads=[tb_])
    S.barrier()
    es.close()
    esP.close()
    es_o.close()

    es = ExitStack()
    GT = 256
    SBT = 16
    lnp = S.sb("lnp2", [128, 2, D], F32, es)
    stats = S.sb("stats2", [128, 2, 6], F32, es)
    mv = S.sb("mv2", [128, 2], F32, es)
    rstd = S.sb("rstd2", [128, 1], F32, es)
    LNS.update({"lnp": lnp, "stats": stats, "mv": mv, "rstd": rstd})
    for i, dsrc in enumerate((ln2g_d, ln2b_d)):
        S.dma('sp', lambda e: e.dma_start(out=lnp[:, i, :], in_=dsrc.to_broadcast([128, D])), lnp, writes=[lnp])
    x1Tg = [S.sb(f"x1Tg{i}", [128, 8, GT], BF16, es) for i in range(2)]
    idxg = [[S.sb(f"idxg{i}_{n3}", [128, GT], BF16, es) for n3 in range(3)] for i in range(2)]
    Wg2 = [S.sb(f"Wg{i}", [128, GT, 128], BF16, es) for i in range(2)]
    UTr = [S.sb(f"UTr{i}", [128, D], BF16, es) for i in range(4)]
    Vr = [S.sb(f"Vr{i}", [128, D], BF16, es) for i in range(6)]
    OJ = [S.sb(f"OJ{i}", [128, SBT, 128], BF16, es) for i in range(2)]
    PI = [S.sb(f"PI{i}", [128, SBT, 128], BF16, es) for i in range(2)]
    gelb = [S.sb(f"gelb{i}", [128, GT], BF16, es) for i in range(2)]
    actb = [S.sb(f"actb{i}", [128, GT], BF16, es) for i in range(2)]
    x1r = S.sb("x1r", [128, D], F32, es)
    ysb = S.sb("ysb", [128, 4, 512], F32, es)
    r2 = x1r
    yout = x1r
    YB = PSb[0:4]
    HP = PSb[4:6]
    WP = PSb[6:8]
    io3 = iota128[:].unsqueeze(1).to_broadcast([128, SBT, 128])
    cc = {"w": 0}
    ngroups = nblk // 2
    def group_load(g):
        xt = x1Tg[g % 2]
        S.dma('sp', lambda e: e.dma_start(out=xt[:], in_=x1T_sv[:, :, g * GT:(g + 1) * GT]), xt, writes=[xt])
        for n3 in range(3):
            tb_ = idxg[g % 2][n3]
            S.dma('sp', lambda e: e.dma_start(out=tb_[:], in_=idx_s[n3][:, g * GT:(g + 1) * GT]), tb_, writes=[tb_])

    def w_build_stages(g, sbt):
        Wg = Wg2[g % 2]
        i1g, i2g, gg = idxg[g % 2]
        t0 = sbt * SBT
        oj, pi = OJ[sbt % 2], PI[sbt % 2]

        def s_oj():
            S.op('dve', lambda e: e.tensor_tensor(
                out=oj[:], in0=io3, in1=i2g[:, t0:t0 + SBT].unsqueeze(2).to_broadcast([128, SBT, 128]),
                op=ALU.is_equal), reads=[iota128, i2g], writes=[oj])

        def s_pi():
            S.op('dve', lambda e: e.tensor_tensor(
                out=pi[:], in0=io3, in1=i1g[:, t0:t0 + SBT].unsqueeze(2).to_broadcast([128, SBT, 128]),
                op=ALU.is_equal), reads=[iota128, i1g], writes=[pi])

        def s_gate():
            S.op('pool', lambda e: e.tensor_tensor(
                out=pi[:], in0=pi[:], in1=gg[:, t0:t0 + SBT].unsqueeze(2).to_broadcast([128, SBT, 128]),
                op=ALU.mult), reads=[pi, gg], writes=[pi])

        def mk_mm(q):
            def s_mm():
                wp = WP[(sbt * (SBT // 4) + q) % 2]
                for u in range(4):
                    tl = q * 4 + u
                    S.op('pe', lambda e: e.matmul(out=wp[:, u * 128:(u + 1) * 128], lhsT=oj[:, tl, :], rhs=pi[:, tl, :],
                                                  start=True, stop=True), reads=[oj, pi], writes=[wp])
            return s_mm

        def mk_ev(q):
            def s_ev():
                wp = WP[(sbt * (SBT // 4) + q) % 2]
                tg = t0 + q * 4
                dst = Wg[:, tg:tg + 4, :].rearrange("p t i -> p (t i)")
                S.op('act', lambda e: e.copy(out=dst, in_=wp[:, :]), reads=[wp], writes=[Wg])
            return s_ev

        st = [(0, s_oj), (1, s_pi), (4, s_gate)]
        for q in range(SBT // 4):
            st.append((7 + q, mk_mm(q)))
            st.append((8 + q, mk_ev(q)))
        return st

    def w_build_sub(g, sbt):
        for _, fn in w_build_stages(g, sbt):
            fn()

    def chunk_stages(g, c, n):
        k2 = n % 2
        ut, vv, hp, gl, ab = UTr[n % 4], Vr[n % 6], HP[k2], gelb[k2], actb[k2]
        Wg, xt = Wg2[g % 2], x1Tg[g % 2]

        def sD():
            S.dma('sp', lambda e: e.dma_start(out=ut[:], in_=ut_s[c]), ut, writes=[ut])
            S.dma('sp', lambda e: e.dma_start(out=vv[:], in_=v_s[c * 128:(c + 1) * 128, :]), vv, writes=[vv])

        def sH():
            for dc in range(8):
                S.op('pe', lambda e: e.matmul(out=hp[:, 0:GT], lhsT=ut[:, dc * 128:(dc + 1) * 128],
                                              rhs=xt[:, dc, :],
                                              start=(dc == 0), stop=(dc == 7)), reads=[ut, xt], writes=[hp])
            S.op('act', lambda e: e.activation(out=gl[:], in_=hp[:, 0:GT], func=ACT.Gelu), reads=[hp], writes=[gl])

        def sM():
            meng = 'pool' if (c % 8) in (1, 4, 6) else 'dve'
            S.op(meng, lambda e: e.tensor_tensor(out=ab[:], in0=gl[:], in1=Wg[:, :, c], op=ALU.mult),
                 reads=[gl, Wg], writes=[ab])

        def sY():
            for tb in range(2):
                for half in range(2):
                    S.op('pe', lambda e: e.matmul(out=YB[tb * 2 + half][:, :], lhsT=ab[:, tb * 128:(tb + 1) * 128],
                                                  rhs=vv[:, half * 512:(half + 1) * 512],
                                                  start=(c == 0), stop=(c == nchunks - 1)),
                         reads=[ab, vv], writes=[YB[tb * 2 + half]])
            if c == nchunks - 1:
                final_group(g)

        return [sD, sH, sM, sY]

    def final_group(g):
        for k4 in range(4):
            S.op('act', lambda e: e.copy(out=ysb[:, k4, :], in_=YB[k4][:, :]), reads=[YB[k4]], writes=[ysb])
        for tb in range(2):
            j = g * 2 + tb
            S.dma('sp', lambda e: e.dma_start(out=x1r[:], in_=x1_s[j * 128:(j + 1) * 128, :]), x1r, writes=[x1r])
            S.op('dve', lambda e: e.scalar_tensor_tensor(
                out=r2[:], in0=x1r[:], scalar=ALPHA,
                in1=ysb[:, tb * 2:tb * 2 + 2, :].rearrange("p a b -> p (a b)"), op0=ALU.mult, op1=ALU.add),
                reads=[x1r, ysb], writes=[r2])
            for _ in layer_norm(r2, yout, 0, 1):
                pass
            S.dma('sp', lambda e: e.dma_start(out=out_d[j * 128:(j + 1) * 128, :], in_=yout[:]), yout, reads=[yout])

    group_load(0)
    for sbt in range(GT // SBT):
        w_build_sub(0, sbt)
    iters = []
    for g in range(ngroups):
        for c in range(nchunks):
            iters.append(chunk_stages(g, c, len(iters)))
    step = {"n": 0}
    NSB = GT // SBT

    sched = {}
    for g in range(ngroups - 1):
        for k in range(NSB):
            base = g * nchunks + 8 + 6 * k
            for rel, fn in w_build_stages(g + 1, k):
                sched.setdefault(base + rel, []).append(fn)

    def c2_hook():
        n = step["n"]
        step["n"] += 1
        g, r = n // nchunks, n % nchunks
        if g + 1 < ngroups and r == 4:
            group_load(g + 1)
        for fn in sched.get(n, []):
            fn()

    run_pipeline(iters, [0, 2, 3, 4], hook=c2_hook)
    S.finish('sp', [yout])
    return nc


def make_in_maps(inputs):
    x = np.ascontiguousarray(inputs["x"], dtype=np.float32)
    w_in = np.ascontiguousarray(inputs["w_in"][0], dtype=np.float32)
    w_out = np.ascontiguousarray(inputs["w_out"][0], dtype=np.float32)
    wq = np.ascontiguousarray(inputs["peer_wq"][0], dtype=np.float32)
    keys = np.ascontiguousarray(inputs["peer_sub_keys"][0], dtype=np.float32).reshape(16 * 128, 128)
    pu = np.ascontiguousarray(inputs["peer_u"][0], dtype=np.float32)
    pv = np.ascontiguousarray(inputs["peer_v"][0], dtype=np.float32)
    maps = []
    for c in range(8):
        b, hf = c // 2, c % 2
        own = np.concatenate([(2 * j + hf) * 128 + np.arange(128) for j in range(NQ)])
        xb = x[b]
        m = {
            "xT": np.ascontiguousarray(xb.T),
            "xTq": np.ascontiguousarray(xb[own].T),
            "xown": np.ascontiguousarray(xb[own]),
            "w_in": w_in, "w_out": w_out, "wq": wq, "keys": keys,
            "peer_u": pu, "peer_v": pv,
            "ln1_g": np.ascontiguousarray(inputs["ln1_g"], dtype=np.float32).reshape(1, D),
            "ln1_b": np.ascontiguousarray(inputs["ln1_b"], dtype=np.float32).reshape(1, D),
            "ln2_g": np.ascontiguousarray(inputs["ln2_g"], dtype=np.float32).reshape(1, D),
            "ln2_b": np.ascontiguousarray(inputs["ln2_b"], dtype=np.float32).reshape(1, D),
        }
        for k, v in host_consts(hf).items():
            m["c_" + k] = np.ascontiguousarray(v, dtype=np.float32)
        maps.append(m)
    return maps


def kernel(**inputs):
    nc = build("full")
    maps = make_in_maps(inputs)
    res = run_bass_kernel_spmd(nc, maps, core_ids=list(range(8)))
    out = np.zeros((4, SEQ, D), np.float32)
    for c in range(8):
        b, hf = c // 2, c % 2
        o = res.results[c]["out"]
        for j in range(NQ):
            g = 2 * j + hf
            out[b, g * 128:(g + 1) * 128] = o[j * 128:(j + 1) * 128]
    return out
```

```python
import os
import numpy as np
import concourse.bass as bass
import concourse.mybir as mybir
from concourse.bass import IndirectOffsetOnAxis
from concourse.bass_utils import run_bass_kernel_spmd
from contextlib import ExitStack

F32 = mybir.dt.float32
BF16 = mybir.dt.bfloat16
U32 = mybir.dt.uint32
I32 = mybir.dt.int32
ACT = mybir.ActivationFunctionType
ALU = mybir.AluOpType
AX = mybir.AxisListType

D = 1024
SEQ = 4096
NB = 32
NQ = 16
ALPHA = 2.0 ** 0.25
LN_EPS = 1e-5
THETA = 500000.0
NEG = -1.0e30


class StopBuild(Exception):
    pass


class Buf:
    _n = 0

    def __init__(self, t, name=None):
        self.t = t
        self.w = None
        self.r = {}
        self.dsem = None
        self.dcnt = 0
        Buf._n += 1
        self.name = name or f"b{Buf._n}"

    def __getitem__(self, k):
        return self.t[k]


class Sched:
    def __init__(self, nc):
        self.nc = nc
        self.eng = {'pe': nc.tensor, 'act': nc.scalar, 'dve': nc.vector,
                    'pool': nc.gpsimd, 'sp': nc.sync}
        self.sem, self.cnt, self.seen = {}, {}, {}
        for k in self.eng:
            self.sem[k] = nc.alloc_semaphore(f"s_{k}")
            self.cnt[k] = 0
            self.seen[k] = {}
        self.n_ins = 0
        self.n_wait = 0
        self.allbufs = []

    def sb(self, name, shape, dt, es=None):
        self._uid = getattr(self, "_uid", 0) + 1
        uname = f"{name}_{self._uid}"
        if es is None:
            t = self.nc.alloc_sbuf_tensor(uname, shape, dt)
        else:
            t = es.enter_context(self.nc.sbuf_tensor(uname, shape, dt))
        b = Buf(t, uname)
        self.allbufs.append(b)
        return b

    def barrier(self):
        evs = [(self.sem[k], self.cnt[k]) for k in self.eng if self.cnt[k]]
        for b in self.allbufs:
            if b.dsem is not None and b.dcnt:
                evs.append((b.dsem, b.dcnt))
        for e in self.eng:
            self._wait(e, [ev for ev in evs if ev[0] is not self.sem[e]])

    def _wait(self, e, deps):
        E = self.eng[e]
        seen = self.seen[e]
        best = {}
        for ev in deps:
            if ev is None:
                continue
            sem, val = ev
            key = id(sem)
            if key not in best or best[key][1] < val:
                best[key] = (sem, val)
        for key, (sem, val) in best.items():
            if seen.get(key, 0) >= val:
                continue
            if e == 'pe' and sem is self.sem['pe']:
                continue
            E.wait_ge(sem, val)
            self.n_wait += 1
            seen[key] = val

    def _deps(self, reads, writes):
        deps = []
        for b in reads:
            deps.append(b.w)
        for b in writes:
            deps.append(b.w)
            deps.extend(b.r.values())
        return deps

    def _mark(self, ev, reads, writes):
        for b in writes:
            b.w = ev
            b.r = {}
        for b in reads:
            if b in writes:
                continue
            b.r[id(ev[0])] = ev

    def op(self, e, fn, reads=(), writes=()):
        self._wait(e, self._deps(reads, writes))
        ins = fn(self.eng[e])
        self.cnt[e] += 1
        ins.then_inc(self.sem[e], 1)
        self._mark((self.sem[e], self.cnt[e]), reads, writes)
        self.n_ins += 1
        return ins

    def dma(self, q, fn, track, reads=(), writes=()):
        if track.dsem is None:
            track.dsem = self.nc.alloc_semaphore(f"d_{track.name}")
        deps = self._deps(reads, writes)
        if track.dcnt:
            deps.append((track.dsem, track.dcnt))
        self._wait(q, deps)
        ins = fn(self.eng[q])
        track.dcnt += 16
        ins.then_inc(track.dsem, 16)
        self._mark((track.dsem, track.dcnt), reads, writes)
        self.n_ins += 1
        return ins

    def finish(self, e, bufs):
        deps = []
        for b in bufs:
            deps.append(b.w)
            deps.extend(b.r.values())
            if b.dsem is not None and b.dcnt:
                deps.append((b.dsem, b.dcnt))
        self._wait(e, deps)


def _mult(delta):
    m = ((delta >= 0) & (delta <= 128)).astype(np.float32)
    m += ((delta >= 0) & (delta <= 512) & (delta % 4 == 0)).astype(np.float32)
    m += ((delta >= 0) & (delta <= 2048) & (delta % 16 == 0)).astype(np.float32)
    return m


def host_consts(hf):
    f32 = np.float32
    c = {}
    c["ident"] = np.eye(128, dtype=f32)
    jj = np.arange(128)
    c["mtri"] = (jj[:, None] > jj[None, :]).astype(f32)
    rmat = np.zeros((128, 128), f32)
    for p in range(128):
        i = p % 64
        if i < 8:
            rmat[p + 8, p] = -1.0
        elif i < 16:
            rmat[p - 8, p] = 1.0
    c["rmat"] = rmat
    z = np.zeros((128, 255), f32)
    z[:, 127] = 1.0
    c["zcol"] = z
    c["iota16"] = np.tile(np.arange(16, dtype=f32)[None, :], (128, 1))
    c["iota128"] = np.tile(np.arange(128, dtype=f32)[None, :], (128, 1))
    inv_freq = (f32(THETA) ** (-(np.arange(8, dtype=f32) * f32(2.0) / f32(16)))).astype(f32)

    def rope(pos):
        ang = pos.astype(f32)[:, None] * inv_freq[None, :]
        cos, sin = np.cos(ang).astype(f32), np.sin(ang).astype(f32)
        C = np.ones((128, pos.shape[0]), f32)
        Sn = np.zeros((128, pos.shape[0]), f32)
        for p in range(128):
            i = p % 64
            if i < 16:
                C[p] = cos[:, i % 8]
                Sn[p] = sin[:, i % 8]
        return C, Sn

    c["ropeK_C"], c["ropeK_S"] = rope(np.arange(SEQ))
    own = np.concatenate([(2 * j + hf) * 128 + np.arange(128) for j in range(NQ)])
    c["ropeQ_C"], c["ropeQ_S"] = rope(own)
    s = np.arange(128)[:, None]
    q = np.arange(128)[None, :]
    mda = np.zeros((128, 18, 128), f32)
    for mi in range(18):
        o = mi - 16
        mda[:, mi, :] = _mult((hf - o) * 128 + q - s)
    c["maskDA"] = mda
    msb = np.zeros((128, 2, 128), f32)
    for o in range(2):
        msb[:, o, :] = (((hf - o) * 128 + q - s) > 0).astype(f32)
    c["maskSB"] = msb
    return c


CONST_SHAPES = {
    "ident": [128, 128], "mtri": [128, 128], "rmat": [128, 128], "zcol": [128, 255],
    "iota16": [128, 16], "iota128": [128, 128], "ropeK_C": [128, SEQ], "ropeK_S": [128, SEQ],
    "ropeQ_C": [128, 2048], "ropeQ_S": [128, 2048],
    "maskDA": [128, 18, 128], "maskSB": [128, 2, 128],
}


def build(stage="full", dbg=False):
    nc = bass.Bass("TRN2", target_bir_lowering=False)
    S = Sched(nc)

    def din(name, shape, dt=F32):
        return nc.dram_tensor(name, shape, dt, kind="ExternalInput").ap()

    xT_d = din("xT", [D, SEQ])
    xTq_d = din("xTq", [D, 2048])
    xown_d = din("xown", [2048, D])
    w_in_d = din("w_in", [D, 3072])
    w_out_d = din("w_out", [D, D])
    ln1g_d = din("ln1_g", [1, D])
    ln1b_d = din("ln1_b", [1, D])
    wq_d = din("wq", [D, 2048])
    keys_d = din("keys", [16 * 128, 128])
    pu_d = din("peer_u", [16384, D])
    pv_d = din("peer_v", [16384, D])
    ln2g_d = din("ln2_g", [1, D])
    ln2b_d = din("ln2_b", [1, D])
    cd = {k: din("c_" + k, v) for k, v in CONST_SHAPES.items()}
    out_d = nc.dram_tensor("out", [2048, D], F32, kind="ExternalOutput").ap()
    dbg_outs = {}

    def dbg_dump(name, buf, ap, shape, dt=F32, allb=()):
        if not dbg:
            return
        t = nc.dram_tensor("dbg_" + name, shape, dt, kind="ExternalOutput").ap()
        S._wait('sp', [b.w for b in allb])
        S.dma('sp', lambda e: e.dma_start(out=t, in_=ap), buf, reads=[buf])
        dbg_outs[name] = buf

    PS = [Buf(nc.alloc_psum_tensor(f"ps{i}", [128, 512], F32), f"ps{i}") for i in range(8)]

    ident = S.sb("ident", [128, 128], F32)
    identb = S.sb("identb", [128, 128], BF16)
    mtri = S.sb("mtri", [128, 128], BF16)
    onesb = S.sb("onesb", [128, 128], BF16)
    zerob = S.sb("zerob", [128, 512], BF16)
    rmat = S.sb("rmat", [128, 128], BF16)
    zcol = S.sb("zcol", [128, 255], BF16)
    iota16 = S.sb("iota16", [128, 16], F32)
    iota128 = S.sb("iota128", [128, 128], F32)
    maskDA = S.sb("maskDA", [128, 18, 128], BF16)
    maskSB = S.sb("maskSB", [128, 2, 128], BF16)
    es_o = ExitStack()
    o_all = S.sb("o_all", [128, NQ, D], BF16, es_o)
    if dbg:
        S.op('pool', lambda e: e.memset(o_all[:], 0.0), writes=[o_all])
    es0 = ExitStack()
    cstage = S.sb("cstage", [128, 18 * 128], F32, es0)

    def load_const(dst, name, n):
        src = cd[name]
        if len(src.shape) == 3:
            src = src.rearrange("p a b -> p (a b)")
        S.dma('sp', lambda e: e.dma_start(out=cstage[:, 0:n], in_=src), cstage, writes=[cstage])
        dflat = dst[:]
        if len(dflat.shape) == 3:
            dflat = dflat.rearrange("p a b -> p (a b)")
        S.op('dve', lambda e: e.tensor_copy(out=dflat, in_=cstage[:, 0:n]), reads=[cstage], writes=[dst])

    S.dma('sp', lambda e: e.dma_start(out=ident[:], in_=cd["ident"]), ident, writes=[ident])
    S.dma('sp', lambda e: e.dma_start(out=iota16[:], in_=cd["iota16"]), iota16, writes=[iota16])
    S.dma('sp', lambda e: e.dma_start(out=iota128[:], in_=cd["iota128"]), iota128, writes=[iota128])
    load_const(identb, "ident", 128)
    load_const(mtri, "mtri", 128)
    load_const(rmat, "rmat", 128)
    load_const(zcol, "zcol", 255)
    load_const(maskDA, "maskDA", 18 * 128)
    load_const(maskSB, "maskSB", 2 * 128)
    S.op('pool', lambda e: e.memset(onesb[:], 1.0), writes=[onesb])
    S.op('pool', lambda e: e.memset(zerob[:], 0.0), writes=[zerob])
    S.barrier()
    es0.close()

    TC = 256
    NTC = SEQ // TC
    NTCQ = 2048 // TC

    es_kv = ExitStack()
    KT = es_kv.enter_context(nc.sbuf_tensor("KT", [128, 4, SEQ], BF16))
    QT = es_kv.enter_context(nc.sbuf_tensor("QT", [128, 8, 2048], BF16))
    V = es_kv.enter_context(nc.sbuf_tensor("V", [128, NB, 8, 65], BF16))
    KTb = [[Buf(KT, f"KT{hp}_{tc}") for tc in range(NTC)] for hp in range(4)]
    QTb = [[Buf(QT, f"QT{h}_{tc}") for tc in range(NTCQ)] for h in range(8)]
    S.op('pool', lambda e: e.memset(QT[:, :, :], 0.0), writes=[b for r in QTb for b in r])
    Vb = [Buf(V, f"V{g}") for g in range(NB)]
    Vones = Buf(V, "Vones")
    S.op('pool', lambda e: e.memset(V[:, :, :, 64:65], 1.0), writes=[Vones])

    def ktb(hp, kb):
        return KTb[hp][(kb * 128) // TC]

    def qtb(h, j):
        return QTb[h][(j * 128) // TC]

    ut_s = nc.dram_tensor("ut_s", [128, 128, 1024], BF16).ap()
    v_s = nc.dram_tensor("v_s", [16384, D], BF16).ap()

    class C0:
        NPH = 7

        def __init__(self):
            self.c = 0
            self.phase = 0
            self.loaded = set()
            self.t = None
            self.prefetch = True

        def bind(self, es_):
            self.t = {
                "us": [S.sb(f"ustg{i}", [128, D], F32, es_) for i in range(2)],
                "vs": [S.sb(f"vstg{i}", [128, D], F32, es_) for i in range(2)],
                "ub": S.sb("utb", [128, D], BF16, es_),
                "vb": S.sb("vbb", [128, D], BF16, es_),
                "bank": Buf(PS[1].t, f"c0bank{self.c}"),
            }
            self.loaded = set()

        def load(self, c):
            if c in self.loaded or c >= 128:
                return
            us, vs = self.t["us"][c % 2], self.t["vs"][c % 2]
            S.dma('sp', lambda e: e.dma_start(out=us[:], in_=pu_d[c * 128:(c + 1) * 128, :]), us, writes=[us])
            S.dma('sp', lambda e: e.dma_start(out=vs[:], in_=pv_d[c * 128:(c + 1) * 128, :]), vs, writes=[vs])
            self.loaded.add(c)

        def pe_round(self, c, r):
            us, bank = self.t["us"][c % 2], self.t["bank"]
            for q4 in range(4):
                dc = r * 4 + q4
                S.op('pe', lambda e: e.transpose(out=bank[:, q4 * 128:(q4 + 1) * 128],
                                                in_=us[:, dc * 128:(dc + 1) * 128], identity=ident[:]),
                     reads=[us, ident], writes=[bank])

        def act_round(self, c, r):
            ub, bank = self.t["ub"], self.t["bank"]
            S.op('act', lambda e: e.copy(out=ub[:, r * 512:(r + 1) * 512], in_=bank[:, :]), reads=[bank], writes=[ub])

        def cast(self, c):
            vs, vb = self.t["vs"][c % 2], self.t["vb"]
            S.op('pool', lambda e: e.tensor_copy(out=vb[:], in_=vs[:]), reads=[vs], writes=[vb])

        def store(self, c):
            ub, vb = self.t["ub"], self.t["vb"]
            S.dma('sp', lambda e: e.dma_start(out=ut_s[c], in_=ub[:]), ub, reads=[ub])
            S.dma('sp', lambda e: e.dma_start(out=v_s[c * 128:(c + 1) * 128, :], in_=vb[:]), vb, reads=[vb])

        def tick(self):
            if self.c >= 128:
                return
            c, ph = self.c, self.phase
            if ph == 0:
                self.load(c)
                self.pe_round(c, 0)
                if self.prefetch:
                    self.load(c + 1)
            elif ph == 1:
                self.act_round(c, 0)
            elif ph == 2:
                self.pe_round(c, 1)
            elif ph == 3:
                self.act_round(c, 1)
                self.cast(c)
            elif ph == 4:
                self.store(c)
            self.phase += 1
            if self.phase == self.NPH:
                self.phase = 0
                self.loaded.discard(c)
                self.c += 1

        def flush(self):
            self.prefetch = False
            while self.c < 128 and (self.phase != 0 or self.c in self.loaded):
                self.tick()
            self.prefetch = True

        def finish_all(self):
            while self.c < 128:
                self.tick()

    c0 = C0()
    P = {}

    def alloc_proj(es):
        P["w_bf"] = S.sb("w_bf", [128, 8, 1536], BF16, es)
        P["wst"] = [S.sb(f"wst{i}", [128, 1536], F32, es) for i in range(1)]
        P["xst"] = [S.sb(f"xst{i}", [128, 8, TC], F32, es) for i in range(2)]
        P["xcb"] = [S.sb(f"xcb{i}", [128, 8, TC], BF16, es) for i in range(2)]
        P["ropC"] = [S.sb(f"ropC{i}", [128, TC], F32, es) for i in range(2)]
        P["ropS"] = [S.sb(f"ropS{i}", [128, TC], F32, es) for i in range(2)]
        P["rawb"] = [S.sb(f"rawb{i}", [128, TC], BF16, es) for i in range(2)]
        P["rt1"] = [S.sb(f"rt1_{i}", [128, TC], F32, es) for i in range(2)]
        P["rt2"] = [S.sb(f"rt2_{i}", [128, TC], F32, es) for i in range(2)]

    def alloc_attn(es):
        P["e_t"] = [S.sb(f"e_t{i}", [128, 512], F32, es) for i in range(3)]
        P["sp_t"] = [S.sb(f"sp_t{i}", [128, 512], F32, es) for i in range(5)]
        P["l1m_t"] = [S.sb(f"l1m_t{i}", [128, 512], BF16, es) for i in range(4)]
        P["tt_t"] = [S.sb(f"tt_t{i}", [128, 512], F32, es) for i in range(4)]
        P["a_t"] = [S.sb(f"a_t{i}", [128, 512], BF16, es) for i in range(5)]
        P["am_t"] = [S.sb(f"am_t{i}", [128, 512], BF16, es) for i in range(5)]
        P["lsum"] = [S.sb(f"lsum{i}", [128, 512], BF16, es) for i in range(6)]
        P["rec_t"] = [S.sb(f"rec_t{i}", [128, 4], F32, es) for i in range(2)]
        c0.bind(es)

    cnt = {"w": 0, "x": 0, "pk": 0, "ev": 0}

    def load_pass_weights(colq, colk, colv):
        for dc in range(8):
            st = P["wst"][0]
            cnt["w"] += 1
            for i, c0 in enumerate((colq, colk, colv)):
                q = 'sp'
                S.dma(q, lambda e, c0=c0, i=i: e.dma_start(
                    out=st[:, i * 512:(i + 1) * 512], in_=w_in_d[dc * 128:(dc + 1) * 128, c0:c0 + 512]),
                    st, writes=[st])
            S.op('pool', lambda e: e.tensor_copy(out=P["w_bf"][:, dc, :], in_=st[:]), reads=[st], writes=[P["w_bf"]])

    def evac(kind, rope, ps, dsts, Cb, Sb, scale):
        k = cnt["ev"] % 2
        cnt["ev"] += 1
        if not rope:
            for (p0, p1, dst_ap, dst_buf) in dsts:
                S.op('act', lambda e: e.mul(out=dst_ap, in_=ps[p0:p1, 0:TC], mul=scale),
                     reads=[ps], writes=[dst_buf])
            return
        raw, t1, t2 = P["rawb"][k], P["rt1"][k], P["rt2"][k]
        psw = PS[2 + k]
        S.op('act', lambda e: e.mul(out=raw[:], in_=ps[:, 0:TC], mul=scale),
             reads=[ps], writes=[raw])
        S.op('pe', lambda e: e.matmul(out=psw[:, 0:TC], lhsT=rmat[:], rhs=raw[:], start=True, stop=True),
             reads=[rmat, raw], writes=[psw])
        S.op('dve', lambda e: e.tensor_tensor(out=t1[:], in0=psw[:, 0:TC], in1=Sb[:], op=ALU.mult),
             reads=[psw, Sb], writes=[t1])
        S.op('dve', lambda e: e.scalar_tensor_tensor(out=t2[:], in0=ps[:, 0:TC], scalar=scale, in1=Cb[:],
                                                    op0=ALU.mult, op1=ALU.mult),
             reads=[ps, Cb], writes=[t2])
        for (p0, p1, dst_ap, dst_buf) in dsts:
            S.op('pool', lambda e: e.tensor_tensor(out=dst_ap, in0=t1[p0:p1, :], in1=t2[p0:p1, :], op=ALU.add),
                 reads=[t1, t2], writes=[dst_buf])

    def project_pass(rope):
        for tc in range(NTC):
            k = cnt["x"] % 2
            cnt["x"] += 1
            xs, xb, Cb, Sb = P["xst"][k], P["xcb"][k], P["ropC"][k], P["ropS"][k]
            S.dma('sp', lambda e: e.dma_start(
                out=xs[:], in_=xT_d[:, tc * TC:(tc + 1) * TC].rearrange("(c p) t -> p c t", p=128)),
                xs, writes=[xs])
            S.op('dve', lambda e: e.tensor_copy(out=xb[:], in_=xs[:]), reads=[xs], writes=[xb])
            if rope:
                S.dma('sp', lambda e: e.dma_start(out=Cb[:], in_=cd["ropeK_C"][:, tc * TC:(tc + 1) * TC]),
                      Cb, writes=[Cb])
                S.dma('sp', lambda e: e.dma_start(out=Sb[:], in_=cd["ropeK_S"][:, tc * TC:(tc + 1) * TC]),
                      Sb, writes=[Sb])
            for hp in range(4):
                ps = PS[cnt["pk"] % 2]
                cnt["pk"] += 1
                for dc in range(8):
                    S.op('pe', lambda e, dc=dc: e.matmul(
                        out=ps[:, 0:TC], lhsT=P["w_bf"][:, dc, 512 + hp * 128:512 + (hp + 1) * 128],
                        rhs=xb[:, dc, :], start=(dc == 0), stop=(dc == 7)),
                        reads=[P["w_bf"], xb], writes=[ps])
                evac("k", rope, ps, [(0, 128, KT[:, hp, tc * TC:(tc + 1) * TC], KTb[hp][tc])], Cb, Sb, 1.0)
            for blk in range(TC // 128):
                g = tc * (TC // 128) + blk
                ps = PS[cnt["pk"] % 2]
                cnt["pk"] += 1
                for dc in range(8):
                    S.op('pe', lambda e, dc=dc: e.matmul(
                        out=ps[:, :], lhsT=xb[:, dc, blk * 128:(blk + 1) * 128],
                        rhs=P["w_bf"][:, dc, 1024:1536], start=(dc == 0), stop=(dc == 7)),
                        reads=[P["w_bf"], xb], writes=[ps])
                S.op('act', lambda e: e.copy(
                    out=V[:, g, :, 0:64], in_=ps[:, :].rearrange("p (h d) -> p h d", d=64)),
                    reads=[ps], writes=[Vb[g]])
        for tc in range(NTCQ):
            k = cnt["x"] % 2
            cnt["x"] += 1
            xs, xb, Cb, Sb = P["xst"][k], P["xcb"][k], P["ropC"][k], P["ropS"][k]
            S.dma('sp', lambda e: e.dma_start(
                out=xs[:], in_=xTq_d[:, tc * TC:(tc + 1) * TC].rearrange("(c p) t -> p c t", p=128)),
                xs, writes=[xs])
            S.op('dve', lambda e: e.tensor_copy(out=xb[:], in_=xs[:]), reads=[xs], writes=[xb])
            if rope:
                S.dma('sp', lambda e: e.dma_start(out=Cb[:], in_=cd["ropeQ_C"][:, tc * TC:(tc + 1) * TC]),
                      Cb, writes=[Cb])
                S.dma('sp', lambda e: e.dma_start(out=Sb[:], in_=cd["ropeQ_S"][:, tc * TC:(tc + 1) * TC]),
                      Sb, writes=[Sb])
            for hp in range(4):
                ps = PS[cnt["pk"] % 2]
                cnt["pk"] += 1
                for dc in range(8):
                    S.op('pe', lambda e, dc=dc: e.matmul(
                        out=ps[:, 0:TC], lhsT=P["w_bf"][:, dc, hp * 128:(hp + 1) * 128],
                        rhs=xb[:, dc, :], start=(dc == 0), stop=(dc == 7)),
                        reads=[P["w_bf"], xb], writes=[ps])
                evac("q", rope, ps, [(0, 64, QT[0:64, 2 * hp, tc * TC:(tc + 1) * TC], QTb[2 * hp][tc]),
                                     (64, 128, QT[64:128, 2 * hp + 1, tc * TC:(tc + 1) * TC], QTb[2 * hp + 1][tc])],
                     Cb, Sb, 0.125)

    def zmatmuls(pz, hb, j, kb):
        for hh in range(4):
            h = hb * 4 + hh
            hp = h // 2
            S.op('pe', lambda e, hh=hh, hp=hp, h=h: e.matmul(
                out=pz[:, hh * 128:(hh + 1) * 128],
                lhsT=KT[:, hp, kb * 128:(kb + 1) * 128],
                rhs=QT[:, h, j * 128:(j + 1) * 128], start=True, stop=True),
                reads=[ktb(hp, kb), qtb(h, j)], writes=[pz])

    def run_pipeline(iters, skews, hook=None):
        N = len(iters)
        for n in range(N + max(skews)):
            if hook is not None:
                hook()
            for st, sk in enumerate(skews):
                i = n - sk
                if 0 <= i < N:
                    iters[i][st]()

    PZ = [PS[0], PS[2], PS[3]]
    PL = [PS[4], PS[5]]

    def attention_da(jlist):
        iters = []
        nb = 0
        for j in jlist:
            for hb in range(2):
                po = PS[6 + (nb % 2)]
                rec = P["rec_t"][nb % 2]
                nb += 1
                kbs = [2 * j + o for o in range(1, -17, -1) if 2 * j + o >= 0]
                for n, kb in enumerate(kbs):
                    it = len(iters)
                    pz, a, am = PZ[it % 3], P["a_t"][it % 5], P["am_t"][it % 5]
                    mi = kb - 2 * j + 16
                    first, last = (n == 0), (n == len(kbs) - 1)

                    def stA(pz=pz, a=a, am=am, hb=hb, j=j, kb=kb, mi=mi):
                        zmatmuls(pz, hb, j, kb)
                        S.op('act', lambda e: e.activation(out=a[:], in_=pz[:, :], func=ACT.Exp),
                             reads=[pz], writes=[a])
                        S.op('dve', lambda e: e.tensor_tensor(
                            out=am[:].rearrange("p (h q) -> p h q", h=4),
                            in0=a[:].rearrange("p (h q) -> p h q", h=4),
                            in1=maskDA[:, mi, :].unsqueeze(1).to_broadcast([128, 4, 128]), op=ALU.mult),
                            reads=[a, maskDA], writes=[am])

                    def stB(po=po, am=am, hb=hb, j=j, kb=kb, first=first, last=last, rec=rec):
                        if first:
                            S.op('pe', lambda e: e.matmul(out=po[:, 0:260], lhsT=zerob[:, 0:128], rhs=zerob[:, 0:260],
                                                          start=True, stop=False, skip_group_check=True),
                                 reads=[zerob], writes=[po])
                        for hh in range(4):
                            h = hb * 4 + hh
                            S.op('pe', lambda e, hh=hh, h=h: e.matmul(
                                out=po[:, hh * 65:(hh + 1) * 65], lhsT=am[:, hh * 128:(hh + 1) * 128],
                                rhs=V[:, kb, h, 0:65], start=False, stop=False, skip_group_check=True),
                                reads=[am, Vb[kb], Vones], writes=[po])
                        if last:
                            S.op('pe', lambda e: e.matmul(out=po[:, 0:260], lhsT=zerob[:, 0:128], rhs=zerob[:, 0:260],
                                                          start=False, stop=True, skip_group_check=True),
                                 reads=[zerob], writes=[po])
                            pov = po[:, 0:260].rearrange("p (h d) -> p h d", d=65)
                            S.op('dve', lambda e: e.reciprocal(out=rec[:], in_=pov[:, :, 64]), reads=[po], writes=[rec])
                            S.op('dve', lambda e: e.tensor_tensor(
                                out=o_all[:, j, hb * 256:(hb + 1) * 256].rearrange("p (h d) -> p h d", d=64),
                                in0=pov[:, :, 0:64], in1=rec[:].unsqueeze(2).to_broadcast([128, 4, 64]), op=ALU.mult),
                                reads=[po, rec], writes=[o_all])

                    iters.append([stA, stB])
        run_pipeline(iters, [0, 3], hook=c0.tick)
        c0.flush()

    def attention_sb(jlist):
        iters = []
        nb = 0
        for j in jlist:
            for hb in range(2):
                po = PS[6 + (nb % 2)]
                LS = P["lsum"][(nb % 2) * 3:(nb % 2) * 3 + 3]
                nb += 1
                kbs = list(range(2 * j + 1, -1, -1))
                for n, kb in enumerate(kbs):
                    it = len(iters)
                    pz, pl = PZ[it % 3], PL[it % 2]
                    e_, sp, l1m, tt, a, am = (P["e_t"][it % 3], P["sp_t"][it % 5], P["l1m_t"][it % 4], P["tt_t"][it % 4],
                                              P["a_t"][it % 5], P["am_t"][it % 5])
                    o = kb - 2 * j
                    masked = o >= 0
                    first, last = (n == 0), (n == len(kbs) - 1)
                    lcur, lnew = LS[n % 3], LS[(n + 1) % 3]
                    mk = maskSB[:, max(o, 0), :].unsqueeze(1).to_broadcast([128, 4, 128])

                    def stA1(pz=pz, e_=e_, sp=sp, hb=hb, j=j, kb=kb):
                        zmatmuls(pz, hb, j, kb)
                        S.op('act', lambda e: e.activation(out=e_[:], in_=pz[:, :], func=ACT.Exp, scale=-1.0),
                             reads=[pz], writes=[e_])
                        S.op('act', lambda e: e.activation(out=sp[:], in_=e_[:], func=ACT.Ln, bias=1.0),
                             reads=[e_], writes=[sp])

                    def stA2(pz=pz, sp=sp, l1m=l1m, masked=masked, mk=mk, first=first, last=last, lcur=lcur, lnew=lnew):
                        S.op('dve', lambda e: e.scalar_tensor_tensor(
                            out=l1m[:], in0=pz[:, :], scalar=-1.0, in1=sp[:], op0=ALU.mult, op1=ALU.subtract),
                            reads=[pz, sp], writes=[l1m])
                        if masked:
                            S.op('pool', lambda e: e.tensor_tensor(
                                out=l1m[:].rearrange("p (h q) -> p h q", h=4),
                                in0=l1m[:].rearrange("p (h q) -> p h q", h=4), in1=mk, op=ALU.mult),
                                reads=[l1m, maskSB], writes=[l1m])
                        if not last:
                            if first:
                                S.op('dve', lambda e: e.tensor_copy(out=lnew[:], in_=l1m[:]),
                                     reads=[l1m], writes=[lnew])
                            else:
                                S.op('dve', lambda e: e.tensor_tensor(out=lnew[:], in0=lcur[:], in1=l1m[:], op=ALU.add),
                                     reads=[l1m, lcur], writes=[lnew])

                    def stB1(pl=pl, sp=sp, l1m=l1m, tt=tt, first=first, lcur=lcur):
                        S.op('pe', lambda e: e.matmul(out=pl[:, :], lhsT=mtri[:], rhs=l1m[:], start=True, stop=first),
                             reads=[mtri, l1m], writes=[pl])
                        if not first:
                            S.op('pe', lambda e: e.matmul(out=pl[:, :], lhsT=onesb[:], rhs=lcur[:], start=False, stop=True),
                                 reads=[onesb, lcur], writes=[pl])
                        S.op('dve', lambda e: e.tensor_tensor(out=tt[:], in0=pl[:, :], in1=sp[:], op=ALU.subtract),
                             reads=[pl, sp], writes=[tt])

                    def stB2(tt=tt, a=a, am=am, masked=masked, mk=mk):
                        S.op('act', lambda e: e.activation(out=a[:], in_=tt[:], func=ACT.Exp), reads=[tt], writes=[a])
                        if masked:
                            S.op('pool', lambda e: e.tensor_tensor(
                                out=am[:].rearrange("p (h q) -> p h q", h=4),
                                in0=a[:].rearrange("p (h q) -> p h q", h=4), in1=mk, op=ALU.mult),
                                reads=[a, maskSB], writes=[am])

                    def stC(po=po, a=a, am=am, masked=masked, hb=hb, j=j, kb=kb, first=first, last=last):
                        src = am if masked else a
                        if first:
                            S.op('pe', lambda e: e.matmul(out=po[:, 0:256], lhsT=zerob[:, 0:128], rhs=zerob[:, 0:256],
                                                          start=True, stop=False, skip_group_check=True),
                                 reads=[zerob], writes=[po])
                        for hh in range(4):
                            h = hb * 4 + hh
                            S.op('pe', lambda e, hh=hh, h=h: e.matmul(
                                out=po[:, hh * 64:(hh + 1) * 64], lhsT=src[:, hh * 128:(hh + 1) * 128],
                                rhs=V[:, kb, h, 0:64], start=False, stop=False, skip_group_check=True),
                                reads=[src, Vb[kb]], writes=[po])
                        if last:
                            S.op('pe', lambda e: e.matmul(out=po[:, 0:256], lhsT=zerob[:, 0:128], rhs=zerob[:, 0:256],
                                                          start=False, stop=True, skip_group_check=True),
                                 reads=[zerob], writes=[po])
                            S.op('act', lambda e: e.copy(out=o_all[:, j, 512 + hb * 256:512 + (hb + 1) * 256],
                                                         in_=po[:, 0:256]), reads=[po], writes=[o_all])

                    iters.append([stA1, stB1, stA2, stB2, stC])
        run_pipeline(iters, [0, 2, 1, 3, 4], hook=c0.tick)
        c0.flush()

    jl = list(range(NQ))
    if stage == "projda":
        jl = []
    if stage in ("ln1", "topk", "peer2") or stage.startswith("topk_"):
        jl = [0, 1]
    if stage == "peer4":
        jl = [0, 1, 2, 3]
    es = ExitStack()
    alloc_proj(es)
    load_pass_weights(0, 512, 1024)
    project_pass(rope=True)
    if dbg and stage in ("projda",):
        dbg_dump("KT", KTb[0][0], KT[:, :, :], [128, 4, SEQ], BF16, [b for r in KTb for b in r])
        dbg_dump("QT", QTb[0][0], QT[:, :, :], [128, 8, 2048], BF16, [b for r in QTb for b in r])
        dbg_dump("V", Vb[0], V[:, :, :, :], [128, NB, 8, 65], BF16, Vb)
        S.finish('sp', list(dbg_outs.values()) + [b for r in KTb for b in r] + [b for r in QTb for b in r] + Vb)
        return nc
    S.barrier()
    es.close()
    es = ExitStack()
    alloc_attn(es)
    attention_da({"da1": [0, 9], "sb1": []}.get(stage, jl))
    if stage in ("da", "da1"):
        dbg_dump("o_all", o_all, o_all[:, :, :], [128, NQ, D], BF16)
        S.finish('sp', [o_all])
        return nc
    S.barrier()
    es.close()
    es = ExitStack()
    alloc_proj(es)
    load_pass_weights(1536, 2048, 2560)
    project_pass(rope=False)
    S.barrier()
    es.close()
    es = ExitStack()
    alloc_attn(es)
    attention_sb({"sb1": [0, 9]}.get(stage, jl))
    if stage in ("sb", "sb1", "attn"):
        dbg_dump("o_all", o_all, o_all[:, :, :], [128, NQ, D], BF16)
        S.finish('sp', [o_all])
        return nc


    S.barrier()
    es.close()
    es_kv.close()
    PSb = [Buf(PS[i].t, f"psc{i}") for i in range(8)]
    if c0.c < 128:
        es_c0 = ExitStack()
        c0.bind(es_c0)
        c0.finish_all()
        S.barrier()
        es_c0.close()
    x1_s = nc.dram_tensor("x1_s", [2048, D], F32).ap()
    esP = ExitStack()
    lnp = S.sb("lnp", [128, 2, D], F32, esP)
    i1T_all = S.sb("i1T_all", [128, 2048], BF16, esP)
    i2T_all = S.sb("i2T_all", [128, 2048], BF16, esP)
    gT_all = S.sb("gT_all", [128, 2048], BF16, esP)
    stats = S.sb("stats", [128, 2, 6], F32, esP)
    mv = S.sb("mv", [128, 2], F32, esP)
    rstd = S.sb("rstd", [128, 1], F32, esP)
    for i, dsrc in enumerate((ln1g_d, ln1b_d)):
        S.dma('sp', lambda e: e.dma_start(out=lnp[:, i, :], in_=dsrc.to_broadcast([128, D])), lnp, writes=[lnp])
    LNS = {"lnp": lnp, "stats": stats, "mv": mv, "rstd": rstd}
    if stage != "full":
        for tb_ in (i1T_all, i2T_all, gT_all):
            S.op('pool', lambda e: e.memset(tb_[:], 0.0), writes=[tb_])
    x1T_s = nc.dram_tensor("x1T_s", [128, 8 * 2048], BF16).ap()
    idx_s = nc.dram_tensor("idx_s", [3, 128, 2048], BF16).ap()
    x1T_sv = x1T_s.rearrange("p (a b) -> p a b", a=8)

    nchunks = 128
    es = ExitStack()
    w_out_bf = S.sb("w_out_bf", [128, 8, D], BF16, es)
    wq_bf = S.sb("wq_bf", [128, 8, 2048], BF16, es)
    keysT = S.sb("keysT", [128, 16, 128], BF16, es)
    def load_c1_weights(stg):
        for dc in range(8):
            S.dma('sp', lambda e: e.dma_start(out=stg[:, 0:D], in_=w_out_d[dc * 128:(dc + 1) * 128, :]), stg, writes=[stg])
            S.op('pool', lambda e: e.tensor_copy(out=w_out_bf[:, dc, :], in_=stg[:, 0:D]), reads=[stg], writes=[w_out_bf])
        for dc in range(8):
            for hh_ in range(2):
                S.dma('sp', lambda e: e.dma_start(out=stg[:, :], in_=wq_d[dc * 128:(dc + 1) * 128, hh_ * 1024:(hh_ + 1) * 1024]),
                      stg, writes=[stg])
                S.op('dve', lambda e: e.tensor_copy(out=wq_bf[:, dc, hh_ * 1024:(hh_ + 1) * 1024], in_=stg[:, :]),
                     reads=[stg], writes=[wq_bf])
        for hpc in range(16):
            S.dma('sp', lambda e: e.dma_start(out=stg[:, 0:128], in_=keys_d[hpc * 128:(hpc + 1) * 128, :]), stg, writes=[stg])
            S.op('pe', lambda e: e.transpose(out=PSb[4][:, 0:128], in_=stg[:, 0:128], identity=ident[:]),
                 reads=[stg, ident], writes=[PSb[4]])
            S.op('act', lambda e: e.copy(out=keysT[:, hpc, :], in_=PSb[4][:, 0:128]), reads=[PSb[4]], writes=[keysT])

    def mkset(tag, pmix, pgen):
        T = {}
        T["xo"] = S.sb(f"xo{tag}", [128, D], F32, es)
        T["oT"] = S.sb(f"oT{tag}", [128, 8, 128], BF16, es)
        T["r1"] = S.sb(f"r1{tag}", [128, D], F32, es)
        T["x1"] = S.sb(f"x1{tag}", [128, D], F32, es)
        T["x1b"] = S.sb(f"x1b{tag}", [128, D], BF16, es)
        T["x1T"] = S.sb(f"x1T{tag}", [128, 8, 128], BF16, es)
        T["qT"] = S.sb(f"qT{tag}", [128, 16, 128], BF16, es)
        T["big"] = S.sb(f"big{tag}", [128, 2048], F32, es)
        T["wk"] = [S.sb(f"wk{tag}{i}", [128, 128], F32, es) for i in range(2)]
        T["cw"] = [S.sb(f"cw{tag}{i}", [128, 256], F32, es) for i in range(2)]
        T["tv"] = S.sb(f"tv{tag}", [128, 16, 16], F32, es)
        T["ti"] = S.sb(f"ti{tag}", [128, 16, 16], U32, es)
        T["tif"] = S.sb(f"tif{tag}", [128, 16, 16], F32, es)
        for nm in ("bs", "paf", "pbf", "dd", "ee", "gates", "i1", "i2"):
            T[nm] = S.sb(f"{nm}{tag}", [128, 8, 16], F32, es)
        for nm in ("bp", "pa", "pb"):
            T[nm] = S.sb(f"{nm}{tag}", [128, 8, 16], U32, es)
        T["ssum"] = S.sb(f"ssum{tag}", [128, 8], F32, es)
        T["rsum"] = S.sb(f"rsum{tag}", [128, 8], F32, es)
        T["stats"] = S.sb(f"stats{tag}", [128, 2, 6], F32, es)
        T["mv"] = S.sb(f"mv{tag}", [128, 2], F32, es)
        T["rstd"] = S.sb(f"rstd{tag}", [128, 1], F32, es)
        T["pmix"] = pmix
        T["pgen"] = pgen
        return T

    TA = mkset("A", [PSb[0], PSb[1]], [PSb[4], PSb[5]])
    TB = mkset("B", [PSb[2], PSb[3]], [PSb[6], PSb[7]])

    def layer_norm(src, dst, gi, bi, T=None):
        lnp = LNS["lnp"]
        TT = T if T is not None else LNS
        stats, mv, rstd = TT["stats"], TT["mv"], TT["rstd"]
        for c in range(2):
            yield S.op('dve', lambda e: e.bn_stats(out=stats[:, c, :], in_=src[:, c * 512:(c + 1) * 512]),
                 reads=[src], writes=[stats])
        yield S.op('dve', lambda e: e.bn_aggr(out=mv[:], in_=stats[:].rearrange("p a b -> p (a b)")),
             reads=[stats], writes=[mv])
        yield S.op('act', lambda e: e.activation(out=rstd[:], in_=mv[:, 1:2], func=ACT.Sqrt, bias=LN_EPS),
             reads=[mv], writes=[rstd])
        yield S.op('dve', lambda e: e.reciprocal(out=rstd[:], in_=rstd[:]), reads=[rstd], writes=[rstd])
        yield S.op('dve', lambda e: e.tensor_scalar(out=dst[:], in0=src[:], scalar1=mv[:, 0:1], scalar2=rstd[:, 0:1],
                                              op0=ALU.subtract, op1=ALU.mult), reads=[src, mv, rstd], writes=[dst])
        yield S.op('pool', lambda e: e.tensor_tensor(out=dst[:], in0=dst[:], in1=lnp[:, gi, :], op=ALU.mult),
             reads=[dst, lnp], writes=[dst])
        yield S.op('pool', lambda e: e.tensor_tensor(out=dst[:], in0=dst[:], in1=lnp[:, bi, :], op=ALU.add),
             reads=[dst, lnp], writes=[dst])

    def ln1_block(j, T):
        X1, X1b, xo, oT, r1 = T["x1"], T["x1b"], T["xo"], T["oT"], T["r1"]
        PM = T["pmix"]
        pT = T["pgen"][0]
        pTb = pT[:, :].bitcast(BF16)
        for fc in range(8):
            yield S.op('pe', lambda e: e.transpose(out=pTb[:, fc * 128:(fc + 1) * 128],
                                            in_=o_all[:, j, fc * 128:(fc + 1) * 128], identity=identb[:]),
                 reads=[o_all, identb], writes=[pT])
        yield S.op('act', lambda e: e.copy(out=oT[:].rearrange("p a b -> p (a b)"), in_=pTb[:, 0:1024]),
             reads=[pT], writes=[oT])
        for half in range(2):
            for fc in range(8):
                yield S.op('pe', lambda e: e.matmul(out=PM[half][:, :], lhsT=oT[:, fc, :],
                                              rhs=w_out_bf[:, fc, half * 512:(half + 1) * 512],
                                              start=(fc == 0), stop=(fc == 7)),
                     reads=[oT, w_out_bf], writes=[PM[half]])
        yield S.dma('sp', lambda e: e.dma_start(out=xo[:], in_=xown_d[j * 128:(j + 1) * 128, :]), xo, writes=[xo])
        for half in range(2):
            yield S.op('dve', lambda e: e.scalar_tensor_tensor(
                out=r1[:, half * 512:(half + 1) * 512], in0=xo[:, half * 512:(half + 1) * 512], scalar=ALPHA,
                in1=PM[half][:, :], op0=ALU.mult, op1=ALU.add), reads=[xo, PM[half]], writes=[r1])
        yield from layer_norm(r1, X1, 0, 1, T)
        yield S.op('act', lambda e: e.copy(out=X1b[:], in_=X1[:]), reads=[X1], writes=[X1b])
        yield S.dma('sp', lambda e: e.dma_start(out=x1_s[j * 128:(j + 1) * 128, :], in_=X1[:]), X1, reads=[X1])

    def topk_block(j, T):
        X1b, x1T, qT, big, wk, cw, tv, ti, tif = (T[k] for k in ("x1b", "x1T", "qT", "big", "wk", "cw", "tv", "ti", "tif"))
        bs, bp, pa, pb, paf, pbf, dd, ee, ssum, rsum, gates, i1, i2 = (T[k] for k in ("bs", "bp", "pa", "pb", "paf", "pbf", "dd", "ee", "ssum", "rsum", "gates", "i1", "i2"))
        PG = T["pgen"]
        pT = PG[0]
        pTb = pT[:, :].bitcast(BF16)
        for dc in range(8):
            yield S.op('pe', lambda e: e.transpose(out=pTb[:, dc * 128:(dc + 1) * 128],
                                            in_=X1b[:, dc * 128:(dc + 1) * 128], identity=identb[:]),
                 reads=[X1b, identb], writes=[pT])
        yield S.op('act', lambda e: e.copy(out=x1T[:].rearrange("p a b -> p (a b)"), in_=pTb[:, 0:1024]),
             reads=[pT], writes=[x1T])
        yield S.dma('sp', lambda e: e.dma_start(out=x1T_sv[:, :, j * 128:(j + 1) * 128], in_=x1T[:]), x1T, reads=[x1T])
        for g4 in range(4):
            bank = PG[g4 % 2]
            for i in range(4):
                hpc = g4 * 4 + i
                for dc in range(8):
                    yield S.op('pe', lambda e: e.matmul(out=bank[:, i * 128:(i + 1) * 128],
                                                  lhsT=wq_bf[:, dc, hpc * 128:(hpc + 1) * 128],
                                                  rhs=x1T[:, dc, :],
                                                  start=(dc == 0), stop=(dc == 7)),
                         reads=[wq_bf, x1T], writes=[bank])
            yield S.op('act', lambda e: e.copy(out=qT[:, g4 * 4:(g4 + 1) * 4, :].rearrange("p a b -> p (a b)"),
                                         in_=bank[:, :]), reads=[bank], writes=[qT])
        sc = big
        for g4 in range(4):
            bank = PG[g4 % 2]
            for i in range(4):
                hpc = g4 * 4 + i
                yield S.op('pe', lambda e: e.matmul(out=bank[:, i * 128:(i + 1) * 128], lhsT=qT[:, hpc, :],
                                              rhs=keysT[:, hpc, :], start=True, stop=True),
                     reads=[qT, keysT], writes=[bank])
            yield S.op('act', lambda e: e.copy(out=sc[:, g4 * 512:(g4 + 1) * 512], in_=bank[:, :]),
                 reads=[bank], writes=[sc])
        for hpc in range(16):
            src = sc[:, hpc * 128:(hpc + 1) * 128]
            w = wk[hpc % 2]
            yield S.op('dve', lambda e: e.max(out=tv[:, hpc, 0:8], in_=src), reads=[sc], writes=[tv])
            yield S.op('dve', lambda e: e.max_index(out=ti[:, hpc, 0:8], in_max=tv[:, hpc, 0:8], in_values=src),
                 reads=[sc, tv], writes=[ti])
            yield S.op('dve', lambda e: e.match_replace(out=w[:], in_to_replace=tv[:, hpc, 0:8], in_values=src,
                                                 imm_value=NEG), reads=[sc, tv], writes=[w])
            yield S.op('dve', lambda e: e.max(out=tv[:, hpc, 8:16], in_=w[:]), reads=[w], writes=[tv])
            yield S.op('dve', lambda e: e.max_index(out=ti[:, hpc, 8:16], in_max=tv[:, hpc, 8:16], in_values=w[:]),
                 reads=[w, tv], writes=[ti])
        yield S.op('dve', lambda e: e.tensor_copy(out=tif[:], in_=ti[:]), reads=[ti], writes=[tif])
        cs = big
        cs4 = cs[:].rearrange("p (h a b) -> p h a b", h=8, a=16)
        yield S.op('dve', lambda e: e.tensor_tensor(
            out=cs4, in0=tv[:, 0::2, :].unsqueeze(3).to_broadcast([128, 8, 16, 16]),
            in1=tv[:, 1::2, :].unsqueeze(2).to_broadcast([128, 8, 16, 16]), op=ALU.add),
            reads=[tv], writes=[cs])
        for h in range(8):
            src = cs[:, h * 256:(h + 1) * 256]
            w = cw[h % 2]
            yield S.op('dve', lambda e: e.max(out=bs[:, h, 0:8], in_=src), reads=[cs], writes=[bs])
            yield S.op('dve', lambda e: e.max_index(out=bp[:, h, 0:8], in_max=bs[:, h, 0:8], in_values=src),
                 reads=[cs, bs], writes=[bp])
            yield S.op('dve', lambda e: e.match_replace(out=w[:], in_to_replace=bs[:, h, 0:8], in_values=src,
                                                 imm_value=NEG), reads=[cs, bs], writes=[w])
            yield S.op('dve', lambda e: e.max(out=bs[:, h, 8:16], in_=w[:]), reads=[w], writes=[bs])
            yield S.op('dve', lambda e: e.max_index(out=bp[:, h, 8:16], in_max=bs[:, h, 8:16], in_values=w[:]),
                 reads=[w, bs], writes=[bp])
        yield S.op('dve', lambda e: e.tensor_tensor(out=dd[:], in0=bs[:], in1=bs[:, :, 0:1].to_broadcast([128, 8, 16]),
                                              op=ALU.subtract), reads=[bs], writes=[dd])
        yield S.op('act', lambda e: e.activation(out=ee[:], in_=dd[:], func=ACT.Exp), reads=[dd], writes=[ee])
        yield S.op('dve', lambda e: e.tensor_reduce(out=ssum[:], in_=ee[:], axis=AX.X, op=ALU.add), reads=[ee], writes=[ssum])
        yield S.op('dve', lambda e: e.reciprocal(out=rsum[:], in_=ssum[:]), reads=[ssum], writes=[rsum])
        yield S.op('dve', lambda e: e.tensor_tensor(out=gates[:], in0=ee[:], in1=rsum[:].unsqueeze(2).to_broadcast([128, 8, 16]),
                                              op=ALU.mult), reads=[ee, rsum], writes=[gates])
        yield S.op('dve', lambda e: e.tensor_scalar(out=pa[:], in0=bp[:], scalar1=4, scalar2=None,
                                              op0=ALU.logical_shift_right), reads=[bp], writes=[pa])
        yield S.op('dve', lambda e: e.tensor_scalar(out=pb[:], in0=bp[:], scalar1=15, scalar2=None,
                                              op0=ALU.bitwise_and), reads=[bp], writes=[pb])
        yield S.op('dve', lambda e: e.tensor_copy(out=paf[:], in_=pa[:]), reads=[pa], writes=[paf])
        yield S.op('dve', lambda e: e.tensor_copy(out=pbf[:], in_=pb[:]), reads=[pb], writes=[pbf])
        eq = big
        eq4 = eq[:].rearrange("p (h k a) -> p h k a", h=8, k=16)
        io4 = iota16[:].unsqueeze(1).unsqueeze(1).to_broadcast([128, 8, 16, 16])
        for (pf, par, dst) in ((paf, 0, i1), (pbf, 1, i2)):
            yield S.op('dve', lambda e: e.tensor_tensor(out=eq4, in0=pf[:].unsqueeze(3).to_broadcast([128, 8, 16, 16]),
                                                  in1=io4, op=ALU.is_equal), reads=[pf, iota16], writes=[eq])
            yield S.op('dve', lambda e: e.tensor_tensor(out=eq4, in0=eq4,
                                                  in1=tif[:, par::2, :].unsqueeze(2).to_broadcast([128, 8, 16, 16]),
                                                  op=ALU.mult), reads=[eq, tif], writes=[eq])
            yield S.op('dve', lambda e: e.tensor_reduce(out=dst[:], in_=eq4, axis=AX.X, op=ALU.add), reads=[eq], writes=[dst])
        pt2 = PG[1]
        for n3, (srcb, dstb) in enumerate(((i1, i1T_all), (i2, i2T_all), (gates, gT_all))):
            yield S.op('pe', lambda e: e.transpose(out=pt2[:, n3 * 128:(n3 + 1) * 128],
                                            in_=srcb[:].rearrange("p a b -> p (a b)"), identity=ident[:]),
                 reads=[srcb, ident], writes=[pt2])
        for n3, (srcb, dstb) in enumerate(((i1, i1T_all), (i2, i2T_all), (gates, gT_all))):
            yield S.op('act', lambda e: e.copy(out=dstb[:, j * 128:(j + 1) * 128], in_=pt2[:, n3 * 128:(n3 + 1) * 128]),
                 reads=[pt2], writes=[dstb])

    es_stg = ExitStack()
    stg_ = S.sb("stg", [128, 1024], F32, es_stg)
    load_c1_weights(stg_)
    S.barrier()
    es_stg.close()
    nblk = {"peer2": 2, "peer4": 4}.get(stage, NQ)
    def blk(j, T):
        yield from ln1_block(j, T)
        yield from topk_block(j, T)

    for jp in range(0, nblk, 2):
        gens = [blk(jp, TA), blk(jp + 1, TB)]
        while gens:
            for gq in list(gens):
                try:
                    next(gq)
                except StopIteration:
                    gens.remove(gq)
    for n3, tb_ in enumerate((i1T_all, i2T_all, gT_all)):
        S.dma('sp', lambda e: e.dma_start(out=idx_s[n3], in_=tb_[:]), tb_, reads=[tb_])
    S.barrier()
    es.close()
    esP.close()
    es_o.close()

    es = ExitStack()
    GT = 256
    SBT = 16
    lnp = S.sb("lnp2", [128, 2, D], F32, es)
    stats = S.sb("stats2", [128, 2, 6], F32, es)
    mv = S.sb("mv2", [128, 2], F32, es)
    rstd = S.sb("rstd2", [128, 1], F32, es)
    LNS.update({"lnp": lnp, "stats": stats, "mv": mv, "rstd": rstd})
    for i, dsrc in enumerate((ln2g_d, ln2b_d)):
        S.dma('sp', lambda e: e.dma_start(out=lnp[:, i, :], in_=dsrc.to_broadcast([128, D])), lnp, writes=[lnp])
    x1Tg = [S.sb(f"x1Tg{i}", [128, 8, GT], BF16, es) for i in range(2)]
    idxg = [[S.sb(f"idxg{i}_{n3}", [128, GT], BF16, es) for n3 in range(3)] for i in range(2)]
    Wg2 = [S.sb(f"Wg{i}", [128, GT, 128], BF16, es) for i in range(2)]
    UTr = [S.sb(f"UTr{i}", [128, D], BF16, es) for i in range(4)]
    Vr = [S.sb(f"Vr{i}", [128, D], BF16, es) for i in range(6)]
    OJ = [S.sb(f"OJ{i}", [128, SBT, 128], BF16, es) for i in range(2)]
    PI = [S.sb(f"PI{i}", [128, SBT, 128], BF16, es) for i in range(2)]
    gelb = [S.sb(f"gelb{i}", [128, GT], BF16, es) for i in range(2)]
    actb = [S.sb(f"actb{i}", [128, GT], BF16, es) for i in range(2)]
    x1r = S.sb("x1r", [128, D], F32, es)
    ysb = S.sb("ysb", [128, 4, 512], F32, es)
    r2 = x1r
    yout = x1r
    YB = PSb[0:4]
    HP = PSb[4:6]
    WP = PSb[6:8]
    io3 = iota128[:].unsqueeze(1).to_broadcast([128, SBT, 128])
    cc = {"w": 0}
    ngroups = nblk // 2
    def group_load(g):
        xt = x1Tg[g % 2]
        S.dma('sp', lambda e: e.dma_start(out=xt[:], in_=x1T_sv[:, :, g * GT:(g + 1) * GT]), xt, writes=[xt])
        for n3 in range(3):
            tb_ = idxg[g % 2][n3]
            S.dma('sp', lambda e: e.dma_start(out=tb_[:], in_=idx_s[n3][:, g * GT:(g + 1) * GT]), tb_, writes=[tb_])

    def w_build_stages(g, sbt):
        Wg = Wg2[g % 2]
        i1g, i2g, gg = idxg[g % 2]
        t0 = sbt * SBT
        oj, pi = OJ[sbt % 2], PI[sbt % 2]

        def s_oj():
            S.op('dve', lambda e: e.tensor_tensor(
                out=oj[:], in0=io3, in1=i2g[:, t0:t0 + SBT].unsqueeze(2).to_broadcast([128, SBT, 128]),
                op=ALU.is_equal), reads=[iota128, i2g], writes=[oj])

        def s_pi():
            S.op('dve', lambda e: e.tensor_tensor(
                out=pi[:], in0=io3, in1=i1g[:, t0:t0 + SBT].unsqueeze(2).to_broadcast([128, SBT, 128]),
                op=ALU.is_equal), reads=[iota128, i1g], writes=[pi])

        def s_gate():
            S.op('pool', lambda e: e.tensor_tensor(
                out=pi[:], in0=pi[:], in1=gg[:, t0:t0 + SBT].unsqueeze(2).to_broadcast([128, SBT, 128]),
                op=ALU.mult), reads=[pi, gg], writes=[pi])

        def mk_mm(q):
            def s_mm():
                wp = WP[(sbt * (SBT // 4) + q) % 2]
                for u in range(4):
                    tl = q * 4 + u
                    S.op('pe', lambda e: e.matmul(out=wp[:, u * 128:(u + 1) * 128], lhsT=oj[:, tl, :], rhs=pi[:, tl, :],
                                                  start=True, stop=True), reads=[oj, pi], writes=[wp])
            return s_mm

        def mk_ev(q):
            def s_ev():
                wp = WP[(sbt * (SBT // 4) + q) % 2]
                tg = t0 + q * 4
                dst = Wg[:, tg:tg + 4, :].rearrange("p t i -> p (t i)")
                S.op('act', lambda e: e.copy(out=dst, in_=wp[:, :]), reads=[wp], writes=[Wg])
            return s_ev

        st = [(0, s_oj), (1, s_pi), (4, s_gate)]
        for q in range(SBT // 4):
            st.append((7 + q, mk_mm(q)))
            st.append((8 + q, mk_ev(q)))
        return st

    def w_build_sub(g, sbt):
        for _, fn in w_build_stages(g, sbt):
            fn()

    def chunk_stages(g, c, n):
        k2 = n % 2
        ut, vv, hp, gl, ab = UTr[n % 4], Vr[n % 6], HP[k2], gelb[k2], actb[k2]
        Wg, xt = Wg2[g % 2], x1Tg[g % 2]

        def sD():
            S.dma('sp', lambda e: e.dma_start(out=ut[:], in_=ut_s[c]), ut, writes=[ut])
            S.dma('sp', lambda e: e.dma_start(out=vv[:], in_=v_s[c * 128:(c + 1) * 128, :]), vv, writes=[vv])

        def sH():
            for dc in range(8):
                S.op('pe', lambda e: e.matmul(out=hp[:, 0:GT], lhsT=ut[:, dc * 128:(dc + 1) * 128],
                                              rhs=xt[:, dc, :],
                                              start=(dc == 0), stop=(dc == 7)), reads=[ut, xt], writes=[hp])
            S.op('act', lambda e: e.activation(out=gl[:], in_=hp[:, 0:GT], func=ACT.Gelu), reads=[hp], writes=[gl])

        def sM():
            meng = 'pool' if (c % 8) in (1, 4, 6) else 'dve'
            S.op(meng, lambda e: e.tensor_tensor(out=ab[:], in0=gl[:], in1=Wg[:, :, c], op=ALU.mult),
                 reads=[gl, Wg], writes=[ab])

        def sY():
            for tb in range(2):
                for half in range(2):
                    S.op('pe', lambda e: e.matmul(out=YB[tb * 2 + half][:, :], lhsT=ab[:, tb * 128:(tb + 1) * 128],
                                                  rhs=vv[:, half * 512:(half + 1) * 512],
                                                  start=(c == 0), stop=(c == nchunks - 1)),
                         reads=[ab, vv], writes=[YB[tb * 2 + half]])
            if c == nchunks - 1:
                final_group(g)

        return [sD, sH, sM, sY]

    def final_group(g):
        for k4 in range(4):
            S.op('act', lambda e: e.copy(out=ysb[:, k4, :], in_=YB[k4][:, :]), reads=[YB[k4]], writes=[ysb])
        for tb in range(2):
            j = g * 2 + tb
            S.dma('sp', lambda e: e.dma_start(out=x1r[:], in_=x1_s[j * 128:(j + 1) * 128, :]), x1r, writes=[x1r])
            S.op('dve', lambda e: e.scalar_tensor_tensor(
                out=r2[:], in0=x1r[:], scalar=ALPHA,
                in1=ysb[:, tb * 2:tb * 2 + 2, :].rearrange("p a b -> p (a b)"), op0=ALU.mult, op1=ALU.add),
                reads=[x1r, ysb], writes=[r2])
            for _ in layer_norm(r2, yout, 0, 1):
                pass
            S.dma('sp', lambda e: e.dma_start(out=out_d[j * 128:(j + 1) * 128, :], in_=yout[:]), yout, reads=[yout])

    group_load(0)
    for sbt in range(GT // SBT):
        w_build_sub(0, sbt)
    iters = []
    for g in range(ngroups):
        for c in range(nchunks):
            iters.append(chunk_stages(g, c, len(iters)))
    step = {"n": 0}
    NSB = GT // SBT

    sched = {}
    for g in range(ngroups - 1):
        for k in range(NSB):
            base = g * nchunks + 8 + 6 * k
            for rel, fn in w_build_stages(g + 1, k):
                sched.setdefault(base + rel, []).append(fn)

    def c2_hook():
        n = step["n"]
        step["n"] += 1
        g, r = n // nchunks, n % nchunks
        if g + 1 < ngroups and r == 4:
            group_load(g + 1)
        for fn in sched.get(n, []):
            fn()

    run_pipeline(iters, [0, 2, 3, 4], hook=c2_hook)
    S.finish('sp', [yout])
    return nc


def make_in_maps(inputs):
    x = np.ascontiguousarray(inputs["x"], dtype=np.float32)
    w_in = np.ascontiguousarray(inputs["w_in"][0], dtype=np.float32)
    w_out = np.ascontiguousarray(inputs["w_out"][0], dtype=np.float32)
    wq = np.ascontiguousarray(inputs["peer_wq"][0], dtype=np.float32)
    keys = np.ascontiguousarray(inputs["peer_sub_keys"][0], dtype=np.float32).reshape(16 * 128, 128)
    pu = np.ascontiguousarray(inputs["peer_u"][0], dtype=np.float32)
    pv = np.ascontiguousarray(inputs["peer_v"][0], dtype=np.float32)
    maps = []
    for c in range(8):
        b, hf = c // 2, c % 2
        own = np.concatenate([(2 * j + hf) * 128 + np.arange(128) for j in range(NQ)])
        xb = x[b]
        m = {
            "xT": np.ascontiguousarray(xb.T),
            "xTq": np.ascontiguousarray(xb[own].T),
            "xown": np.ascontiguousarray(xb[own]),
            "w_in": w_in, "w_out": w_out, "wq": wq, "keys": keys,
            "peer_u": pu, "peer_v": pv,
            "ln1_g": np.ascontiguousarray(inputs["ln1_g"], dtype=np.float32).reshape(1, D),
            "ln1_b": np.ascontiguousarray(inputs["ln1_b"], dtype=np.float32).reshape(1, D),
            "ln2_g": np.ascontiguousarray(inputs["ln2_g"], dtype=np.float32).reshape(1, D),
            "ln2_b": np.ascontiguousarray(inputs["ln2_b"], dtype=np.float32).reshape(1, D),
        }
        for k, v in host_consts(hf).items():
            m["c_" + k] = np.ascontiguousarray(v, dtype=np.float32)
        maps.append(m)
    return maps


def kernel(**inputs):
    nc = build("full")
    maps = make_in_maps(inputs)
    res = run_bass_kernel_spmd(nc, maps, core_ids=list(range(8)))
    out = np.zeros((4, SEQ, D), np.float32)
    for c in range(8):
        b, hf = c // 2, c % 2
        o = res.results[c]["out"]
        for j in range(NQ):
            g = 2 * j + hf
            out[b, g * 128:(g + 1) * 128] = o[j * 128:(j + 1) * 128]
    return out
```

```python
import os
import numpy as np
import concourse.bass as bass
import concourse.mybir as mybir
from concourse.bass import IndirectOffsetOnAxis
from concourse.bass_utils import run_bass_kernel_spmd
from contextlib import ExitStack

F32 = mybir.dt.float32
BF16 = mybir.dt.bfloat16
U32 = mybir.dt.uint32
I32 = mybir.dt.int32
ACT = mybir.ActivationFunctionType
ALU = mybir.AluOpType
AX = mybir.AxisListType

D = 1024
SEQ = 4096
NB = 32
NQ = 16
ALPHA = 2.0 ** 0.25
LN_EPS = 1e-5
THETA = 500000.0
NEG = -1.0e30


class StopBuild(Exception):
    pass


class Buf:
    _n = 0

    def __init__(self, t, name=None):
        self.t = t
        self.w = None
        self.r = {}
        self.dsem = None
        self.dcnt = 0
        Buf._n += 1
        self.name = name or f"b{Buf._n}"

    def __getitem__(self, k):
        return self.t[k]


class Sched:
    def __init__(self, nc):
        self.nc = nc
        self.eng = {'pe': nc.tensor, 'act': nc.scalar, 'dve': nc.vector,
                    'pool': nc.gpsimd, 'sp': nc.sync}
        self.sem, self.cnt, self.seen = {}, {}, {}
        for k in self.eng:
            self.sem[k] = nc.alloc_semaphore(f"s_{k}")
            self.cnt[k] = 0
            self.seen[k] = {}
        self.n_ins = 0
        self.n_wait = 0
        self.allbufs = []

    def sb(self, name, shape, dt, es=None):
        self._uid = getattr(self, "_uid", 0) + 1
        uname = f"{name}_{self._uid}"
        if es is None:
            t = self.nc.alloc_sbuf_tensor(uname, shape, dt)
        else:
            t = es.enter_context(self.nc.sbuf_tensor(uname, shape, dt))
        b = Buf(t, uname)
        self.allbufs.append(b)
        return b

    def barrier(self):
        evs = [(self.sem[k], self.cnt[k]) for k in self.eng if self.cnt[k]]
        for b in self.allbufs:
            if b.dsem is not None and b.dcnt:
                evs.append((b.dsem, b.dcnt))
        for e in self.eng:
            self._wait(e, [ev for ev in evs if ev[0] is not self.sem[e]])

    def _wait(self, e, deps):
        E = self.eng[e]
        seen = self.seen[e]
        best = {}
        for ev in deps:
            if ev is None:
                continue
            sem, val = ev
            key = id(sem)
            if key not in best or best[key][1] < val:
                best[key] = (sem, val)
        for key, (sem, val) in best.items():
            if seen.get(key, 0) >= val:
                continue
            if e == 'pe' and sem is self.sem['pe']:
                continue
            E.wait_ge(sem, val)
            self.n_wait += 1
            seen[key] = val

    def _deps(self, reads, writes):
        deps = []
        for b in reads:
            deps.append(b.w)
        for b in writes:
            deps.append(b.w)
            deps.extend(b.r.values())
        return deps

    def _mark(self, ev, reads, writes):
        for b in writes:
            b.w = ev
            b.r = {}
        for b in reads:
            if b in writes:
                continue
            b.r[id(ev[0])] = ev

    def op(self, e, fn, reads=(), writes=()):
        self._wait(e, self._deps(reads, writes))
        ins = fn(self.eng[e])
        self.cnt[e] += 1
        ins.then_inc(self.sem[e], 1)
        self._mark((self.sem[e], self.cnt[e]), reads, writes)
        self.n_ins += 1
        return ins

    def dma(self, q, fn, track, reads=(), writes=()):
        if track.dsem is None:
            track.dsem = self.nc.alloc_semaphore(f"d_{track.name}")
        deps = self._deps(reads, writes)
        if track.dcnt:
            deps.append((track.dsem, track.dcnt))
        self._wait(q, deps)
        ins = fn(self.eng[q])
        track.dcnt += 16
        ins.then_inc(track.dsem, 16)
        self._mark((track.dsem, track.dcnt), reads, writes)
        self.n_ins += 1
        return ins

    def finish(self, e, bufs):
        deps = []
        for b in bufs:
            deps.append(b.w)
            deps.extend(b.r.values())
            if b.dsem is not None and b.dcnt:
                deps.append((b.dsem, b.dcnt))
        self._wait(e, deps)


def _mult(delta):
    m = ((delta >= 0) & (delta <= 128)).astype(np.float32)
    m += ((delta >= 0) & (delta <= 512) & (delta % 4 == 0)).astype(np.float32)
    m += ((delta >= 0) & (delta <= 2048) & (delta % 16 == 0)).astype(np.float32)
    return m


def host_consts(hf):
    f32 = np.float32
    c = {}
    c["ident"] = np.eye(128, dtype=f32)
    jj = np.arange(128)
    c["mtri"] = (jj[:, None] > jj[None, :]).astype(f32)
    rmat = np.zeros((128, 128), f32)
    for p in range(128):
        i = p % 64
        if i < 8:
            rmat[p + 8, p] = -1.0
        elif i < 16:
            rmat[p - 8, p] = 1.0
    c["rmat"] = rmat
    z = np.zeros((128, 255), f32)
    z[:, 127] = 1.0
    c["zcol"] = z
    c["iota16"] = np.tile(np.arange(16, dtype=f32)[None, :], (128, 1))
    c["iota128"] = np.tile(np.arange(128, dtype=f32)[None, :], (128, 1))
    inv_freq = (f32(THETA) ** (-(np.arange(8, dtype=f32) * f32(2.0) / f32(16)))).astype(f32)

    def rope(pos):
        ang = pos.astype(f32)[:, None] * inv_freq[None, :]
        cos, sin = np.cos(ang).astype(f32), np.sin(ang).astype(f32)
        C = np.ones((128, pos.shape[0]), f32)
        Sn = np.zeros((128, pos.shape[0]), f32)
        for p in range(128):
            i = p % 64
            if i < 16:
                C[p] = cos[:, i % 8]
                Sn[p] = sin[:, i % 8]
        return C, Sn

    c["ropeK_C"], c["ropeK_S"] = rope(np.arange(SEQ))
    own = np.concatenate([(2 * j + hf) * 128 + np.arange(128) for j in range(NQ)])
    c["ropeQ_C"], c["ropeQ_S"] = rope(own)
    s = np.arange(128)[:, None]
    q = np.arange(128)[None, :]
    mda = np.zeros((128, 18, 128), f32)
    for mi in range(18):
        o = mi - 16
        mda[:, mi, :] = _mult((hf - o) * 128 + q - s)
    c["maskDA"] = mda
    msb = np.zeros((128, 2, 128), f32)
    for o in range(2):
        msb[:, o, :] = (((hf - o) * 128 + q - s) > 0).astype(f32)
    c["maskSB"] = msb
    return c


CONST_SHAPES = {
    "ident": [128, 128], "mtri": [128, 128], "rmat": [128, 128], "zcol": [128, 255],
    "iota16": [128, 16], "iota128": [128, 128], "ropeK_C": [128, SEQ], "ropeK_S": [128, SEQ],
    "ropeQ_C": [128, 2048], "ropeQ_S": [128, 2048],
    "maskDA": [128, 18, 128], "maskSB": [128, 2, 128],
}


def build(stage="full", dbg=False):
    nc = bass.Bass("TRN2", target_bir_lowering=False)
    S = Sched(nc)

    def din(name, shape, dt=F32):
        return nc.dram_tensor(name, shape, dt, kind="ExternalInput").ap()

    xT_d = din("xT", [D, SEQ])
    xTq_d = din("xTq", [D, 2048])
    xown_d = din("xown", [2048, D])
    w_in_d = din("w_in", [D, 3072])
    w_out_d = din("w_out", [D, D])
    ln1g_d = din("ln1_g", [1, D])
    ln1b_d = din("ln1_b", [1, D])
    wq_d = din("wq", [D, 2048])
    keys_d = din("keys", [16 * 128, 128])
    pu_d = din("peer_u", [16384, D])
    pv_d = din("peer_v", [16384, D])
    ln2g_d = din("ln2_g", [1, D])
    ln2b_d = din("ln2_b", [1, D])
    cd = {k: din("c_" + k, v) for k, v in CONST_SHAPES.items()}
    out_d = nc.dram_tensor("out", [2048, D], F32, kind="ExternalOutput").ap()
    dbg_outs = {}

    def dbg_dump(name, buf, ap, shape, dt=F32, allb=()):
        if not dbg:
            return
        t = nc.dram_tensor("dbg_" + name, shape, dt, kind="ExternalOutput").ap()
        S._wait('sp', [b.w for b in allb])
        S.dma('sp', lambda e: e.dma_start(out=t, in_=ap), buf, reads=[buf])
        dbg_outs[name] = buf

    PS = [Buf(nc.alloc_psum_tensor(f"ps{i}", [128, 512], F32), f"ps{i}") for i in range(8)]

    ident = S.sb("ident", [128, 128], F32)
    identb = S.sb("identb", [128, 128], BF16)
    mtri = S.sb("mtri", [128, 128], BF16)
    onesb = S.sb("onesb", [128, 128], BF16)
    compb = S.sb("compb", [128, 128], BF16)
    zerob = S.sb("zerob", [128, 512], BF16)
    rmat = S.sb("rmat", [128, 128], BF16)
    zcol = S.sb("zcol", [128, 255], BF16)
    iota16 = S.sb("iota16", [128, 16], F32)
    iota128 = S.sb("iota128", [128, 128], F32)
    maskDA = S.sb("maskDA", [128, 18, 128], BF16)
    maskSB = S.sb("maskSB", [128, 2, 128], BF16)
    es_o = ExitStack()
    o_all = S.sb("o_all", [128, NQ, D], BF16, es_o)
    if dbg:
        S.op('pool', lambda e: e.memset(o_all[:], 0.0), writes=[o_all])
    es0 = ExitStack()
    cstage = S.sb("cstage", [128, 18 * 128], F32, es0)

    def load_const(dst, name, n):
        src = cd[name]
        if len(src.shape) == 3:
            src = src.rearrange("p a b -> p (a b)")
        S.dma('sp', lambda e: e.dma_start(out=cstage[:, 0:n], in_=src), cstage, writes=[cstage])
        dflat = dst[:]
        if len(dflat.shape) == 3:
            dflat = dflat.rearrange("p a b -> p (a b)")
        S.op('dve', lambda e: e.tensor_copy(out=dflat, in_=cstage[:, 0:n]), reads=[cstage], writes=[dst])

    S.dma('sp', lambda e: e.dma_start(out=ident[:], in_=cd["ident"]), ident, writes=[ident])
    S.dma('sp', lambda e: e.dma_start(out=iota16[:], in_=cd["iota16"]), iota16, writes=[iota16])
    S.dma('sp', lambda e: e.dma_start(out=iota128[:], in_=cd["iota128"]), iota128, writes=[iota128])
    load_const(identb, "ident", 128)
    load_const(mtri, "mtri", 128)
    load_const(rmat, "rmat", 128)
    load_const(zcol, "zcol", 255)
    load_const(maskDA, "maskDA", 18 * 128)
    load_const(maskSB, "maskSB", 2 * 128)
    S.op('pool', lambda e: e.memset(onesb[:], 1.0), writes=[onesb])
    S.op('pool', lambda e: e.memset(zerob[:], 0.0), writes=[zerob])
    S.op('pool', lambda e: e.tensor_tensor(out=compb[:], in0=onesb[:], in1=mtri[:], op=ALU.subtract),
         reads=[onesb, mtri], writes=[compb])
    S.barrier()
    es0.close()

    TC = 256
    NTC = SEQ // TC
    NTCQ = 2048 // TC

    es_kv = ExitStack()
    KT = es_kv.enter_context(nc.sbuf_tensor("KT", [128, 4, SEQ], BF16))
    QT = es_kv.enter_context(nc.sbuf_tensor("QT", [128, 8, 2048], BF16))
    V = es_kv.enter_context(nc.sbuf_tensor("V", [128, NB, 8, 65], BF16))
    KTb = [[Buf(KT, f"KT{hp}_{tc}") for tc in range(NTC)] for hp in range(4)]
    QTb = [[Buf(QT, f"QT{h}_{tc}") for tc in range(NTCQ)] for h in range(8)]
    S.op('pool', lambda e: e.memset(QT[:, :, :], 0.0), writes=[b for r in QTb for b in r])
    Vb = [Buf(V, f"V{g}") for g in range(NB)]
    Vones = Buf(V, "Vones")
    S.op('pool', lambda e: e.memset(V[:, :, :, 64:65], 1.0), writes=[Vones])

    def ktb(hp, kb):
        return KTb[hp][(kb * 128) // TC]

    def qtb(h, j):
        return QTb[h][(j * 128) // TC]

    ut_s = nc.dram_tensor("ut_s", [128, 128, 1024], BF16).ap()
    v_s = nc.dram_tensor("v_s", [16384, D], BF16).ap()

    class C0:
        NPH = 7

        def __init__(self):
            self.c = 0
            self.phase = 0
            self.loaded = set()
            self.t = None
            self.prefetch = True

        def bind(self, es_):
            self.t = {
                "us": [S.sb(f"ustg{i}", [128, D], F32, es_) for i in range(2)],
                "vs": [S.sb(f"vstg{i}", [128, D], F32, es_) for i in range(2)],
                "ub": S.sb("utb", [128, D], BF16, es_),
                "vb": S.sb("vbb", [128, D], BF16, es_),
                "bank": Buf(PS[1].t, f"c0bank{self.c}"),
            }
            self.loaded = set()

        def load(self, c):
            if c in self.loaded or c >= 128:
                return
            us, vs = self.t["us"][c % 2], self.t["vs"][c % 2]
            S.dma('sp', lambda e: e.dma_start(out=us[:], in_=pu_d[c * 128:(c + 1) * 128, :]), us, writes=[us])
            S.dma('sp', lambda e: e.dma_start(out=vs[:], in_=pv_d[c * 128:(c + 1) * 128, :]), vs, writes=[vs])
            self.loaded.add(c)

        def pe_round(self, c, r):
            us, bank = self.t["us"][c % 2], self.t["bank"]
            for q4 in range(4):
                dc = r * 4 + q4
                S.op('pe', lambda e: e.transpose(out=bank[:, q4 * 128:(q4 + 1) * 128],
                                                in_=us[:, dc * 128:(dc + 1) * 128], identity=ident[:]),
                     reads=[us, ident], writes=[bank])

        def act_round(self, c, r):
            ub, bank = self.t["ub"], self.t["bank"]
            S.op('act', lambda e: e.copy(out=ub[:, r * 512:(r + 1) * 512], in_=bank[:, :]), reads=[bank], writes=[ub])

        def cast(self, c):
            vs, vb = self.t["vs"][c % 2], self.t["vb"]
            S.op('pool', lambda e: e.tensor_copy(out=vb[:], in_=vs[:]), reads=[vs], writes=[vb])

        def store(self, c):
            ub, vb = self.t["ub"], self.t["vb"]
            S.dma('sp', lambda e: e.dma_start(out=ut_s[c], in_=ub[:]), ub, reads=[ub])
            S.dma('sp', lambda e: e.dma_start(out=v_s[c * 128:(c + 1) * 128, :], in_=vb[:]), vb, reads=[vb])

        def tick(self):
            if self.c >= 128:
                return
            c, ph = self.c, self.phase
            if ph == 0:
                self.load(c)
                self.pe_round(c, 0)
                if self.prefetch:
                    self.load(c + 1)
            elif ph == 1:
                self.act_round(c, 0)
            elif ph == 2:
                self.pe_round(c, 1)
            elif ph == 3:
                self.act_round(c, 1)
                self.cast(c)
            elif ph == 4:
                self.store(c)
            self.phase += 1
            if self.phase == self.NPH:
                self.phase = 0
                self.loaded.discard(c)
                self.c += 1

        def flush(self):
            self.prefetch = False
            while self.c < 128 and (self.phase != 0 or self.c in self.loaded):
                self.tick()
            self.prefetch = True

        def finish_all(self):
            while self.c < 128:
                self.tick()

    c0 = C0()
    P = {}

    def alloc_proj(es):
        P["w_bf"] = S.sb("w_bf", [128, 8, 1536], BF16, es)
        P["wst"] = [S.sb(f"wst{i}", [128, 1536], F32, es) for i in range(1)]
        P["xst"] = [S.sb(f"xst{i}", [128, 8, TC], F32, es) for i in range(2)]
        P["xcb"] = [S.sb(f"xcb{i}", [128, 8, TC], BF16, es) for i in range(2)]
        P["ropC"] = [S.sb(f"ropC{i}", [128, TC], F32, es) for i in range(2)]
        P["ropS"] = [S.sb(f"ropS{i}", [128, TC], F32, es) for i in range(2)]
        P["rawb"] = [S.sb(f"rawb{i}", [128, TC], BF16, es) for i in range(2)]
        P["rt1"] = [S.sb(f"rt1_{i}", [128, TC], F32, es) for i in range(2)]
        P["rt2"] = [S.sb(f"rt2_{i}", [128, TC], F32, es) for i in range(2)]

    def alloc_attn(es):
        P["e_t"] = [S.sb(f"e_t{i}", [128, 512], F32, es) for i in range(3)]
        P["sp_t"] = [S.sb(f"sp_t{i}", [128, 512], F32, es) for i in range(5)]
        P["l1m_t"] = [S.sb(f"l1m_t{i}", [128, 512], BF16, es) for i in range(4)]
        P["tt_t"] = [S.sb(f"tt_t{i}", [128, 512], F32, es) for i in range(4)]
        P["a_t"] = [S.sb(f"a_t{i}", [128, 512], BF16, es) for i in range(5)]
        P["am_t"] = [S.sb(f"am_t{i}", [128, 512], BF16, es) for i in range(5)]
        P["lsum"] = [S.sb(f"lsum{i}", [128, 512], BF16, es) for i in range(6)]
        P["rec_t"] = [S.sb(f"rec_t{i}", [128, 4], F32, es) for i in range(2)]
        c0.bind(es)

    cnt = {"w": 0, "x": 0, "pk": 0, "ev": 0}

    def load_pass_weights(colq, colk, colv):
        for dc in range(8):
            st = P["wst"][0]
            cnt["w"] += 1
            for i, c0 in enumerate((colq, colk, colv)):
                q = 'sp'
                S.dma(q, lambda e, c0=c0, i=i: e.dma_start(
                    out=st[:, i * 512:(i + 1) * 512], in_=w_in_d[dc * 128:(dc + 1) * 128, c0:c0 + 512]),
                    st, writes=[st])
            S.op('pool', lambda e: e.tensor_copy(out=P["w_bf"][:, dc, :], in_=st[:]), reads=[st], writes=[P["w_bf"]])

    def evac(kind, rope, ps, dsts, Cb, Sb, scale):
        k = cnt["ev"] % 2
        cnt["ev"] += 1
        if not rope:
            for (p0, p1, dst_ap, dst_buf) in dsts:
                S.op('act', lambda e: e.mul(out=dst_ap, in_=ps[p0:p1, 0:TC], mul=scale),
                     reads=[ps], writes=[dst_buf])
            return
        raw, t1, t2 = P["rawb"][k], P["rt1"][k], P["rt2"][k]
        psw = PS[2 + k]
        S.op('act', lambda e: e.mul(out=raw[:], in_=ps[:, 0:TC], mul=scale),
             reads=[ps], writes=[raw])
        S.op('pe', lambda e: e.matmul(out=psw[:, 0:TC], lhsT=rmat[:], rhs=raw[:], start=True, stop=True),
             reads=[rmat, raw], writes=[psw])
        S.op('dve', lambda e: e.tensor_tensor(out=t1[:], in0=psw[:, 0:TC], in1=Sb[:], op=ALU.mult),
             reads=[psw, Sb], writes=[t1])
        S.op('dve', lambda e: e.scalar_tensor_tensor(out=t2[:], in0=ps[:, 0:TC], scalar=scale, in1=Cb[:],
                                                    op0=ALU.mult, op1=ALU.mult),
             reads=[ps, Cb], writes=[t2])
        for (p0, p1, dst_ap, dst_buf) in dsts:
            S.op('pool', lambda e: e.tensor_tensor(out=dst_ap, in0=t1[p0:p1, :], in1=t2[p0:p1, :], op=ALU.add),
                 reads=[t1, t2], writes=[dst_buf])

    def project_pass(rope):
        for tc in range(NTC):
            k = cnt["x"] % 2
            cnt["x"] += 1
            xs, xb, Cb, Sb = P["xst"][k], P["xcb"][k], P["ropC"][k], P["ropS"][k]
            S.dma('sp', lambda e: e.dma_start(
                out=xs[:], in_=xT_d[:, tc * TC:(tc + 1) * TC].rearrange("(c p) t -> p c t", p=128)),
                xs, writes=[xs])
            S.op('dve', lambda e: e.tensor_copy(out=xb[:], in_=xs[:]), reads=[xs], writes=[xb])
            if rope:
                S.dma('sp', lambda e: e.dma_start(out=Cb[:], in_=cd["ropeK_C"][:, tc * TC:(tc + 1) * TC]),
                      Cb, writes=[Cb])
                S.dma('sp', lambda e: e.dma_start(out=Sb[:], in_=cd["ropeK_S"][:, tc * TC:(tc + 1) * TC]),
                      Sb, writes=[Sb])
            for hp in range(4):
                ps = PS[cnt["pk"] % 2]
                cnt["pk"] += 1
                for dc in range(8):
                    S.op('pe', lambda e, dc=dc: e.matmul(
                        out=ps[:, 0:TC], lhsT=P["w_bf"][:, dc, 512 + hp * 128:512 + (hp + 1) * 128],
                        rhs=xb[:, dc, :], start=(dc == 0), stop=(dc == 7)),
                        reads=[P["w_bf"], xb], writes=[ps])
                evac("k", rope, ps, [(0, 128, KT[:, hp, tc * TC:(tc + 1) * TC], KTb[hp][tc])], Cb, Sb, 1.0)
            for blk in range(TC // 128):
                g = tc * (TC // 128) + blk
                ps = PS[cnt["pk"] % 2]
                cnt["pk"] += 1
                for dc in range(8):
                    S.op('pe', lambda e, dc=dc: e.matmul(
                        out=ps[:, :], lhsT=xb[:, dc, blk * 128:(blk + 1) * 128],
                        rhs=P["w_bf"][:, dc, 1024:1536], start=(dc == 0), stop=(dc == 7)),
                        reads=[P["w_bf"], xb], writes=[ps])
                S.op('act', lambda e: e.copy(
                    out=V[:, g, :, 0:64], in_=ps[:, :].rearrange("p (h d) -> p h d", d=64)),
                    reads=[ps], writes=[Vb[g]])
        for tc in range(NTCQ):
            k = cnt["x"] % 2
            cnt["x"] += 1
            xs, xb, Cb, Sb = P["xst"][k], P["xcb"][k], P["ropC"][k], P["ropS"][k]
            S.dma('sp', lambda e: e.dma_start(
                out=xs[:], in_=xTq_d[:, tc * TC:(tc + 1) * TC].rearrange("(c p) t -> p c t", p=128)),
                xs, writes=[xs])
            S.op('dve', lambda e: e.tensor_copy(out=xb[:], in_=xs[:]), reads=[xs], writes=[xb])
            if rope:
                S.dma('sp', lambda e: e.dma_start(out=Cb[:], in_=cd["ropeQ_C"][:, tc * TC:(tc + 1) * TC]),
                      Cb, writes=[Cb])
                S.dma('sp', lambda e: e.dma_start(out=Sb[:], in_=cd["ropeQ_S"][:, tc * TC:(tc + 1) * TC]),
                      Sb, writes=[Sb])
            for hp in range(4):
                ps = PS[cnt["pk"] % 2]
                cnt["pk"] += 1
                for dc in range(8):
                    S.op('pe', lambda e, dc=dc: e.matmul(
                        out=ps[:, 0:TC], lhsT=P["w_bf"][:, dc, hp * 128:(hp + 1) * 128],
                        rhs=xb[:, dc, :], start=(dc == 0), stop=(dc == 7)),
                        reads=[P["w_bf"], xb], writes=[ps])
                evac("q", rope, ps, [(0, 64, QT[0:64, 2 * hp, tc * TC:(tc + 1) * TC], QTb[2 * hp][tc]),
                                     (64, 128, QT[64:128, 2 * hp + 1, tc * TC:(tc + 1) * TC], QTb[2 * hp + 1][tc])],
                     Cb, Sb, 0.125)

    def zmatmuls(pz, hb, j, kb):
        for hh in range(4):
            h = hb * 4 + hh
            hp = h // 2
            S.op('pe', lambda e, hh=hh, hp=hp, h=h: e.matmul(
                out=pz[:, hh * 128:(hh + 1) * 128],
                lhsT=KT[:, hp, kb * 128:(kb + 1) * 128],
                rhs=QT[:, h, j * 128:(j + 1) * 128], start=True, stop=True),
                reads=[ktb(hp, kb), qtb(h, j)], writes=[pz])

    def run_pipeline(iters, skews, hook=None):
        N = len(iters)
        for n in range(N + max(skews)):
            if hook is not None:
                hook()
            for st, sk in enumerate(skews):
                i = n - sk
                if 0 <= i < N:
                    iters[i][st]()

    PZ = [PS[0], PS[2], PS[3]]
    PL = [PS[4], PS[5]]

    def attention_da(jlist):
        iters = []
        nb = 0
        for j in jlist:
            for hb in range(2):
                po = PS[6 + (nb % 2)]
                rec = P["rec_t"][nb % 2]
                nb += 1
                kbs = [2 * j + o for o in range(1, -17, -1) if 2 * j + o >= 0]
                for n, kb in enumerate(kbs):
                    it = len(iters)
                    pz, a, am = PZ[it % 3], P["a_t"][it % 5], P["am_t"][it % 5]
                    mi = kb - 2 * j + 16
                    first, last = (n == 0), (n == len(kbs) - 1)

                    def stA(pz=pz, a=a, am=am, hb=hb, j=j, kb=kb, mi=mi):
                        zmatmuls(pz, hb, j, kb)
                        S.op('act', lambda e: e.activation(out=a[:], in_=pz[:, :], func=ACT.Exp),
                             reads=[pz], writes=[a])
                        S.op('dve', lambda e: e.tensor_tensor(
                            out=am[:].rearrange("p (h q) -> p h q", h=4),
                            in0=a[:].rearrange("p (h q) -> p h q", h=4),
                            in1=maskDA[:, mi, :].unsqueeze(1).to_broadcast([128, 4, 128]), op=ALU.mult),
                            reads=[a, maskDA], writes=[am])

                    def stB(po=po, am=am, hb=hb, j=j, kb=kb, first=first, last=last, rec=rec):
                        if first:
                            S.op('pe', lambda e: e.matmul(out=po[:, 0:260], lhsT=zerob[:, 0:128], rhs=zerob[:, 0:260],
                                                          start=True, stop=False, skip_group_check=True),
                                 reads=[zerob], writes=[po])
                        for hh in range(4):
                            h = hb * 4 + hh
                            S.op('pe', lambda e, hh=hh, h=h: e.matmul(
                                out=po[:, hh * 65:(hh + 1) * 65], lhsT=am[:, hh * 128:(hh + 1) * 128],
                                rhs=V[:, kb, h, 0:65], start=False, stop=False, skip_group_check=True),
                                reads=[am, Vb[kb], Vones], writes=[po])
                        if last:
                            S.op('pe', lambda e: e.matmul(out=po[:, 0:260], lhsT=zerob[:, 0:128], rhs=zerob[:, 0:260],
                                                          start=False, stop=True, skip_group_check=True),
                                 reads=[zerob], writes=[po])
                            pov = po[:, 0:260].rearrange("p (h d) -> p h d", d=65)
                            S.op('dve', lambda e: e.reciprocal(out=rec[:], in_=pov[:, :, 64]), reads=[po], writes=[rec])
                            S.op('dve', lambda e: e.tensor_tensor(
                                out=o_all[:, j, hb * 256:(hb + 1) * 256].rearrange("p (h d) -> p h d", d=64),
                                in0=pov[:, :, 0:64], in1=rec[:].unsqueeze(2).to_broadcast([128, 4, 64]), op=ALU.mult),
                                reads=[po, rec], writes=[o_all])

                    iters.append([stA, stB])
        run_pipeline(iters, [0, 3], hook=c0.tick)
        c0.flush()

    def attention_sb(jlist):
        iters = []
        nb = 0
        for j in jlist:
            for hb in range(2):
                po = PS[6 + (nb % 2)]
                LS = P["lsum"][(nb % 2) * 3:(nb % 2) * 3 + 3]
                nb += 1
                kbs = list(range(2 * j + 1, -1, -1))
                for n, kb in enumerate(kbs):
                    it = len(iters)
                    pz, pl = PZ[it % 3], PL[(nb - 1) % 2]
                    e_, sp, l1m, tt, a, am = (P["e_t"][it % 3], P["sp_t"][it % 5], P["l1m_t"][it % 4], P["tt_t"][it % 4],
                                              P["a_t"][it % 5], P["am_t"][it % 5])
                    o = kb - 2 * j
                    masked = o >= 0
                    first, last = (n == 0), (n == len(kbs) - 1)
                    l1prev = P["l1m_t"][(it - 1) % 4]
                    mk = maskSB[:, max(o, 0), :].unsqueeze(1).to_broadcast([128, 4, 128])

                    def stA1(pz=pz, e_=e_, sp=sp, hb=hb, j=j, kb=kb):
                        zmatmuls(pz, hb, j, kb)
                        S.op('act', lambda e: e.activation(out=e_[:], in_=pz[:, :], func=ACT.Exp, scale=-1.0),
                             reads=[pz], writes=[e_])
                        S.op('act', lambda e: e.activation(out=sp[:], in_=e_[:], func=ACT.Ln, bias=1.0),
                             reads=[e_], writes=[sp])

                    def stA2(pz=pz, sp=sp, l1m=l1m, masked=masked, mk=mk):
                        S.op('dve', lambda e: e.scalar_tensor_tensor(
                            out=l1m[:], in0=pz[:, :], scalar=-1.0, in1=sp[:], op0=ALU.mult, op1=ALU.subtract),
                            reads=[pz, sp], writes=[l1m])
                        if masked:
                            S.op('pool', lambda e: e.tensor_tensor(
                                out=l1m[:].rearrange("p (h q) -> p h q", h=4),
                                in0=l1m[:].rearrange("p (h q) -> p h q", h=4), in1=mk, op=ALU.mult),
                                reads=[l1m, maskSB], writes=[l1m])

                    def stB1(pl=pl, sp=sp, l1m=l1m, tt=tt, first=first, l1prev=l1prev):
                        if not first:
                            S.op('pe', lambda e: e.matmul(out=pl[:, :], lhsT=compb[:], rhs=l1prev[:], start=False, stop=False,
                                                          skip_group_check=True), reads=[compb, l1prev], writes=[pl])
                        S.op('pe', lambda e: e.matmul(out=pl[:, :], lhsT=mtri[:], rhs=l1m[:], start=first, stop=True,
                                                      skip_group_check=True), reads=[mtri, l1m], writes=[pl])
                        S.op('dve', lambda e: e.tensor_tensor(out=tt[:], in0=pl[:, :], in1=sp[:], op=ALU.subtract),
                             reads=[pl, sp], writes=[tt])

                    def stB2(tt=tt, a=a, am=am, masked=masked, mk=mk):
                        S.op('act', lambda e: e.activation(out=a[:], in_=tt[:], func=ACT.Exp), reads=[tt], writes=[a])
                        if masked:
                            S.op('pool', lambda e: e.tensor_tensor(
                                out=am[:].rearrange("p (h q) -> p h q", h=4),
                                in0=a[:].rearrange("p (h q) -> p h q", h=4), in1=mk, op=ALU.mult),
                                reads=[a, maskSB], writes=[am])

                    def stC(po=po, a=a, am=am, masked=masked, hb=hb, j=j, kb=kb, first=first, last=last):
                        src = am if masked else a
                        if first:
                            S.op('pe', lambda e: e.matmul(out=po[:, 0:256], lhsT=zerob[:, 0:128], rhs=zerob[:, 0:256],
                                                          start=True, stop=False, skip_group_check=True),
                                 reads=[zerob], writes=[po])
                        for hh in range(4):
                            h = hb * 4 + hh
                            S.op('pe', lambda e, hh=hh, h=h: e.matmul(
                                out=po[:, hh * 64:(hh + 1) * 64], lhsT=src[:, hh * 128:(hh + 1) * 128],
                                rhs=V[:, kb, h, 0:64], start=False, stop=False, skip_group_check=True),
                                reads=[src, Vb[kb]], writes=[po])
                        if last:
                            S.op('pe', lambda e: e.matmul(out=po[:, 0:256], lhsT=zerob[:, 0:128], rhs=zerob[:, 0:256],
                                                          start=False, stop=True, skip_group_check=True),
                                 reads=[zerob], writes=[po])
                            S.op('act', lambda e: e.copy(out=o_all[:, j, 512 + hb * 256:512 + (hb + 1) * 256],
                                                         in_=po[:, 0:256]), reads=[po], writes=[o_all])

                    iters.append([stA1, stB1, stA2, stB2, stC])
        run_pipeline(iters, [0, 2, 1, 3, 4], hook=c0.tick)
        c0.flush()

    jl = list(range(NQ))
    if stage == "projda":
        jl = []
    if stage in ("ln1", "topk", "peer2") or stage.startswith("topk_"):
        jl = [0, 1]
    if stage == "peer4":
        jl = [0, 1, 2, 3]
    es = ExitStack()
    alloc_proj(es)
    load_pass_weights(0, 512, 1024)
    project_pass(rope=True)
    if dbg and stage in ("projda",):
        dbg_dump("KT", KTb[0][0], KT[:, :, :], [128, 4, SEQ], BF16, [b for r in KTb for b in r])
        dbg_dump("QT", QTb[0][0], QT[:, :, :], [128, 8, 2048], BF16, [b for r in QTb for b in r])
        dbg_dump("V", Vb[0], V[:, :, :, :], [128, NB, 8, 65], BF16, Vb)
        S.finish('sp', list(dbg_outs.values()) + [b for r in KTb for b in r] + [b for r in QTb for b in r] + Vb)
        return nc
    S.barrier()
    es.close()
    es = ExitStack()
    alloc_attn(es)
    attention_da({"da1": [0, 9], "sb1": []}.get(stage, jl))
    if stage in ("da", "da1"):
        dbg_dump("o_all", o_all, o_all[:, :, :], [128, NQ, D], BF16)
        S.finish('sp', [o_all])
        return nc
    S.barrier()
    es.close()
    es = ExitStack()
    alloc_proj(es)
    load_pass_weights(1536, 2048, 2560)
    project_pass(rope=False)
    S.barrier()
    es.close()
    es = ExitStack()
    alloc_attn(es)
    attention_sb({"sb1": [0, 9]}.get(stage, jl))
    if stage in ("sb", "sb1", "attn"):
        dbg_dump("o_all", o_all, o_all[:, :, :], [128, NQ, D], BF16)
        S.finish('sp', [o_all])
        return nc


    S.barrier()
    es.close()
    es_kv.close()
    PSb = [Buf(PS[i].t, f"psc{i}") for i in range(8)]
    if c0.c < 128:
        es_c0 = ExitStack()
        c0.bind(es_c0)
        c0.finish_all()
        S.barrier()
        es_c0.close()
    x1_s = nc.dram_tensor("x1_s", [2048, D], F32).ap()
    esP = ExitStack()
    lnp = S.sb("lnp", [128, 2, D], F32, esP)
    i1T_all = S.sb("i1T_all", [128, 2048], BF16, esP)
    i2T_all = S.sb("i2T_all", [128, 2048], BF16, esP)
    gT_all = S.sb("gT_all", [128, 2048], BF16, esP)
    stats = S.sb("stats", [128, 2, 6], F32, esP)
    mv = S.sb("mv", [128, 2], F32, esP)
    rstd = S.sb("rstd", [128, 1], F32, esP)
    for i, dsrc in enumerate((ln1g_d, ln1b_d)):
        S.dma('sp', lambda e: e.dma_start(out=lnp[:, i, :], in_=dsrc.to_broadcast([128, D])), lnp, writes=[lnp])
    LNS = {"lnp": lnp, "stats": stats, "mv": mv, "rstd": rstd}
    if stage != "full":
        for tb_ in (i1T_all, i2T_all, gT_all):
            S.op('pool', lambda e: e.memset(tb_[:], 0.0), writes=[tb_])
    x1T_s = nc.dram_tensor("x1T_s", [128, 8 * 2048], BF16).ap()
    idx_s = nc.dram_tensor("idx_s", [3, 128, 2048], BF16).ap()
    x1T_sv = x1T_s.rearrange("p (a b) -> p a b", a=8)

    nchunks = 128
    es = ExitStack()
    w_out_bf = S.sb("w_out_bf", [128, 8, D], BF16, es)
    wq_bf = S.sb("wq_bf", [128, 8, 2048], BF16, es)
    keysT = S.sb("keysT", [128, 16, 128], BF16, es)
    def load_c1_weights(stg):
        for dc in range(8):
            S.dma('sp', lambda e: e.dma_start(out=stg[:, 0:D], in_=w_out_d[dc * 128:(dc + 1) * 128, :]), stg, writes=[stg])
            S.op('pool', lambda e: e.tensor_copy(out=w_out_bf[:, dc, :], in_=stg[:, 0:D]), reads=[stg], writes=[w_out_bf])
        for dc in range(8):
            for hh_ in range(2):
                S.dma('sp', lambda e: e.dma_start(out=stg[:, :], in_=wq_d[dc * 128:(dc + 1) * 128, hh_ * 1024:(hh_ + 1) * 1024]),
                      stg, writes=[stg])
                S.op('dve', lambda e: e.tensor_copy(out=wq_bf[:, dc, hh_ * 1024:(hh_ + 1) * 1024], in_=stg[:, :]),
                     reads=[stg], writes=[wq_bf])
        for hpc in range(16):
            S.dma('sp', lambda e: e.dma_start(out=stg[:, 0:128], in_=keys_d[hpc * 128:(hpc + 1) * 128, :]), stg, writes=[stg])
            S.op('pe', lambda e: e.transpose(out=PSb[4][:, 0:128], in_=stg[:, 0:128], identity=ident[:]),
                 reads=[stg, ident], writes=[PSb[4]])
            S.op('act', lambda e: e.copy(out=keysT[:, hpc, :], in_=PSb[4][:, 0:128]), reads=[PSb[4]], writes=[keysT])

    def mkset(tag, pmix, pgen):
        T = {}
        T["xo"] = S.sb(f"xo{tag}", [128, D], F32, es)
        T["oT"] = S.sb(f"oT{tag}", [128, 8, 128], BF16, es)
        T["r1"] = S.sb(f"r1{tag}", [128, D], F32, es)
        T["x1"] = S.sb(f"x1{tag}", [128, D], F32, es)
        T["x1b"] = S.sb(f"x1b{tag}", [128, D], BF16, es)
        T["x1T"] = S.sb(f"x1T{tag}", [128, 8, 128], BF16, es)
        T["qT"] = S.sb(f"qT{tag}", [128, 16, 128], BF16, es)
        T["big"] = S.sb(f"big{tag}", [128, 2048], F32, es)
        T["wk"] = [S.sb(f"wk{tag}{i}", [128, 128], F32, es) for i in range(2)]
        T["cw"] = [S.sb(f"cw{tag}{i}", [128, 256], F32, es) for i in range(2)]
        T["tv"] = S.sb(f"tv{tag}", [128, 16, 16], F32, es)
        T["ti"] = S.sb(f"ti{tag}", [128, 16, 16], U32, es)
        T["tif"] = S.sb(f"tif{tag}", [128, 16, 16], F32, es)
        for nm in ("bs", "paf", "pbf", "dd", "ee", "gates", "i1", "i2"):
            T[nm] = S.sb(f"{nm}{tag}", [128, 8, 16], F32, es)
        for nm in ("bp", "pa", "pb"):
            T[nm] = S.sb(f"{nm}{tag}", [128, 8, 16], U32, es)
        T["ssum"] = S.sb(f"ssum{tag}", [128, 8], F32, es)
        T["rsum"] = S.sb(f"rsum{tag}", [128, 8], F32, es)
        T["stats"] = S.sb(f"stats{tag}", [128, 2, 6], F32, es)
        T["mv"] = S.sb(f"mv{tag}", [128, 2], F32, es)
        T["rstd"] = S.sb(f"rstd{tag}", [128, 1], F32, es)
        T["pmix"] = pmix
        T["pgen"] = pgen
        return T

    TA = mkset("A", [PSb[0], PSb[1]], [PSb[4], PSb[5]])
    TB = mkset("B", [PSb[2], PSb[3]], [PSb[6], PSb[7]])

    def layer_norm(src, dst, gi, bi, T=None):
        lnp = LNS["lnp"]
        TT = T if T is not None else LNS
        stats, mv, rstd = TT["stats"], TT["mv"], TT["rstd"]
        for c in range(2):
            yield S.op('dve', lambda e: e.bn_stats(out=stats[:, c, :], in_=src[:, c * 512:(c + 1) * 512]),
                 reads=[src], writes=[stats])
        yield S.op('dve', lambda e: e.bn_aggr(out=mv[:], in_=stats[:].rearrange("p a b -> p (a b)")),
             reads=[stats], writes=[mv])
        yield S.op('act', lambda e: e.activation(out=rstd[:], in_=mv[:, 1:2], func=ACT.Sqrt, bias=LN_EPS),
             reads=[mv], writes=[rstd])
        yield S.op('dve', lambda e: e.reciprocal(out=rstd[:], in_=rstd[:]), reads=[rstd], writes=[rstd])
        yield S.op('dve', lambda e: e.tensor_scalar(out=dst[:], in0=src[:], scalar1=mv[:, 0:1], scalar2=rstd[:, 0:1],
                                              op0=ALU.subtract, op1=ALU.mult), reads=[src, mv, rstd], writes=[dst])
        yield S.op('pool', lambda e: e.tensor_tensor(out=dst[:], in0=dst[:], in1=lnp[:, gi, :], op=ALU.mult),
             reads=[dst, lnp], writes=[dst])
        yield S.op('pool', lambda e: e.tensor_tensor(out=dst[:], in0=dst[:], in1=lnp[:, bi, :], op=ALU.add),
             reads=[dst, lnp], writes=[dst])

    def ln1_block(j, T):
        X1, X1b, xo, oT, r1 = T["x1"], T["x1b"], T["xo"], T["oT"], T["r1"]
        PM = T["pmix"]
        pT = T["pgen"][0]
        pTb = pT[:, :].bitcast(BF16)
        for fc in range(8):
            yield S.op('pe', lambda e: e.transpose(out=pTb[:, fc * 128:(fc + 1) * 128],
                                            in_=o_all[:, j, fc * 128:(fc + 1) * 128], identity=identb[:]),
                 reads=[o_all, identb], writes=[pT])
        yield S.op('act', lambda e: e.copy(out=oT[:].rearrange("p a b -> p (a b)"), in_=pTb[:, 0:1024]),
             reads=[pT], writes=[oT])
        for half in range(2):
            for fc in range(8):
                yield S.op('pe', lambda e: e.matmul(out=PM[half][:, :], lhsT=oT[:, fc, :],
                                              rhs=w_out_bf[:, fc, half * 512:(half + 1) * 512],
                                              start=(fc == 0), stop=(fc == 7)),
                     reads=[oT, w_out_bf], writes=[PM[half]])
        yield S.dma('sp', lambda e: e.dma_start(out=xo[:], in_=xown_d[j * 128:(j + 1) * 128, :]), xo, writes=[xo])
        for half in range(2):
            yield S.op('dve', lambda e: e.scalar_tensor_tensor(
                out=r1[:, half * 512:(half + 1) * 512], in0=xo[:, half * 512:(half + 1) * 512], scalar=ALPHA,
                in1=PM[half][:, :], op0=ALU.mult, op1=ALU.add), reads=[xo, PM[half]], writes=[r1])
        yield from layer_norm(r1, X1, 0, 1, T)
        yield S.op('act', lambda e: e.copy(out=X1b[:], in_=X1[:]), reads=[X1], writes=[X1b])
        yield S.dma('sp', lambda e: e.dma_start(out=x1_s[j * 128:(j + 1) * 128, :], in_=X1[:]), X1, reads=[X1])

    def topk_block(j, T):
        X1b, x1T, qT, big, wk, cw, tv, ti, tif = (T[k] for k in ("x1b", "x1T", "qT", "big", "wk", "cw", "tv", "ti", "tif"))
        bs, bp, pa, pb, paf, pbf, dd, ee, ssum, rsum, gates, i1, i2 = (T[k] for k in ("bs", "bp", "pa", "pb", "paf", "pbf", "dd", "ee", "ssum", "rsum", "gates", "i1", "i2"))
        PG = T["pgen"]
        pT = PG[0]
        pTb = pT[:, :].bitcast(BF16)
        for dc in range(8):
            yield S.op('pe', lambda e: e.transpose(out=pTb[:, dc * 128:(dc + 1) * 128],
                                            in_=X1b[:, dc * 128:(dc + 1) * 128], identity=identb[:]),
                 reads=[X1b, identb], writes=[pT])
        yield S.op('act', lambda e: e.copy(out=x1T[:].rearrange("p a b -> p (a b)"), in_=pTb[:, 0:1024]),
             reads=[pT], writes=[x1T])
        yield S.dma('sp', lambda e: e.dma_start(out=x1T_sv[:, :, j * 128:(j + 1) * 128], in_=x1T[:]), x1T, reads=[x1T])
        for g4 in range(4):
            bank = PG[g4 % 2]
            for i in range(4):
                hpc = g4 * 4 + i
                for dc in range(8):
                    yield S.op('pe', lambda e: e.matmul(out=bank[:, i * 128:(i + 1) * 128],
                                                  lhsT=wq_bf[:, dc, hpc * 128:(hpc + 1) * 128],
                                                  rhs=x1T[:, dc, :],
                                                  start=(dc == 0), stop=(dc == 7)),
                         reads=[wq_bf, x1T], writes=[bank])
            yield S.op('act', lambda e: e.copy(out=qT[:, g4 * 4:(g4 + 1) * 4, :].rearrange("p a b -> p (a b)"),
                                         in_=bank[:, :]), reads=[bank], writes=[qT])
        sc = big
        for g4 in range(4):
            bank = PG[g4 % 2]
            for i in range(4):
                hpc = g4 * 4 + i
                yield S.op('pe', lambda e: e.matmul(out=bank[:, i * 128:(i + 1) * 128], lhsT=qT[:, hpc, :],
                                              rhs=keysT[:, hpc, :], start=True, stop=True),
                     reads=[qT, keysT], writes=[bank])
            yield S.op('act', lambda e: e.copy(out=sc[:, g4 * 512:(g4 + 1) * 512], in_=bank[:, :]),
                 reads=[bank], writes=[sc])
        for hpc in range(16):
            src = sc[:, hpc * 128:(hpc + 1) * 128]
            w = wk[hpc % 2]
            yield S.op('dve', lambda e: e.max(out=tv[:, hpc, 0:8], in_=src), reads=[sc], writes=[tv])
            yield S.op('dve', lambda e: e.max_index(out=ti[:, hpc, 0:8], in_max=tv[:, hpc, 0:8], in_values=src),
                 reads=[sc, tv], writes=[ti])
            yield S.op('dve', lambda e: e.match_replace(out=w[:], in_to_replace=tv[:, hpc, 0:8], in_values=src,
                                                 imm_value=NEG), reads=[sc, tv], writes=[w])
            yield S.op('dve', lambda e: e.max(out=tv[:, hpc, 8:16], in_=w[:]), reads=[w], writes=[tv])
            yield S.op('dve', lambda e: e.max_index(out=ti[:, hpc, 8:16], in_max=tv[:, hpc, 8:16], in_values=w[:]),
                 reads=[w, tv], writes=[ti])
        yield S.op('dve', lambda e: e.tensor_copy(out=tif[:], in_=ti[:]), reads=[ti], writes=[tif])
        cs = big
        cs4 = cs[:].rearrange("p (h a b) -> p h a b", h=8, a=16)
        yield S.op('dve', lambda e: e.tensor_tensor(
            out=cs4, in0=tv[:, 0::2, :].unsqueeze(3).to_broadcast([128, 8, 16, 16]),
            in1=tv[:, 1::2, :].unsqueeze(2).to_broadcast([128, 8, 16, 16]), op=ALU.add),
            reads=[tv], writes=[cs])
        for h in range(8):
            src = cs[:, h * 256:(h + 1) * 256]
            w = cw[h % 2]
            yield S.op('dve', lambda e: e.max(out=bs[:, h, 0:8], in_=src), reads=[cs], writes=[bs])
            yield S.op('dve', lambda e: e.max_index(out=bp[:, h, 0:8], in_max=bs[:, h, 0:8], in_values=src),
                 reads=[cs, bs], writes=[bp])
            yield S.op('dve', lambda e: e.match_replace(out=w[:], in_to_replace=bs[:, h, 0:8], in_values=src,
                                                 imm_value=NEG), reads=[cs, bs], writes=[w])
            yield S.op('dve', lambda e: e.max(out=bs[:, h, 8:16], in_=w[:]), reads=[w], writes=[bs])
            yield S.op('dve', lambda e: e.max_index(out=bp[:, h, 8:16], in_max=bs[:, h, 8:16], in_values=w[:]),
                 reads=[w, bs], writes=[bp])
        yield S.op('dve', lambda e: e.tensor_tensor(out=dd[:], in0=bs[:], in1=bs[:, :, 0:1].to_broadcast([128, 8, 16]),
                                              op=ALU.subtract), reads=[bs], writes=[dd])
        yield S.op('act', lambda e: e.activation(out=ee[:], in_=dd[:], func=ACT.Exp), reads=[dd], writes=[ee])
        yield S.op('dve', lambda e: e.tensor_reduce(out=ssum[:], in_=ee[:], axis=AX.X, op=ALU.add), reads=[ee], writes=[ssum])
        yield S.op('dve', lambda e: e.reciprocal(out=rsum[:], in_=ssum[:]), reads=[ssum], writes=[rsum])
        yield S.op('dve', lambda e: e.tensor_tensor(out=gates[:], in0=ee[:], in1=rsum[:].unsqueeze(2).to_broadcast([128, 8, 16]),
                                              op=ALU.mult), reads=[ee, rsum], writes=[gates])
        yield S.op('dve', lambda e: e.tensor_scalar(out=pa[:], in0=bp[:], scalar1=4, scalar2=None,
                                              op0=ALU.logical_shift_right), reads=[bp], writes=[pa])
        yield S.op('dve', lambda e: e.tensor_scalar(out=pb[:], in0=bp[:], scalar1=15, scalar2=None,
                                              op0=ALU.bitwise_and), reads=[bp], writes=[pb])
        yield S.op('dve', lambda e: e.tensor_copy(out=paf[:], in_=pa[:]), reads=[pa], writes=[paf])
        yield S.op('dve', lambda e: e.tensor_copy(out=pbf[:], in_=pb[:]), reads=[pb], writes=[pbf])
        eq = big
        eq4 = eq[:].rearrange("p (h k a) -> p h k a", h=8, k=16)
        io4 = iota16[:].unsqueeze(1).unsqueeze(1).to_broadcast([128, 8, 16, 16])
        for (pf, par, dst) in ((paf, 0, i1), (pbf, 1, i2)):
            yield S.op('dve', lambda e: e.tensor_tensor(out=eq4, in0=pf[:].unsqueeze(3).to_broadcast([128, 8, 16, 16]),
                                                  in1=io4, op=ALU.is_equal), reads=[pf, iota16], writes=[eq])
            yield S.op('dve', lambda e: e.tensor_tensor(out=eq4, in0=eq4,
                                                  in1=tif[:, par::2, :].unsqueeze(2).to_broadcast([128, 8, 16, 16]),
                                                  op=ALU.mult), reads=[eq, tif], writes=[eq])
            yield S.op('dve', lambda e: e.tensor_reduce(out=dst[:], in_=eq4, axis=AX.X, op=ALU.add), reads=[eq], writes=[dst])
        pt2 = PG[1]
        for n3, (srcb, dstb) in enumerate(((i1, i1T_all), (i2, i2T_all), (gates, gT_all))):
            yield S.op('pe', lambda e: e.transpose(out=pt2[:, n3 * 128:(n3 + 1) * 128],
                                            in_=srcb[:].rearrange("p a b -> p (a b)"), identity=ident[:]),
                 reads=[srcb, ident], writes=[pt2])
        for n3, (srcb, dstb) in enumerate(((i1, i1T_all), (i2, i2T_all), (gates, gT_all))):
            yield S.op('act', lambda e: e.copy(out=dstb[:, j * 128:(j + 1) * 128], in_=pt2[:, n3 * 128:(n3 + 1) * 128]),
                 reads=[pt2], writes=[dstb])

    es_stg = ExitStack()
    stg_ = S.sb("stg", [128, 1024], F32, es_stg)
    load_c1_weights(stg_)
    S.barrier()
    es_stg.close()
    nblk = {"peer2": 2, "peer4": 4}.get(stage, NQ)
    def blk(j, T):
        yield from ln1_block(j, T)
        yield from topk_block(j, T)

    for jp in range(0, nblk, 2):
        gens = [blk(jp, TA), blk(jp + 1, TB)]
        while gens:
            for gq in list(gens):
                try:
                    next(gq)
                except StopIteration:
                    gens.remove(gq)
    for n3, tb_ in enumerate((i1T_all, i2T_all, gT_all)):
        S.dma('sp', lambda e: e.dma_start(out=idx_s[n3], in_=tb_[:]), tb_, reads=[tb_])
    S.barrier()
    es.close()
    esP.close()
    es_o.close()

    es = ExitStack()
    GT = 256
    SBT = 16
    lnp = S.sb("lnp2", [128, 2, D], F32, es)
    stats = S.sb("stats2", [128, 2, 6], F32, es)
    mv = S.sb("mv2", [128, 2], F32, es)
    rstd = S.sb("rstd2", [128, 1], F32, es)
    LNS.update({"lnp": lnp, "stats": stats, "mv": mv, "rstd": rstd})
    for i, dsrc in enumerate((ln2g_d, ln2b_d)):
        S.dma('sp', lambda e: e.dma_start(out=lnp[:, i, :], in_=dsrc.to_broadcast([128, D])), lnp, writes=[lnp])
    x1Tg = [S.sb(f"x1Tg{i}", [128, 8, GT], BF16, es) for i in range(2)]
    idxg = [[S.sb(f"idxg{i}_{n3}", [128, GT], BF16, es) for n3 in range(3)] for i in range(2)]
    Wg2 = [S.sb(f"Wg{i}", [128, GT, 128], BF16, es) for i in range(2)]
    UTr = [S.sb(f"UTr{i}", [128, D], BF16, es) for i in range(4)]
    Vr = [S.sb(f"Vr{i}", [128, D], BF16, es) for i in range(6)]
    OJ = [S.sb(f"OJ{i}", [128, SBT, 128], BF16, es) for i in range(2)]
    PI = [S.sb(f"PI{i}", [128, SBT, 128], BF16, es) for i in range(2)]
    gelb = [S.sb(f"gelb{i}", [128, GT], BF16, es) for i in range(2)]
    actb = [S.sb(f"actb{i}", [128, GT], BF16, es) for i in range(2)]
    x1r = S.sb("x1r", [128, D], F32, es)
    ysb = S.sb("ysb", [128, 4, 512], F32, es)
    r2 = x1r
    yout = x1r
    YB = PSb[0:4]
    HP = PSb[4:6]
    WP = PSb[6:8]
    io3 = iota128[:].unsqueeze(1).to_broadcast([128, SBT, 128])
    cc = {"w": 0}
    ngroups = nblk // 2
    def group_load(g):
        xt = x1Tg[g % 2]
        S.dma('sp', lambda e: e.dma_start(out=xt[:], in_=x1T_sv[:, :, g * GT:(g + 1) * GT]), xt, writes=[xt])
        for n3 in range(3):
            tb_ = idxg[g % 2][n3]
            S.dma('sp', lambda e: e.dma_start(out=tb_[:], in_=idx_s[n3][:, g * GT:(g + 1) * GT]), tb_, writes=[tb_])

    def w_build_stages(g, sbt):
        Wg = Wg2[g % 2]
        i1g, i2g, gg = idxg[g % 2]
        t0 = sbt * SBT
        oj, pi = OJ[sbt % 2], PI[sbt % 2]

        def s_oj():
            S.op('dve', lambda e: e.tensor_tensor(
                out=oj[:], in0=io3, in1=i2g[:, t0:t0 + SBT].unsqueeze(2).to_broadcast([128, SBT, 128]),
                op=ALU.is_equal), reads=[iota128, i2g], writes=[oj])

        def s_pi():
            S.op('dve', lambda e: e.tensor_tensor(
                out=pi[:], in0=io3, in1=i1g[:, t0:t0 + SBT].unsqueeze(2).to_broadcast([128, SBT, 128]),
                op=ALU.is_equal), reads=[iota128, i1g], writes=[pi])

        def s_gate():
            S.op('pool', lambda e: e.tensor_tensor(
                out=pi[:], in0=pi[:], in1=gg[:, t0:t0 + SBT].unsqueeze(2).to_broadcast([128, SBT, 128]),
                op=ALU.mult), reads=[pi, gg], writes=[pi])

        def mk_mm(q):
            def s_mm():
                wp = WP[(sbt * (SBT // 4) + q) % 2]
                for u in range(4):
                    tl = q * 4 + u
                    S.op('pe', lambda e: e.matmul(out=wp[:, u * 128:(u + 1) * 128], lhsT=oj[:, tl, :], rhs=pi[:, tl, :],
                                                  start=True, stop=True), reads=[oj, pi], writes=[wp])
            return s_mm

        def mk_ev(q):
            def s_ev():
                wp = WP[(sbt * (SBT // 4) + q) % 2]
                tg = t0 + q * 4
                dst = Wg[:, tg:tg + 4, :].rearrange("p t i -> p (t i)")
                S.op('act', lambda e: e.copy(out=dst, in_=wp[:, :]), reads=[wp], writes=[Wg])
            return s_ev

        st = [(0, s_oj), (1, s_pi), (4, s_gate)]
        for q in range(SBT // 4):
            st.append((7 + q, mk_mm(q)))
            st.append((8 + q, mk_ev(q)))
        return st

    def w_build_sub(g, sbt):
        for _, fn in w_build_stages(g, sbt):
            fn()

    def chunk_stages(g, c, n):
        k2 = n % 2
        ut, vv, hp, gl, ab = UTr[n % 4], Vr[n % 6], HP[k2], gelb[k2], actb[k2]
        Wg, xt = Wg2[g % 2], x1Tg[g % 2]

        def sD():
            S.dma('sp', lambda e: e.dma_start(out=ut[:], in_=ut_s[c]), ut, writes=[ut])
            S.dma('sp', lambda e: e.dma_start(out=vv[:], in_=v_s[c * 128:(c + 1) * 128, :]), vv, writes=[vv])

        def sH():
            for dc in range(8):
                S.op('pe', lambda e: e.matmul(out=hp[:, 0:GT], lhsT=ut[:, dc * 128:(dc + 1) * 128],
                                              rhs=xt[:, dc, :],
                                              start=(dc == 0), stop=(dc == 7)), reads=[ut, xt], writes=[hp])
            S.op('act', lambda e: e.activation(out=gl[:], in_=hp[:, 0:GT], func=ACT.Gelu), reads=[hp], writes=[gl])

        def sM():
            meng = 'pool' if (c % 8) in (1, 4, 6) else 'dve'
            S.op(meng, lambda e: e.tensor_tensor(out=ab[:], in0=gl[:], in1=Wg[:, :, c], op=ALU.mult),
                 reads=[gl, Wg], writes=[ab])

        def sY():
            for tb in range(2):
                for half in range(2):
                    S.op('pe', lambda e: e.matmul(out=YB[tb * 2 + half][:, :], lhsT=ab[:, tb * 128:(tb + 1) * 128],
                                                  rhs=vv[:, half * 512:(half + 1) * 512],
                                                  start=(c == 0), stop=(c == nchunks - 1)),
                         reads=[ab, vv], writes=[YB[tb * 2 + half]])
            if c == nchunks - 1:
                final_group(g)

        return [sD, sH, sM, sY]

    def final_group(g):
        for k4 in range(4):
            S.op('act', lambda e: e.copy(out=ysb[:, k4, :], in_=YB[k4][:, :]), reads=[YB[k4]], writes=[ysb])
        for tb in range(2):
            j = g * 2 + tb
            S.dma('sp', lambda e: e.dma_start(out=x1r[:], in_=x1_s[j * 128:(j + 1) * 128, :]), x1r, writes=[x1r])
            S.op('dve', lambda e: e.scalar_tensor_tensor(
                out=r2[:], in0=x1r[:], scalar=ALPHA,
                in1=ysb[:, tb * 2:tb * 2 + 2, :].rearrange("p a b -> p (a b)"), op0=ALU.mult, op1=ALU.add),
                reads=[x1r, ysb], writes=[r2])
            for _ in layer_norm(r2, yout, 0, 1):
                pass
            S.dma('sp', lambda e: e.dma_start(out=out_d[j * 128:(j + 1) * 128, :], in_=yout[:]), yout, reads=[yout])

    group_load(0)
    for sbt in range(GT // SBT):
        w_build_sub(0, sbt)
    iters = []
    for g in range(ngroups):
        for c in range(nchunks):
            iters.append(chunk_stages(g, c, len(iters)))
    step = {"n": 0}
    NSB = GT // SBT

    sched = {}
    for g in range(ngroups - 1):
        for k in range(NSB):
            base = g * nchunks + 8 + 6 * k
            for rel, fn in w_build_stages(g + 1, k):
                sched.setdefault(base + rel, []).append(fn)

    def c2_hook():
        n = step["n"]
        step["n"] += 1
        g, r = n // nchunks, n % nchunks
        if g + 1 < ngroups and r == 4:
            group_load(g + 1)
        for fn in sched.get(n, []):
            fn()

    run_pipeline(iters, [0, 2, 3, 4], hook=c2_hook)
    S.finish('sp', [yout])
    return nc


def make_in_maps(inputs):
    x = np.ascontiguousarray(inputs["x"], dtype=np.float32)
    w_in = np.ascontiguousarray(inputs["w_in"][0], dtype=np.float32)
    w_out = np.ascontiguousarray(inputs["w_out"][0], dtype=np.float32)
    wq = np.ascontiguousarray(inputs["peer_wq"][0], dtype=np.float32)
    keys = np.ascontiguousarray(inputs["peer_sub_keys"][0], dtype=np.float32).reshape(16 * 128, 128)
    pu = np.ascontiguousarray(inputs["peer_u"][0], dtype=np.float32)
    pv = np.ascontiguousarray(inputs["peer_v"][0], dtype=np.float32)
    maps = []
    for c in range(8):
        b, hf = c // 2, c % 2
        own = np.concatenate([(2 * j + hf) * 128 + np.arange(128) for j in range(NQ)])
        xb = x[b]
        m = {
            "xT": np.ascontiguousarray(xb.T),
            "xTq": np.ascontiguousarray(xb[own].T),
            "xown": np.ascontiguousarray(xb[own]),
            "w_in": w_in, "w_out": w_out, "wq": wq, "keys": keys,
            "peer_u": pu, "peer_v": pv,
            "ln1_g": np.ascontiguousarray(inputs["ln1_g"], dtype=np.float32).reshape(1, D),
            "ln1_b": np.ascontiguousarray(inputs["ln1_b"], dtype=np.float32).reshape(1, D),
            "ln2_g": np.ascontiguousarray(inputs["ln2_g"], dtype=np.float32).reshape(1, D),
            "ln2_b": np.ascontiguousarray(inputs["ln2_b"], dtype=np.float32).reshape(1, D),
        }
        for k, v in host_consts(hf).items():
            m["c_" + k] = np.ascontiguousarray(v, dtype=np.float32)
        maps.append(m)
    return maps


def kernel(**inputs):
    nc = build("full")
    maps = make_in_maps(inputs)
    res = run_bass_kernel_spmd(nc, maps, core_ids=list(range(8)))
    out = np.zeros((4, SEQ, D), np.float32)
    for c in range(8):
        b, hf = c // 2, c % 2
        o = res.results[c]["out"]
        for j in range(NQ):
            g = 2 * j + hf
            out[b, g * 128:(g + 1) * 128] = o[j * 128:(j + 1) * 128]
    return out
```
